# Optimizing a Trainium2 kernel written in Bass

```python
import math
import jax, jax.numpy as jnp
from jax import lax
import numpy as np

D_MODEL = 1024
BATCH = 8
SEQ = 2048
DEPTH = 1
DEC_BATCH = 128
DEC_SEQ = 1
PAST_LEN = 16384
PAGE_SIZE = 128

D_RNN = 1024
RNN_BLOCKS = 8
RNN_BW = D_RNN // RNN_BLOCKS
CONV_W = 4
RG_C = 8.0
GLA_HEADS = 4
GLA_DK = D_MODEL // 2 // GLA_HEADS
GLA_DV = D_MODEL // GLA_HEADS
GLA_RANK = 16
GLA_TAU = 16.0
GLA_CHUNK = 64
D_FF = 2816
EPS = 1e-6

IN_SIZES = (D_RNN, D_RNN, GLA_HEADS * GLA_DK, GLA_HEADS * GLA_DK, GLA_HEADS * GLA_DV,
            GLA_HEADS * GLA_DV, GLA_RANK, D_MODEL, D_MODEL)
D_IN = sum(IN_SIZES)

kernel_name = "hawk_gla_parallel_macaron_step"


def rmsnorm(x, g):
    xf = x.astype(jnp.float32)
    y = xf * lax.rsqrt(jnp.mean(xf * xf, axis=-1, keepdims=True) + EPS)
    return (y * g.astype(jnp.float32)).astype(x.dtype)


def swiglu(x, w_gu, w_down):
    gate, up = jnp.split(x @ w_gu, 2, axis=-1)
    return (jax.nn.silu(gate) * up) @ w_down


def causal_depthwise_conv(xr, buf, w, b):
    T = xr.shape[1]
    xp = jnp.concatenate([buf.astype(xr.dtype), xr], axis=1)
    y = b
    for j in range(CONV_W):
        y = y + xp[:, j:j + T] * w[j]
    return y, xp[:, T:]


def linear_scan(a, b, h0):
    def combine(c1, c2):
        a1, b1 = c1
        a2, b2 = c2
        return a1 * a2, a2 * b1 + b2
    a_cum, h = lax.associative_scan(combine, (a, b), axis=1)
    return h + a_cum * h0[:, None]


def rglru(xc, h0, w_a, b_a, w_x, b_x, lam, reset_first):
    xf = xc.astype(jnp.float32)
    B, T, _ = xf.shape
    xb = xf.reshape(B, T, RNN_BLOCKS, RNN_BW)
    r = jax.nn.sigmoid(jnp.einsum('btni,nij->btnj', xb, w_a.astype(jnp.float32)).reshape(B, T, D_RNN) + b_a)
    i = jax.nn.sigmoid(jnp.einsum('btni,nij->btnj', xb, w_x.astype(jnp.float32)).reshape(B, T, D_RNN) + b_x)
    log_a = -RG_C * r * jax.nn.softplus(-lam.astype(jnp.float32))
    a = jnp.exp(log_a)
    mult = jnp.sqrt(jnp.maximum(-jnp.expm1(2.0 * log_a), 0.0))
    if reset_first:
        mult = mult.at[:, 0].set(1.0)
    h = linear_scan(a, mult * i * xf, h0.astype(jnp.float32))
    return h, h[:, -1]


def gla_chunked(q, k, v, g, S0):
    B, T, H, DK = q.shape
    DV = v.shape[-1]
    C = min(GLA_CHUNK, T)
    pad = (-T) % C
    if pad:
        pw = ((0, 0), (0, pad), (0, 0), (0, 0))
        q, k, v, g = (jnp.pad(t, pw) for t in (q, k, v, g))
    N = (T + pad) // C
    q, k, g = (t.reshape(B, N, C, H, DK) for t in (q, k, g))
    v = v.reshape(B, N, C, H, DV)
    bcum = jnp.cumsum(g, axis=2)
    q_i = q * jnp.exp(bcum)
    k_i = k * jnp.exp(-bcum)
    mask = jnp.tril(jnp.ones((C, C), dtype=bool))
    att = jnp.where(mask, jnp.einsum('bnchk,bnshk->bnhcs', q_i, k_i), 0.0)
    o_intra = jnp.einsum('bnhcs,bnshv->bnchv', att, v)
    b_last = bcum[:, :, -1]
    k_end = k * jnp.exp(b_last[:, :, None] - bcum)

    def step(S, inp):
        kk, vv, dl = inp
        S_new = S * jnp.exp(dl)[..., None] + jnp.einsum('bchk,bchv->bhkv', kk, vv)
        return S_new, S

    xs = (jnp.moveaxis(k_end, 1, 0), jnp.moveaxis(v, 1, 0), jnp.moveaxis(b_last, 1, 0))
    S_final, S_starts = lax.scan(step, S0, xs)
    S_starts = jnp.moveaxis(S_starts, 0, 1)
    o_inter = jnp.einsum('bnchk,bnhkv->bnchv', q_i, S_starts)
    o = (o_intra + o_inter).reshape(B, N * C, H, DV)[:, :T]
    return o, S_final


def token_mixing(u, h0, conv0, S0, reset_first, w_in, conv_w, conv_b, rg_w_a, rg_b_a, rg_w_x, rg_b_x,
                 rg_lambda, gla_w_lr, gla_b_lr, gla_norm_g, w_branch_rnn, w_branch_gla, w_out):
    Bsz, T, _ = u.shape
    f32 = jnp.float32
    split_at = np.cumsum(IN_SIZES)[:-1].tolist()
    xr, yr, q, k, v, og, lr, gate_rnn, gate_gla = jnp.split(u @ w_in, split_at, axis=-1)
    xc, conv_new = causal_depthwise_conv(xr, conv0, conv_w, conv_b)
    h, h_last = rglru(xc, h0, rg_w_a, rg_b_a, rg_w_x, rg_b_x, rg_lambda, reset_first)
    o_rnn = (h * jax.nn.gelu(yr.astype(f32))).astype(u.dtype)
    qh = q.reshape(Bsz, T, GLA_HEADS, GLA_DK).astype(f32) * (GLA_DK ** -0.5)
    kh = k.reshape(Bsz, T, GLA_HEADS, GLA_DK).astype(f32)
    vh = v.reshape(Bsz, T, GLA_HEADS, GLA_DV).astype(f32)
    log_alpha = jax.nn.log_sigmoid((lr @ gla_w_lr + gla_b_lr).astype(f32)) / GLA_TAU
    o, S_new = gla_chunked(qh, kh, vh, log_alpha.reshape(Bsz, T, GLA_HEADS, GLA_DK), S0.astype(f32))
    o = o * lax.rsqrt(jnp.mean(o * o, axis=-1, keepdims=True) + EPS) * gla_norm_g.astype(f32)
    o = o * jax.nn.silu(og.reshape(Bsz, T, GLA_HEADS, GLA_DV).astype(f32))
    o_gla = o.reshape(Bsz, T, GLA_HEADS * GLA_DV).astype(u.dtype)
    merged = jax.nn.sigmoid(gate_rnn) * (o_rnn @ w_branch_rnn) + jax.nn.sigmoid(gate_gla) * (o_gla @ w_branch_gla)
    return merged @ w_out, h_last.astype(u.dtype), conv_new.astype(u.dtype), S_new.astype(u.dtype)


def decoder_layer(x, h0, conv0, S0, reset_first, norm_g, ffn1_w_gu, ffn1_w_down, w_in, conv_w, conv_b,
                  rg_w_a, rg_b_a, rg_w_x, rg_b_x, rg_lambda, gla_w_lr, gla_b_lr, gla_norm_g,
                  w_branch_rnn, w_branch_gla, w_out, ffn2_w_gu, ffn2_w_down):
    x = x + 0.5 * rmsnorm(swiglu(rmsnorm(x, norm_g[0]), ffn1_w_gu, ffn1_w_down), norm_g[1])
    mix, h_last, conv_new, S_new = token_mixing(
        rmsnorm(x, norm_g[2]), h0, conv0, S0, reset_first, w_in, conv_w, conv_b, rg_w_a, rg_b_a,
        rg_w_x, rg_b_x, rg_lambda, gla_w_lr, gla_b_lr, gla_norm_g, w_branch_rnn, w_branch_gla, w_out)
    x = x + rmsnorm(mix, norm_g[3])
    x = x + 0.5 * rmsnorm(swiglu(rmsnorm(x, norm_g[4]), ffn2_w_gu, ffn2_w_down), norm_g[5])
    return x, h_last, conv_new, S_new


def setup_inputs(seed: int = 0) -> dict:
    key = jax.random.key(seed)
    ks = iter(jax.random.split(key, 32))
    f32 = jnp.float32

    def nrm(shape, scale):
        return jax.random.normal(next(ks), shape, f32) * scale

    u = jax.random.uniform(next(ks), (DEPTH, D_RNN), f32, 0.9, 0.999)
    p = u ** (1.0 / RG_C)
    rg_lambda = jnp.log(p) - jnp.log1p(-p)
    return {
        "x_prompt": nrm((BATCH, SEQ, D_MODEL), 1.0),
        "x_sample": nrm((DEC_BATCH, DEC_SEQ, D_MODEL), 1.0),
        "state_rnn_h": nrm((DEPTH, DEC_BATCH, D_RNN), 0.5),
        "state_rnn_conv": nrm((DEPTH, DEC_BATCH, CONV_W - 1, D_RNN), 1.0),
        "state_gla": nrm((DEPTH, DEC_BATCH, GLA_HEADS, GLA_DK, GLA_DV), 1.0),
        "norm_gains": 1.0 + nrm((DEPTH, 6, D_MODEL), 0.02),
        "ffn1_w_gu": nrm((DEPTH, D_MODEL, 2 * D_FF), D_MODEL ** -0.5),
        "ffn1_w_down": nrm((DEPTH, D_FF, D_MODEL), D_FF ** -0.5),
        "w_in": nrm((DEPTH, D_MODEL, D_IN), D_MODEL ** -0.5),
        "conv_w": nrm((DEPTH, CONV_W, D_RNN), CONV_W ** -0.5),
        "conv_b": nrm((DEPTH, D_RNN), 0.01),
        "rg_w_a": nrm((DEPTH, RNN_BLOCKS, RNN_BW, RNN_BW), RNN_BW ** -0.5),
        "rg_b_a": nrm((DEPTH, D_RNN), 0.01),
        "rg_w_x": nrm((DEPTH, RNN_BLOCKS, RNN_BW, RNN_BW), RNN_BW ** -0.5),
        "rg_b_x": nrm((DEPTH, D_RNN), 0.01),
        "rg_lambda": rg_lambda,
        "gla_w_lr": nrm((DEPTH, GLA_RANK, GLA_HEADS * GLA_DK), GLA_RANK ** -0.5),
        "gla_b_lr": nrm((DEPTH, GLA_HEADS * GLA_DK), 0.01),
        "gla_norm_g": 1.0 + nrm((DEPTH, GLA_DV), 0.02),
        "w_branch_rnn": nrm((DEPTH, D_RNN, D_MODEL), D_RNN ** -0.5),
        "w_branch_gla": nrm((DEPTH, GLA_HEADS * GLA_DV, D_MODEL), (GLA_HEADS * GLA_DV) ** -0.5),
        "w_out": nrm((DEPTH, D_MODEL, D_MODEL), D_MODEL ** -0.5),
        "ffn2_w_gu": nrm((DEPTH, D_MODEL, 2 * D_FF), D_MODEL ** -0.5),
        "ffn2_w_down": nrm((DEPTH, D_FF, D_MODEL), D_FF ** -0.5),
    }


def reference(x_prompt, x_sample, state_rnn_h, state_rnn_conv, state_gla, norm_gains, ffn1_w_gu,
              ffn1_w_down, w_in, conv_w, conv_b, rg_w_a, rg_b_a, rg_w_x, rg_b_x, rg_lambda, gla_w_lr,
              gla_b_lr, gla_norm_g, w_branch_rnn, w_branch_gla, w_out, ffn2_w_gu, ffn2_w_down):
    dt = x_prompt.dtype
    Bp = x_prompt.shape[0]
    yp, ys = x_prompt, x_sample
    hp_l, cp_l, sp_l, hs_l, cs_l, ss_l = [], [], [], [], [], []
    for l in range(DEPTH):
        lw = (norm_gains[l], ffn1_w_gu[l], ffn1_w_down[l], w_in[l], conv_w[l], conv_b[l], rg_w_a[l],
              rg_b_a[l], rg_w_x[l], rg_b_x[l], rg_lambda[l], gla_w_lr[l], gla_b_lr[l], gla_norm_g[l],
              w_branch_rnn[l], w_branch_gla[l], w_out[l], ffn2_w_gu[l], ffn2_w_down[l])
        h0 = jnp.zeros((Bp, D_RNN), dt)
        c0 = jnp.zeros((Bp, CONV_W - 1, D_RNN), dt)
        s0 = jnp.zeros((Bp, GLA_HEADS, GLA_DK, GLA_DV), dt)
        yp, hp, cp, sp = decoder_layer(yp, h0, c0, s0, True, *lw)
        ys, hs, cs, ss = decoder_layer(ys, state_rnn_h[l], state_rnn_conv[l], state_gla[l], False, *lw)
        hp_l.append(hp); cp_l.append(cp); sp_l.append(sp)
        hs_l.append(hs); cs_l.append(cs); ss_l.append(ss)
    return (yp, ys, jnp.stack(hp_l), jnp.stack(cp_l), jnp.stack(sp_l),
            jnp.stack(hs_l), jnp.stack(cs_l), jnp.stack(ss_l))
```

```python
import contextlib
import numpy as np
import concourse.bass as bass
import concourse.mybir as mybir
from concourse.bass_utils import run_bass_kernel_spmd

F32 = mybir.dt.float32
BF16 = mybir.dt.bfloat16
AF = mybir.ActivationFunctionType
ALU = mybir.AluOpType

NCORES = 8
D = 1024
SEQ = 2048
TW = 1024
NT = SEQ // TW
NS = 16
DFF = 2816
NJ = DFF // 128
EPS = 1e-6
DK = 128
DV = 256
NH = 4
OFF_XR, OFF_YR, OFF_Q, OFF_K, OFF_V, OFF_OG, OFF_LR, OFF_GA, OFF_GB = 0, 1024, 2048, 2560, 3072, 4096, 5120, 5136, 6160
SLOT = 4096
NSLOT = 3

P_GAIN = 0
P_CW = 48
P_CB = 80
P_BA = 88
P_BX = 96
P_LAM = 104
P_BLR = 112
P_GNG = 116
NPAR = 118
C_ONES1024 = 0
C_ONES256 = 128
C_IDENT = 256
C_MASK = 384
C_RMASK = 512
NCST = 1536

ENGS = ("pe", "act", "dve", "pool", "sp")


class _Op:
    __slots__ = ("eng", "idx", "fn", "deps", "dma", "sig", "sigval", "tag")

    def __init__(self, eng, idx, fn, dma):
        self.eng, self.idx, self.fn, self.dma = eng, idx, fn, dma
        self.tag = ""
        self.deps = {}
        self.sig = False
        self.sigval = None


class Prog:
    def __init__(self, nc):
        self.nc = nc
        self.ops = {e: [] for e in ENGS}
        self.last_write = {}
        self.readers = {}
        self.tag = ""
        self.tagmap = {}

    @staticmethod
    def _flat(keys):
        out = []
        for k in keys:
            if isinstance(k, list):
                out.extend(Prog._flat(k))
            else:
                out.append(k)
        return out

    def add(self, eng, fn, reads=(), writes=(), dma=None):
        op = _Op(eng, len(self.ops[eng]), fn, dma)
        op.tag = self.tag
        reads = self._flat(reads)
        writes = self._flat(writes)
        bk = [k for k in reads if isinstance(k, tuple) and k and k[0] == "bank"]
        if bk:
            reads = [k for k in reads if k not in bk]
            writes = writes + [k for k in bk if k not in writes]

        def dep(o):
            if o is None:
                return
            if o.dma is None and o.eng == "pe" and eng == "pe":
                return
            k = ("d", o.dma) if o.dma is not None else ("e", o.eng)
            cur = op.deps.get(k)
            if cur is None or o.idx > cur.idx:
                op.deps[k] = o

        for k in reads:
            dep(self.last_write.get(k))
        for k in writes:
            dep(self.last_write.get(k))
            for r in self.readers.get(k, ()):
                dep(r)
        for k in reads:
            self.readers.setdefault(k, []).append(op)
        for k in writes:
            self.last_write[k] = op
            self.readers[k] = []
        self.ops[eng].append(op)
        return op

    def emit(self, final_wait_eng="sp"):
        nc = self.nc
        for e in ENGS:
            for op in self.ops[e]:
                for d in op.deps.values():
                    d.sig = True
        for e in ENGS:
            for op in self.ops[e]:
                if op.dma is not None:
                    op.sig = True
        cnt = {}
        for e in ENGS:
            for op in self.ops[e]:
                if not op.sig:
                    continue
                k = ("d", op.dma) if op.dma is not None else ("e", op.eng)
                cnt[k] = cnt.get(k, 0) + (16 if op.dma is not None else 1)
                op.sigval = cnt[k]
        final = dict(cnt)
        with contextlib.ExitStack() as st:
            sems = {}
            for i, k in enumerate(cnt):
                sems[k] = st.enter_context(nc.semaphore("sem%d" % i))
            block = st.enter_context(nc.Block())
            engobj = {"pe": "tensor", "act": "scalar", "dve": "vector", "pool": "gpsimd", "sp": "sync"}

            def mk(e):
                def body(eng):
                    known = {}
                    for op in self.ops[e]:
                        for k, d in op.deps.items():
                            if known.get(k, 0) < d.sigval:
                                eng.wait_ge(sems[k], d.sigval)
                                known[k] = d.sigval
                        ins = op.fn(eng)
                        try:
                            self.tagmap[ins.ins.name] = op.tag
                        except Exception:
                            pass
                        if op.sig:
                            k = ("d", op.dma) if op.dma is not None else ("e", op.eng)
                            ins.then_inc(sems[k], 16 if op.dma is not None else 1)
                    if e == final_wait_eng:
                        for k, v in final.items():
                            if known.get(k, 0) < v:
                                eng.wait_ge(sems[k], v)
                return body

            for e in ENGS:
                if self.ops[e] or e == final_wait_eng:
                    getattr(block, engobj[e])(mk(e))


def _stream_plan():
    plan = []
    for f in (1,):
        pass
    def ffn(tag):
        out = []
        for jb in range(NJ // 2):
            out.append((tag + "gu", jb, 8 * 512))
        for m in range(8):
            out.append((tag + "dn", m, NJ * 128))
        return out
    plan += ffn("f1")
    for n in range(8):
        plan.append(("rnn", n, 8 * 256))
    for hd in range(NH):
        plan.append(("qk", hd, 8 * 256))
        plan.append(("vog", hd, 8 * 512))
    for m in range(8):
        plan.append(("gate", m, 8 * 256))
        plan.append(("br", m, 2 * 8 * 128))
    for mp in range(4):
        plan.append(("wout", mp, 2 * 8 * 128))
    plan += ffn("f2")
    offs = []
    o = 0
    for (_, _, n) in plan:
        offs.append(o)
        o += n
    return plan, offs, o


PLAN, PLAN_OFFS, WLEN = _stream_plan()


def _kc(w, cols):
    return np.ascontiguousarray(w[:, cols].reshape(8, 128, -1).transpose(1, 0, 2))


def _pack_weights(inp):
    wgu = {"f1": inp["ffn1_w_gu"][0], "f2": inp["ffn2_w_gu"][0]}
    wdn = {"f1": inp["ffn1_w_down"][0], "f2": inp["ffn2_w_down"][0]}
    w_in = inp["w_in"][0]
    wbr = inp["w_branch_rnn"][0]
    wbg = inp["w_branch_gla"][0]
    wout = inp["w_out"][0]
    ws = np.empty((128, WLEN), np.float32)
    ar = np.arange
    for (name, i, n), off in zip(PLAN, PLAN_OFFS):
        if name.endswith("gu"):
            w = wgu[name[:2]]
            cols = np.concatenate([ar(i * 256, i * 256 + 256), ar(DFF + i * 256, DFF + i * 256 + 256)])
            blk = _kc(w, cols)
        elif name.endswith("dn"):
            w = wdn[name[:2]]
            blk = w[:, i * 128:(i + 1) * 128].reshape(NJ, 128, 128).transpose(1, 0, 2)
        elif name == "rnn":
            cols = np.concatenate([ar(OFF_XR + i * 128, OFF_XR + i * 128 + 128), ar(OFF_YR + i * 128, OFF_YR + i * 128 + 128)])
            blk = _kc(w_in, cols)
        elif name == "qk":
            cols = np.concatenate([ar(OFF_Q + i * 128, OFF_Q + i * 128 + 128), ar(OFF_K + i * 128, OFF_K + i * 128 + 128)])
            blk = _kc(w_in, cols)
        elif name == "vog":
            cols = np.concatenate([ar(OFF_V + i * 256, OFF_V + i * 256 + 256), ar(OFF_OG + i * 256, OFF_OG + i * 256 + 256)])
            blk = _kc(w_in, cols)
        elif name == "gate":
            cols = np.concatenate([ar(OFF_GA + i * 128, OFF_GA + i * 128 + 128), ar(OFF_GB + i * 128, OFF_GB + i * 128 + 128)])
            blk = _kc(w_in, cols)
        elif name == "br":
            a = _kc(wbr, ar(i * 128, i * 128 + 128))
            b = _kc(wbg, ar(i * 128, i * 128 + 128))
            blk = np.stack([a, b], axis=1)
        elif name == "wout":
            a = _kc(wout, ar((2 * i) * 128, (2 * i) * 128 + 128))
            b = _kc(wout, ar((2 * i + 1) * 128, (2 * i + 1) * 128 + 128))
            blk = np.stack([a, b], axis=1)
        else:
            raise AssertionError(name)
        ws[:, off:off + n] = np.asarray(blk).reshape(128, n)
    return ws


def _fm(v):
    return np.ascontiguousarray(np.asarray(v).reshape(8, 128).T)


def _pack_params(inp):
    p = np.zeros((128, NPAR), np.float32)
    g = inp["norm_gains"][0]
    for i in range(6):
        p[:, P_GAIN + i * 8:P_GAIN + i * 8 + 8] = _fm(g[i])
    cw = inp["conv_w"][0]
    for j in range(4):
        p[:, P_CW + j * 8:P_CW + j * 8 + 8] = _fm(cw[j])
    p[:, P_CB:P_CB + 8] = _fm(inp["conv_b"][0])
    p[:, P_BA:P_BA + 8] = _fm(inp["rg_b_a"][0])
    p[:, P_BX:P_BX + 8] = _fm(inp["rg_b_x"][0])
    p[:, P_LAM:P_LAM + 8] = _fm(inp["rg_lambda"][0])
    p[:, P_BLR:P_BLR + 4] = np.asarray(inp["gla_b_lr"][0]).reshape(4, 128).T
    p[:, P_GNG:P_GNG + 2] = np.asarray(inp["gla_norm_g"][0]).reshape(2, 128).T
    return p


def _consts():
    c = np.zeros((128, NCST), np.float32)
    c[:, C_ONES1024:C_ONES1024 + 128] = 1.0 / 1024.0
    c[:, C_ONES256:C_ONES256 + 128] = 1.0 / 256.0
    c[:, C_IDENT:C_IDENT + 128] = np.eye(128, dtype=np.float32)
    s = np.arange(128)[:, None]
    cc = np.arange(128)[None, :]
    c[:, C_MASK:C_MASK + 128] = (s <= cc).astype(np.float32)
    rm = np.ones((1024,), np.float32)
    rm[::128] = 0.0
    c[:, C_RMASK:C_RMASK + 1024] = rm[None, :]
    return c


class Ctx:
    pass


def build_program(debug=None):
    nc = bass.Bass("TRN2", target_bir_lowering=False)
    di = lambda name, shape: nc.dram_tensor(name, shape, F32, kind="ExternalInput").ap()
    do = lambda name, shape: nc.dram_tensor(name, shape, F32, kind="ExternalOutput").ap()
    d_x = di("xT", [128, 8, SEQ])
    d_xs = di("xsT", [128, 8, NS])
    d_h0 = di("h0", [128, 8, NS])
    d_c0 = di("c0", [128, 8, NS, 3])
    d_s0 = di("s0", [NS, NH, 128, DV])
    d_ws = di("ws", [128, WLEN])
    d_par = di("par", [128, NPAR])
    d_cst = di("cst", [128, NCST])
    d_wlrin = di("wlrin", [128, 8, 16])
    d_rgw = di("rgw", [128, 2, 8, 128])
    d_wlr = di("wlr", [16, 512])
    o_y = do("yT", [128, 8, SEQ])
    o_ys = do("ysT", [128, 8, NS])
    o_hp = do("hp", [128, 8])
    o_cp = do("cp", [128, 8, 3])
    o_sp = do("sp", [128, NH, DV])
    o_hs = do("hs", [128, 8, NS])
    o_cs = do("cs", [128, 8, NS, 3])
    o_ss = do("ss", [NS, NH, 128, DV])
    dbg_out = {}
    if debug:
        for name, shape in debug.items():
            dbg_out[name] = do("dbg_" + name, shape)

    with contextlib.ExitStack() as st:
        def sb(name, shape, dt=F32):
            return st.enter_context(nc.sbuf_tensor("sb_" + name, shape, dt))

        P = Prog(nc)
        par = sb("par", [128, NPAR])
        cst = sb("cst", [128, NCST - 256])
        cbf = sb("cbf", [128, 512], BF16)
        wlrin = sb("wlrin", [128, 8, 16], BF16)
        rgw = sb("rgw", [128, 2, 8, 128], BF16)
        wlr = sb("wlr", [16, 512], BF16)
        der = sb("der", [128, 64])
        DC, DC2, DNB, DEPS, DEPS4, DHC, DHBA, DHBX, DQ25 = 0, 8, 16, 20, 21, 24, 32, 40, 48
        wring = [sb("wring%d" % i, [128, SLOT], BF16) for i in range(NSLOT)]
        ps = st.enter_context(nc.psum_tensor("ps", [128, 4096], F32))

        P.add("sp", lambda e: e.dma_start(out=par[:], in_=d_par), writes=["par"], dma="par")
        P.add("sp", lambda e: e.dma_start(out=cst[:], in_=d_cst[:, 256:NCST]), writes=["cst"], dma="cst")
        P.add("pool", lambda e: e.dma_start(out=cbf[:], in_=d_cst[:, 0:512]), writes=["cbf"], dma="cbf")
        P.add("pool", lambda e: e.dma_start(out=wlrin[:], in_=d_wlrin), writes=["wlrin"], dma="wlrin")
        P.add("pool", lambda e: e.dma_start(out=rgw[:], in_=d_rgw), writes=["rgw"], dma="rgw")
        P.add("pool", lambda e: e.dma_start(out=wlr[:], in_=d_wlr), writes=["wlr"], dma="wlr")
        ones1024 = cbf[:, 0:128]
        ones256 = cbf[:, 128:256]
        identb = cbf[:, 256:384]
        P.add("act", lambda e: e.activation(out=der[:, DC:DC + 8], in_=par[:, P_LAM:P_LAM + 8], func=AF.Exp, scale=-1.0),
              reads=["par"], writes=["der"])
        P.add("act", lambda e: e.activation(out=der[:, DC:DC + 8], in_=der[:, DC:DC + 8], func=AF.Ln, bias=1.0),
              reads=["der"], writes=["der"])
        P.add("dve", lambda e: e.tensor_scalar(out=der[:, DC2:DC2 + 8], in0=der[:, DC:DC + 8], scalar1=-16.0, scalar2=None, op0=ALU.mult),
              reads=["der"], writes=["der"])
        P.add("dve", lambda e: e.tensor_scalar(out=der[:, DC:DC + 8], in0=der[:, DC:DC + 8], scalar1=-8.0, scalar2=None, op0=ALU.mult),
              reads=["der"], writes=["der"])
        P.add("dve", lambda e: e.tensor_scalar(out=der[:, DNB:DNB + 4], in0=par[:, P_BLR:P_BLR + 4], scalar1=-1.0, scalar2=None, op0=ALU.mult),
              reads=["par", "der"], writes=["der"])
        P.add("dve", lambda e: e.tensor_scalar(out=der[:, DHC:DHC + 8], in0=der[:, DC:DC + 8], scalar1=0.5, scalar2=None, op0=ALU.mult),
              reads=["der"], writes=["der"])
        P.add("dve", lambda e: e.tensor_scalar(out=der[:, DHBA:DHBA + 8], in0=par[:, P_BA:P_BA + 8], scalar1=0.5, scalar2=None, op0=ALU.mult),
              reads=["par", "der"], writes=["der"])
        P.add("dve", lambda e: e.tensor_scalar(out=der[:, DHBX:DHBX + 8], in0=par[:, P_BX:P_BX + 8], scalar1=0.5, scalar2=None, op0=ALU.mult),
              reads=["par", "der"], writes=["der"])
        P.add("dve", lambda e: e.memset(der[:, DQ25:DQ25 + 1], 0.25), reads=["der"], writes=["der"])
        P.add("dve", lambda e: e.memset(der[:, DEPS:DEPS + 1], EPS), reads=["der"], writes=["der"])
        P.add("dve", lambda e: e.memset(der[:, DEPS4:DEPS4 + 1], 4.0 * EPS), reads=["der"], writes=["der"])

        def make_ctx(name, W, psA, psB, psC, psM, keyM, banks, psS=None):
            c = Ctx()
            c.name, c.W = name, W
            c.groups = [(g, min(g + 512, W)) for g in range(0, W, 512)]
            c.x = sb(name + "_x", [128, 8, W])
            c.xn = sb(name + "_xn", [128, 8, W], BF16)
            c.hreg = sb(name + "_h", [128, 24, W], BF16)
            c.y = sb(name + "_y", [128, 8, W])
            c.sq = sb(name + "_sq", [128, 2, W], BF16)
            c.rstd = sb(name + "_rstd", [128, W])
            c.ps = {"A": psA, "B": psB}
            if psC is not None:
                c.ps["C"] = psC
            c.banks = banks
            c.pend = None
            c.statset = "C"
            if psS is not None:
                c.ps["S"] = psS
                c.statset = "S"
            c.psM = psM
            c.keyM = keyM
            c.rr = 0
            c.sqi = 0
            return c

        BK = lambda *i: [("bank", j) for j in i]
        cp_ = make_ctx("p", TW, ps[:, 0:1024], ps[:, 1024:2048], ps[:, 2048:3072], ps[:, 2560:3072], BK(5),
                       {"A": BK(0, 1), "B": BK(2, 3), "C": BK(4, 5)})
        cp_.order = ["A", "B", "C"]
        b6, b7 = 3072, 3584
        cs_ = make_ctx("s", NS, ps[:, b6:b6 + 16], ps[:, b7:b7 + 16], None, ps[:, b6 + 16:b6 + 32], BK(6),
                       {"A": BK(6), "B": BK(7), "S": BK(6)}, psS=ps[:, b6 + 16:b6 + 32])
        cs_.order = ["A", "B"]
        cs_.sqall = sb("s_sqall", [128, 8, NS], BF16)

        def K(c, what, i=None):
            if what == "y" and i in (4, 5, 6):
                return [(c.name, "y", i, 0), (c.name, "y", i, 1)]
            return (c.name, what, i)

        def pskey(c, s):
            return c.banks[s]

        def nextset(c, avoid=()):
            order = c.order
            for _ in range(len(order) + 1):
                s = order[c.rr % len(order)]
                c.rr += 1
                if s not in avoid:
                    return s
            raise AssertionError

        def T(c, i):
            return c.y[:, i, :]

        halo_bf = sb("halo_bf", [128, 8, 3], BF16)
        halo = sb("halo", [128, 8, 3])
        hcar = sb("hcar", [128, 8])
        S_f = sb("S_f", [128, NH, DV])
        S_b = sb("S_b", [128, 2, DV], BF16)
        S_t = sb("S_t", [128, DV])
        lr_bf = sb("lr_bf", [16, TW], BF16)
        s_h0 = sb("s_h0", [128, 8, NS])
        s_c0 = sb("s_c0", [128, 8, NS, 3])
        s_cn = sb("s_cn", [128, 8, NS, 3])
        s_hn = sb("s_hn", [128, 8, NS])
        s_xr = sb("s_xr", [128, 8, NS])
        s_t = sb("s_t", [128, 12, NS])
        s_xcb3 = sb("s_xcb3", [128, 8, NS], BF16)
        s_u = sb("s_u", [128, 6, 8, NS])
        s_lr = sb("s_lr", [16, NS], BF16)
        s_ktm = sb("s_ktm", [16, 128], BF16)
        s_vtm = sb("s_vtm", [16, DV], BF16)
        s_km = sb("s_km", [16, 1, 4, 128], BF16)
        s_Sb = sb("s_Sb", [128, 1, 4, DV], BF16)
        s_S = sb("s_S", [128, 3, 4, DV])
        s_q = sb("s_q", [128, NS])
        s_qgb = sb("s_qgb", [128, NS], BF16)
        s_pb = sb("s_pb", [128, 2, NS], BF16)
        s_eg = sb("s_eg", [128, NS])
        s_vq = sb("s_vq", [128, 2, NS])
        s_o = sb("s_o", [128, 2, NS])

        wstate = {"i": 0}

        def load_block(tile, bi):
            name, idx, n = PLAN[bi]
            gi = wstate["i"]
            wstate["i"] += 1
            slot = gi % NSLOT
            off = PLAN_OFFS[bi]
            P.add("pool", lambda e, slot=slot, off=off, n=n: e.dma_start(out=wring[slot][:, 0:n], in_=d_ws[:, off:off + n]),
                  writes=[("w", slot)], dma=("w", slot))
            return slot

        def wv3(slot, n, inner):
            return wring[slot][:, 0:n].rearrange("p (k c) -> p k c", c=inner)

        def wv4(slot):
            return wring[slot][:, 0:2048].rearrange("p (a k c) -> p a k c", a=2, k=8)

        def mm(out, lhsT, rhs, start, stop, reads, writes):
            P.add("pe", lambda e: e.matmul(out, lhsT=lhsT, rhs=rhs, start=start, stop=stop), reads=reads, writes=writes)

        def act(out, in_, func, reads, writes, bias=None, scale=None):
            kw = {}
            if bias is not None:
                kw["bias"] = bias
            if scale is not None:
                kw["scale"] = scale
            P.add("act", lambda e: e.activation(out=out, in_=in_, func=func, **kw), reads=reads, writes=writes)

        def tt(out, in0, in1, op, reads, writes, eng="dve"):
            P.add(eng, lambda e: e.tensor_tensor(out=out, in0=in0, in1=in1, op=op), reads=reads, writes=writes)

        def stt(out, in0, scalar, in1, op0, op1, reads, writes, eng="dve"):
            P.add(eng, lambda e: e.scalar_tensor_tensor(out=out, in0=in0, scalar=scalar, in1=in1, op0=op0, op1=op1), reads=reads, writes=writes)

        def ts(out, in0, s1, s2, op0, op1, reads, writes, eng="dve"):
            if s2 is None:
                P.add(eng, lambda e: e.tensor_scalar(out=out, in0=in0, scalar1=s1, scalar2=None, op0=op0), reads=reads, writes=writes)
            else:
                P.add(eng, lambda e: e.tensor_scalar(out=out, in0=in0, scalar1=s1, scalar2=s2, op0=op0, op1=op1), reads=reads, writes=writes)

        def cp(out, in_, reads, writes, eng="dve"):
            P.add(eng, lambda e: e.tensor_copy(out=out, in_=in_), reads=reads, writes=writes)

        def proj(c, s, lhsT_of_kc, rhs_chunks, nk, reads, M=128):
            for kc in range(nk):
                for (g0, g1) in c.groups:
                    mm(c.ps[s][0:M, g0:g1], lhsT_of_kc(kc), rhs_chunks(kc)[:, g0:g1], kc == 0, kc == nk - 1,
                       reads=reads(kc), writes=[pskey(c, s)])

        def stat_acc(c, src, src_reads, first, last, ones, statset=None):
            statset = statset or c.statset
            i = c.sqi % 2
            c.sqi += 1
            act(c.sq[:, i, :], src, AF.Square, reads=src_reads, writes=[K(c, "sq", i)])
            for (g0, g1) in c.groups:
                mm(c.ps[statset][:, g0:g1], ones, c.sq[:, i, g0:g1], first, last,
                   reads=[K(c, "sq", i), "cbf"], writes=[pskey(c, statset)])

        def rstd_from_stat(c, half, statset=None):
            statset = statset or c.statset
            flush_stat(c)
            if half:
                act(c.rstd[:, :], c.ps[statset][:, 0:c.W], AF.Ln, reads=[pskey(c, statset), "der"], writes=[K(c, "rstd")],
                    bias=der[:, DEPS4:DEPS4 + 1], scale=4.0)
            else:
                act(c.rstd[:, :], c.ps[statset][:, 0:c.W], AF.Ln, reads=[pskey(c, statset), "der"], writes=[K(c, "rstd")],
                    bias=der[:, DEPS:DEPS + 1], scale=1.0)
            act(c.rstd[:, :], c.rstd[:, :], AF.Exp, reads=[K(c, "rstd")], writes=[K(c, "rstd")], scale=-0.5)

        def flush_stat(c):
            if c.pend is not None:
                i, first, last = c.pend
                c.pend = None
                for (g0, g1) in c.groups:
                    mm(c.ps[c.statset][:, g0:g1], ones1024, c.sq[:, i, g0:g1], first, last,
                       reads=[K(c, "sq", i), "cbf"], writes=[pskey(c, c.statset)])

        def stat_all(c, src3, src_keys):
            act(c.sqall[:, :, :], src3, AF.Square, reads=src_keys, writes=[K(c, "sqall")])
            for ch in range(8):
                mm(c.ps["S"][:, 0:c.W], ones1024, c.sqall[:, ch, :], ch == 0, ch == 7, reads=[K(c, "sqall"), "cbf"], writes=[pskey(c, "S")])

        def prenorm_finish(c, gi):
            rstd_from_stat(c, False)
            for ch in range(8):
                stt(c.xn[:, ch, :], c.x[:, ch, :], par[:, P_GAIN + gi * 8 + ch:P_GAIN + gi * 8 + ch + 1], c.rstd[:, :], ALU.mult, ALU.mult,
                    reads=[K(c, "x", ch), K(c, "rstd"), "par"], writes=[K(c, "xn", ch)])

        def prenorm(c, gi):
            if c is cs_:
                s_prenorm(c, gi)
                return
            for ch in range(8):
                stat_acc(c, c.x[:, ch, :], [K(c, "x", ch)], ch == 0, ch == 7, ones1024)
            prenorm_finish(c, gi)

        def epilogue(c, gi, half, after_chunk=None):
            if c is cs_:
                s_epilogue(c, gi, half)
                return
            rstd_from_stat(c, half)
            for ch in range(8):
                stt(c.y[:, ch, :], c.y[:, ch, :], par[:, P_GAIN + gi * 8 + ch:P_GAIN + gi * 8 + ch + 1], c.rstd[:, :], ALU.mult, ALU.mult,
                    reads=[K(c, "y", ch), K(c, "rstd"), "par"], writes=[K(c, "y", ch)])
                tt(c.x[:, ch, :], c.x[:, ch, :], c.y[:, ch, :], ALU.add, reads=[K(c, "x", ch), K(c, "y", ch)], writes=[K(c, "x", ch)])
                if after_chunk is not None:
                    after_chunk(ch)

        def epilogue_prenorm(c, gi_post, half, gi_pre):
            if c is cs_:
                epilogue(c, gi_post, half)
                prenorm(c, gi_pre)
                return
            epilogue(c, gi_post, half, after_chunk=lambda ch: stat_acc(c, c.x[:, ch, :], [K(c, "x", ch)], ch == 0, ch == 7, ones1024))
            prenorm_finish(c, gi_pre)

        def out_chunk(c, s, m, first, last):
            act(c.y[:, m, :], c.ps[s][:, 0:c.W], AF.Copy, reads=[pskey(c, s)], writes=[K(c, "y", m)])
            if c is cs_:
                return
            i = c.sqi % 2
            c.sqi += 1
            act(c.sq[:, i, :], c.ps[s][:, 0:c.W], AF.Square, reads=[pskey(c, s)], writes=[K(c, "sq", i)])
            c.pend = (i, first, last)

        def ffn_up(c, slot, jb):
            w = wv3(slot, 8 * 512, 512)
            for jj in range(2):
                j = jb * 2 + jj
                sg_, su_ = nextset(c), nextset(c)
                proj(c, sg_, lambda kc: w[:, kc, jj * 128:(jj + 1) * 128], lambda kc: c.xn[:, kc, :], 8,
                     lambda kc: [("w", slot), K(c, "xn", kc)])
                proj(c, su_, lambda kc: w[:, kc, 256 + jj * 128:256 + (jj + 1) * 128], lambda kc: c.xn[:, kc, :], 8,
                     lambda kc: [("w", slot), K(c, "xn", kc)])
                ti = 6 + (j % 2)
                act(T(c, ti), c.ps[sg_][:, 0:c.W], AF.Silu, reads=[pskey(c, sg_)], writes=[K(c, "y", ti)])
                tt(c.hreg[:, j, :], T(c, ti), c.ps[su_][:, 0:c.W], ALU.mult, reads=[K(c, "y", ti), pskey(c, su_)], writes=[K(c, "h", j)])

        def ffn_down(c, slot, m):
            w = wv3(slot, NJ * 128, 128)
            s = nextset(c, avoid=("C",))
            proj(c, s, lambda j: w[:, j, :], lambda j: c.hreg[:, j, :], NJ, lambda j: [("w", slot), K(c, "h", j)])
            flush_stat(c)
            out_chunk(c, s, m, m == 0, m == 7)

        def merge_m(c, slot_g, slot_b, m):
            wg = wv3(slot_g, 8 * 256, 256)
            wb = wv4(slot_b)
            sA = nextset(c)
            proj(c, sA, lambda kc: wg[:, kc, 0:128], lambda kc: c.xn[:, kc, :], 8, lambda kc: [("w", slot_g), K(c, "xn", kc)])
            act(T(c, 0), c.ps[sA][:, 0:c.W], AF.Sigmoid, reads=[pskey(c, sA)], writes=[K(c, "y", 0)])
            sY = nextset(c)
            proj(c, sY, lambda n: wb[:, 0, n, :], lambda n: c.hreg[:, n, :], 8, lambda n: [("w", slot_b), K(c, "h", n)])
            tt(T(c, 0), T(c, 0), c.ps[sY][:, 0:c.W], ALU.mult, reads=[K(c, "y", 0), pskey(c, sY)], writes=[K(c, "y", 0)])
            sB = nextset(c)
            proj(c, sB, lambda kc: wg[:, kc, 128:256], lambda kc: c.xn[:, kc, :], 8, lambda kc: [("w", slot_g), K(c, "xn", kc)])
            act(T(c, 1), c.ps[sB][:, 0:c.W], AF.Sigmoid, reads=[pskey(c, sB)], writes=[K(c, "y", 1)])
            sY2 = nextset(c)
            proj(c, sY2, lambda n: wb[:, 1, n, :], lambda n: c.hreg[:, 8 + n, :], 8, lambda n: [("w", slot_b), K(c, "h", 8 + n)])
            tt(T(c, 1), T(c, 1), c.ps[sY2][:, 0:c.W], ALU.mult, reads=[K(c, "y", 1), pskey(c, sY2)], writes=[K(c, "y", 1)])
            tt(c.hreg[:, 16 + m, :], T(c, 0), T(c, 1), ALU.add, reads=[K(c, "y", 0), K(c, "y", 1)], writes=[K(c, "h", 16 + m)])

        def outproj(c, slot, i, m2):
            w = wv4(slot)
            s = nextset(c, avoid=("C",))
            proj(c, s, lambda m: w[:, i, m, :], lambda m: c.hreg[:, 16 + m, :], 8, lambda m: [("w", slot), K(c, "h", 16 + m)])
            flush_stat(c)
            out_chunk(c, s, m2, m2 == 0, m2 == 7)

        s_flat = lambda: s_u[:, :, :, :].rearrange("p a n t -> p (a n t)")
        g3 = lambda gi: par[:, P_GAIN + gi * 8:P_GAIN + gi * 8 + 8].rearrange("p (n o) -> p n o", o=1).to_broadcast([128, 8, NS])
        r3 = lambda c: c.rstd[:, :].rearrange("p (o t) -> p o t", o=1).to_broadcast([128, 8, NS])

        def s_prenorm(c, gi):
            stat_all(c, c.x[:, :, :], [K(c, "x", ch) for ch in range(8)])
            rstd_from_stat(c, False)
            tt(s_u[:, 5, :, :], c.x[:, :, :], g3(gi), ALU.mult, reads=[K(c, "x", ch) for ch in range(8)] + ["par"], writes=[("s_u", 5)])
            tt(c.xn[:, :, :], s_u[:, 5, :, :], r3(c), ALU.mult, reads=[("s_u", 5), K(c, "rstd")], writes=[K(c, "xn", ch) for ch in range(8)])

        def s_epilogue(c, gi, half):
            yk_ = [K(c, "y", ch) for ch in range(8)]
            stat_all(c, c.y[:, :, :], yk_)
            rstd_from_stat(c, half)
            tt(c.y[:, :, :], c.y[:, :, :], g3(gi), ALU.mult, reads=yk_ + ["par"], writes=yk_)
            tt(c.y[:, :, :], c.y[:, :, :], r3(c), ALU.mult, reads=yk_ + [K(c, "rstd")], writes=yk_)
            tt(c.x[:, :, :], c.x[:, :, :], c.y[:, :, :], ALU.add, reads=yk_ + [K(c, "x", ch) for ch in range(8)], writes=[K(c, "x", ch) for ch in range(8)])

        def s_ffn_up(c, slot, jb):
            w = wv3(slot, 8 * 512, 512)
            for jj in range(2):
                j = jb * 2 + jj
                for kc in range(8):
                    mm(ps[:, b6 + j * NS:b6 + (j + 1) * NS], w[:, kc, jj * 128:(jj + 1) * 128], c.xn[:, kc, :], kc == 0, kc == 7,
                       reads=[("w", slot), K(c, "xn", kc)], writes=BK(6))
                for kc in range(8):
                    mm(ps[:, b7 + j * NS:b7 + (j + 1) * NS], w[:, kc, 256 + jj * 128:256 + (jj + 1) * 128], c.xn[:, kc, :], kc == 0, kc == 7,
                       reads=[("w", slot), K(c, "xn", kc)], writes=BK(7))
            if jb == NJ // 2 - 1:
                sg = s_flat()[:, 0:NJ * NS]
                allu = [("s_u", i) for i in range(6)]
                act(sg, ps[:, b6:b6 + NJ * NS], AF.Silu, reads=BK(6), writes=allu)
                tt(c.hreg[:, 0:NJ, :].rearrange("p j t -> p (j t)"), sg, ps[:, b7:b7 + NJ * NS], ALU.mult, reads=allu + BK(7),
                   writes=[K(c, "h", j) for j in range(NJ)])

        def s_outchunks(c, slot_reads, lhs_of, rhs_of, nk, m, last):
            for k in range(nk):
                mm(ps[:, b6 + m * NS:b6 + (m + 1) * NS], lhs_of(k), rhs_of(k), k == 0, k == nk - 1, reads=slot_reads(k), writes=BK(6))
            if last:
                act(c.y[:, :, :], ps[:, b6:b6 + 8 * NS].rearrange("p (m t) -> p m t", t=NS), AF.Copy, reads=BK(6), writes=[K(c, "y", ch) for ch in range(8)])

        def s_ffn_down(c, slot, m):
            w = wv3(slot, NJ * 128, 128)
            s_outchunks(c, lambda j: [("w", slot), K(c, "h", j)], lambda j: w[:, j, :], lambda j: c.hreg[:, j, :], NJ, m, m == 7)

        def s_outproj(c, slot, i, m2):
            w = wv4(slot)
            s_outchunks(c, lambda m: [("w", slot), K(c, "h", 16 + m)], lambda m: w[:, i, m, :], lambda m: c.hreg[:, 16 + m, :], 8, m2, m2 == 7)

        def s_merge_m(c, slot_g, slot_b, m):
            wg = wv3(slot_g, 8 * 256, 256)
            wb = wv4(slot_b)
            cs0, cs1 = m * NS, (m + 1) * NS
            for kc in range(8):
                mm(ps[:, b6 + cs0:b6 + cs1], wg[:, kc, 0:128], c.xn[:, kc, :], kc == 0, kc == 7, reads=[("w", slot_g), K(c, "xn", kc)], writes=BK(6))
            for n in range(8):
                mm(ps[:, b6 + 128 + cs0:b6 + 128 + cs1], wb[:, 0, n, :], c.hreg[:, n, :], n == 0, n == 7, reads=[("w", slot_b), K(c, "h", n)], writes=BK(6))
            for kc in range(8):
                mm(ps[:, b7 + cs0:b7 + cs1], wg[:, kc, 128:256], c.xn[:, kc, :], kc == 0, kc == 7, reads=[("w", slot_g), K(c, "xn", kc)], writes=BK(7))
            for n in range(8):
                mm(ps[:, b7 + 128 + cs0:b7 + 128 + cs1], wb[:, 1, n, :], c.hreg[:, 8 + n, :], n == 0, n == 7, reads=[("w", slot_b), K(c, "h", 8 + n)], writes=BK(7))
            if m == 7:
                f = s_flat()
                allu = [("s_u", i) for i in range(6)]
                act(f[:, 0:128], ps[:, b6:b6 + 128], AF.Sigmoid, reads=BK(6), writes=allu)
                act(f[:, 128:256], ps[:, b7:b7 + 128], AF.Sigmoid, reads=BK(7), writes=allu)
                tt(f[:, 0:128], f[:, 0:128], ps[:, b6 + 128:b6 + 256], ALU.mult, reads=allu + BK(6), writes=allu)
                tt(f[:, 128:256], f[:, 128:256], ps[:, b7 + 128:b7 + 256], ALU.mult, reads=allu + BK(7), writes=allu)
                tt(c.hreg[:, 16:24, :].rearrange("p m t -> p (m t)"), f[:, 0:128], f[:, 128:256], ALU.add, reads=allu,
                   writes=[K(c, "h", 16 + mm_) for mm_ in range(8)])

        pcol = lambda base, i: par[:, base + i:base + i + 1]

        def Hs(c, k):
            ap = c.hreg[:, 16 + 2 * k:18 + 2 * k, :].rearrange("p a w -> p (a w)").bitcast(F32)
            return ap, [K(c, "h", 16 + 2 * k), K(c, "h", 17 + 2 * k)]

        def rnn_slots(c, p):
            if p == 0:
                d = {"gl": (T(c, 0), K(c, "y", 0)), "r": (T(c, 1), K(c, "y", 1)), "i": (T(c, 2), K(c, "y", 2)),
                     "a": (T(c, 3), K(c, "y", 3)), "m": (T(c, 7), K(c, "y", 7)), "xc": Hs(c, 2)}
            else:
                d = {"gl": (T(c, 4), K(c, "y", 4)), "r": (T(c, 5), K(c, "y", 5)), "i": (T(c, 6), K(c, "y", 6)),
                     "a": Hs(c, 0), "m": Hs(c, 1), "xc": Hs(c, 3)}
            return d

        def rnn_parts(c, slot, n, tile):
            p = n % 2
            R = rnn_slots(c, p)
            xcb_p = c.hreg[:, 8 + p, :]
            xcbk = K(c, "h", 8 + p)
            xrb = c.hreg[:, 10 + 2 * p:12 + 2 * p, :].rearrange("p a w -> p (a w)")
            xk = [K(c, "h", 10 + 2 * p), K(c, "h", 11 + 2 * p)]
            dgp = c.hreg[:, 14 + p, :]
            dgk = K(c, "h", 14 + p)
            st = {}

            def A1():
                w = wv3(slot, 8 * 256, 256)
                sx = nextset(c)
                proj(c, sx, lambda kc: w[:, kc, 0:128], lambda kc: c.xn[:, kc, :], 8, lambda kc: [("w", slot), K(c, "xn", kc)])
                sy = nextset(c)
                proj(c, sy, lambda kc: w[:, kc, 128:256], lambda kc: c.xn[:, kc, :], 8, lambda kc: [("w", slot), K(c, "xn", kc)])
                cp(xrb[:, 0:3], halo_bf[:, n, :], reads=["halo_bf"], writes=[xk])
                cp(xrb[:, 3:TW + 3], c.ps[sx][:, 0:TW], reads=[pskey(c, sx)], writes=[xk])
                cp(halo[:, n, :], c.ps[sx][:, TW - 3:TW], reads=[pskey(c, sx)], writes=["halo"])
                cp(halo_bf[:, n, :], xrb[:, TW:TW + 3], reads=[xk], writes=["halo_bf"])
                idb = cst[:, C_IDENT - 256:C_IDENT - 256 + 128].rearrange("p (o c) -> p o c", o=1).to_broadcast([128, 4, 128])
                cwb = par[:, P_CW + n:P_CW + n + 25:8].rearrange("p (j o) -> p j o", o=1).to_broadcast([128, 4, 128])
                tt(dgp[:, 0:512].rearrange("p (j c) -> p j c", c=128), idb, cwb, ALU.mult, reads=["cst", "par"], writes=[dgk])
                act(R["gl"][0], c.ps[sy][:, 0:TW], AF.Gelu_apprx_tanh, reads=[pskey(c, sy)], writes=[R["gl"][1]])
                sc = nextset(c)
                for j in range(4):
                    for (g0, g1) in c.groups:
                        mm(c.ps[sc][:, g0:g1], dgp[:, j * 128:(j + 1) * 128], xrb[:, j + g0:j + g1], j == 0, j == 3, reads=[dgk, xk], writes=[pskey(c, sc)])
                st["sc"] = sc

            def A2():
                sc = st["sc"]
                act(xcb_p, c.ps[sc][:, 0:TW], AF.Identity, reads=[pskey(c, sc), "par"], writes=[xcbk], bias=pcol(P_CB, n))
                ts(R["xc"][0], c.ps[sc][:, 0:TW], pcol(P_CB, n), None, ALU.add, None, reads=[pskey(c, sc), "par"], writes=[R["xc"][1]])
                s1 = nextset(c)
                for (g0, g1) in c.groups:
                    mm(c.ps[s1][:, g0:g1], rgw[:, 0, n, :], xcb_p[:, g0:g1], True, True, reads=["rgw", xcbk], writes=[pskey(c, s1)])
                s2 = nextset(c)
                for (g0, g1) in c.groups:
                    mm(c.ps[s2][:, g0:g1], rgw[:, 1, n, :], xcb_p[:, g0:g1], True, True, reads=["rgw", xcbk], writes=[pskey(c, s2)])
                act(R["r"][0], c.ps[s1][:, 0:TW], AF.Tanh, reads=[pskey(c, s1), "der"], writes=[R["r"][1]], scale=0.5, bias=der[:, DHBA + n:DHBA + n + 1])
                act(R["i"][0], c.ps[s2][:, 0:TW], AF.Tanh, reads=[pskey(c, s2), "der"], writes=[R["i"][1]], scale=0.5, bias=der[:, DHBX + n:DHBX + n + 1])

            def B_act():
                act(R["a"][0], R["r"][0], AF.Exp, reads=[R["r"][1], "der"], writes=[R["a"][1]], scale=der[:, DHC + n:DHC + n + 1], bias=der[:, DHC + n:DHC + n + 1])
                act(R["m"][0], R["a"][0], AF.Square, reads=[R["a"][1]], writes=[R["m"][1]])
                act(R["m"][0], R["m"][0], AF.Sqrt, reads=[R["m"][1], "der"], writes=[R["m"][1]], scale=-0.25, bias=der[:, DQ25:DQ25 + 1])

            def B_dve():
                if tile == 0:
                    P.add("dve", lambda e: e.memset(R["m"][0][:, 0:1], 0.5), reads=[R["m"][1]], writes=[R["m"][1]])
                stt(R["i"][0], R["i"][0], 1.0, R["m"][0], ALU.add, ALU.mult, reads=[R["i"][1], R["m"][1]], writes=[R["i"][1]])
                tt(R["i"][0], R["i"][0], R["xc"][0], ALU.mult, reads=[R["i"][1], R["xc"][1]], writes=[R["i"][1]])
                P.add("dve", lambda e: e.tensor_tensor_scan(out=R["r"][0], data0=R["a"][0], data1=R["i"][0], initial=hcar[:, n:n + 1], op0=ALU.mult, op1=ALU.add),
                      reads=[R["a"][1], R["i"][1], R["r"][1], "hcar"], writes=[R["r"][1]])
                cp(hcar[:, n:n + 1], R["r"][0][:, TW - 1:TW], reads=[R["r"][1]], writes=["hcar"])
                tt(c.hreg[:, n, :], R["r"][0], R["gl"][0], ALU.mult, reads=[R["r"][1], R["gl"][1]], writes=[K(c, "h", n)])

            return A1, A2, B_act, B_dve

        s_xrp = ps[:, b6 + 128:b6 + 256]
        s_yrp = ps[:, b7 + 128:b7 + 256]

        def rnn_sample_proj(c, slot, n):
            w = wv3(slot, 8 * 256, 256)
            for kc in range(8):
                mm(s_xrp[:, n * NS:(n + 1) * NS], w[:, kc, 0:128], c.xn[:, kc, :], kc == 0, kc == 7, reads=[("w", slot), K(c, "xn", kc)], writes=BK(6))
            for kc in range(8):
                mm(s_yrp[:, n * NS:(n + 1) * NS], w[:, kc, 128:256], c.xn[:, kc, :], kc == 0, kc == 7, reads=[("w", slot), K(c, "xn", kc)], writes=BK(7))

        def rnn_sample_tail(c, part):
            v3 = lambda ap: ap.rearrange("p (n t) -> p n t", t=NS)
            bc = lambda col: par[:, col:col + 8].rearrange("p (n o) -> p n o", o=1).to_broadcast([128, 8, NS])
            dbc = lambda col: der[:, col:col + 8].rearrange("p (n o) -> p n o", o=1).to_broadcast([128, 8, NS])
            U = lambda i: s_u[:, i, :, :]
            uk = lambda i: ("s_u", i)
            if part == "a":
                act(s_xr[:, :, :], v3(s_xrp), AF.Copy, reads=BK(6), writes=["s_xr"])
                act(U(0), v3(s_yrp), AF.Gelu_apprx_tanh, reads=BK(7), writes=[uk(0)])
                tt(U(1), s_c0[:, :, :, 0], bc(P_CW + 0), ALU.mult, reads=["s_c0", "par"], writes=[uk(1)])
                for j in (1, 2):
                    tt(U(2), s_c0[:, :, :, j], bc(P_CW + j * 8), ALU.mult, reads=["s_c0", "par"], writes=[uk(2)])
                    tt(U(1), U(1), U(2), ALU.add, reads=[uk(1), uk(2)], writes=[uk(1)])
                tt(U(2), s_xr[:, :, :], bc(P_CW + 3 * 8), ALU.mult, reads=["s_xr", "par"], writes=[uk(2)])
                tt(U(1), U(1), U(2), ALU.add, reads=[uk(1), uk(2)], writes=[uk(1)])
                tt(U(1), U(1), bc(P_CB), ALU.add, reads=[uk(1), "par"], writes=[uk(1)])
                act(s_xcb3[:, :, :], U(1), AF.Copy, reads=[uk(1)], writes=["s_xcb3"])
                return
            for n in range(8):
                mm(s_xrp[:, n * NS:(n + 1) * NS], rgw[:, 0, n, :], s_xcb3[:, n, :], True, True, reads=["rgw", "s_xcb3"], writes=BK(6))
                mm(s_yrp[:, n * NS:(n + 1) * NS], rgw[:, 1, n, :], s_xcb3[:, n, :], True, True, reads=["rgw", "s_xcb3"], writes=BK(7))
            tt(U(2), v3(s_xrp), bc(P_BA), ALU.add, reads=BK(6) + ["par"], writes=[uk(2)])
            tt(U(3), v3(s_yrp), bc(P_BX), ALU.add, reads=BK(7) + ["par"], writes=[uk(3)])
            act(U(2), U(2), AF.Sigmoid, reads=[uk(2)], writes=[uk(2)])
            act(U(3), U(3), AF.Sigmoid, reads=[uk(3)], writes=[uk(3)])
            tt(U(2), U(2), dbc(DC), ALU.mult, reads=[uk(2), "der"], writes=[uk(2)])
            act(U(4), U(2), AF.Exp, reads=[uk(2)], writes=[uk(4)])
            act(U(5), U(2), AF.Exp, reads=[uk(2)], writes=[uk(5)], scale=2.0)
            act(U(5), U(5), AF.Sqrt, reads=[uk(5)], writes=[uk(5)], scale=-1.0, bias=1.0)
            tt(U(3), U(3), U(5), ALU.mult, reads=[uk(3), uk(5)], writes=[uk(3)])
            tt(U(3), U(3), U(1), ALU.mult, reads=[uk(3), uk(1)], writes=[uk(3)])
            tt(U(4), U(4), s_h0[:, :, :], ALU.mult, reads=[uk(4), "s_h0"], writes=[uk(4)])
            tt(s_hn[:, :, :], U(4), U(3), ALU.add, reads=[uk(4), uk(3)], writes=["s_hn"])
            tt(c.hreg[:, 0:8, :], s_hn[:, :, :], U(0), ALU.mult, reads=["s_hn", uk(0)], writes=[K(c, "h", n) for n in range(8)])

        def gla_lr(c, lrb, lrkey):
            s = nextset(c)
            proj(c, s, lambda kc: wlrin[:, kc, :], lambda kc: c.xn[:, kc, :], 8, lambda kc: ["wlrin", K(c, "xn", kc)], M=16)
            act(lrb, c.ps[s][0:16, 0:c.W], AF.Copy, reads=[pskey(c, s)], writes=[lrkey])

        def gla_decay(c, hd, lrb, lrkey, Tl, kl, s=None):
            s = s or nextset(c)
            for (g0, g1) in c.groups:
                mm(c.ps[s][:, g0:g1], wlr[0:16, hd * 128:(hd + 1) * 128], lrb[0:16, g0:g1], True, True,
                   reads=["wlr", lrkey], writes=[pskey(c, s)])
            act(Tl, c.ps[s][:, 0:c.W], AF.Exp, reads=[pskey(c, s), "der"], writes=[kl], scale=-1.0, bias=der[:, DNB + hd:DNB + hd + 1])
            act(Tl, Tl, AF.Ln, reads=[kl], writes=[kl], bias=1.0)

        def gla_gate_out(c, hd, slot_v, o_aps, o_keys, Trs, krs, Tsg, ksg, Tt, ktt, stat_ps, stat_key, sg_pre=None):
            wv = wv3(slot_v, 8 * 512, 512)
            for (g0, g1) in c.groups:
                for dvc in range(2):
                    i = c.sqi % 2
                    c.sqi += 1
                    act(c.sq[:, i, g0:g1], o_aps[dvc][:, g0:g1], AF.Square, reads=[o_keys[dvc]], writes=[K(c, "sq", i)])
                    mm(stat_ps[:, 0:g1 - g0], ones256, c.sq[:, i, g0:g1], dvc == 0, dvc == 1, reads=[K(c, "sq", i), "cbf"], writes=stat_key)
                act(Trs[:, g0:g1], stat_ps[:, 0:g1 - g0], AF.Ln, reads=stat_key + ["der"], writes=[krs], bias=der[:, DEPS:DEPS + 1], scale=1.0)
            act(Trs, Trs, AF.Exp, reads=[krs], writes=[krs], scale=-0.5)
            for dvc in range(2):
                if sg_pre is not None:
                    Tsg_, ksg_ = sg_pre[dvc]
                else:
                    s = c.gla_og or nextset(c)
                    proj(c, s, lambda kc: wv[:, kc, 256 + dvc * 128:256 + (dvc + 1) * 128], lambda kc: c.xn[:, kc, :], 8,
                         lambda kc: [("w", slot_v), K(c, "xn", kc)])
                    act(Tsg, c.ps[s][:, 0:c.W], AF.Silu, reads=[pskey(c, s)], writes=[ksg])
                    Tsg_, ksg_ = Tsg, ksg
                stt(Tt, o_aps[dvc][:, 0:c.W], pcol(P_GNG, dvc), Trs, ALU.mult, ALU.mult, reads=[o_keys[dvc], "par", krs], writes=[ktt])
                tt(c.hreg[:, 8 + hd * 2 + dvc, :], Tt, Tsg_, ALU.mult, reads=[ktt, ksg_], writes=[K(c, "h", 8 + hd * 2 + dvc)])

        def gla_prompt(c, slot_qk, slot_v, hd, tile):
            wq = wv3(slot_qk, 8 * 256, 256)
            wv = wv3(slot_v, 8 * 512, 512)
            yk = lambda i: K(c, "y", i)
            gla_decay(c, hd, lr_bf, "lr_bf", T(c, 0), yk(0), s="C")
            P.add("dve", lambda e: e.tensor_tensor_scan(out=T(c, 1), data0=cst[:, C_RMASK - 256:C_RMASK - 256 + TW], data1=T(c, 0), initial=0.0, op0=ALU.mult, op1=ALU.add),
                  reads=[yk(0), "cst"], writes=[yk(1)])
            act(T(c, 2), T(c, 1), AF.Exp, reads=[yk(1)], writes=[yk(2)], scale=-1.0 / 16.0)
            act(T(c, 3), T(c, 1), AF.Exp, reads=[yk(1)], writes=[yk(3)], scale=1.0 / 16.0)
            qi = c.y[:, 4, :].bitcast(BF16)[:, 0:TW]
            ki = c.y[:, 4, :].bitcast(BF16)[:, TW:2 * TW]
            attT = c.y[:, 5, :].bitcast(BF16)[:, 0:TW]
            kiT = c.y[:, 5, :].bitcast(BF16)[:, TW:2 * TW]
            v_bf = c.y[:, 6, :].bitcast(BF16).rearrange("p (b v) -> p b v", v=DV)
            def vhalf(hf, sv):
                for bb in range(4):
                    bk = hf * 4 + bb
                    for kc in range(8):
                        mm(c.ps[sv][:, bb * DV:(bb + 1) * DV], c.xn[:, kc, bk * 128:(bk + 1) * 128], wv[:, kc, 0:DV], kc == 0, kc == 7,
                           reads=[("w", slot_v), K(c, "xn", kc)], writes=[pskey(c, sv)])
                act(c.y[:, 6, :].bitcast(BF16)[:, hf * TW:(hf + 1) * TW], c.ps[sv][:, 0:TW], AF.Copy, reads=[pskey(c, sv)], writes=[(c.name, "y", 6, hf)])
            sq_ = "A"
            proj(c, sq_, lambda kc: wq[:, kc, 0:128], lambda kc: c.xn[:, kc, :], 8, lambda kc: [("w", slot_qk), K(c, "xn", kc)])
            sk_ = "B"
            proj(c, sk_, lambda kc: wq[:, kc, 128:256], lambda kc: c.xn[:, kc, :], 8, lambda kc: [("w", slot_qk), K(c, "xn", kc)])
            stt(qi, c.ps[sq_][:, 0:TW], float(DK) ** -0.5, T(c, 2), ALU.mult, ALU.mult, reads=[pskey(c, sq_), yk(2)], writes=[(c.name, "y", 4, 0)])
            tt(ki, c.ps[sk_][:, 0:TW], T(c, 3), ALU.mult, reads=[pskey(c, sk_), yk(3)], writes=[(c.name, "y", 4, 1)])
            vhalf(0, "C")
            sa = "A"
            for bk in range(8):
                b0, b1 = bk * 128, (bk + 1) * 128
                mm(c.ps[sa][:, b0:b1], ki[:, b0:b1], qi[:, b0:b1], True, True, reads=[(c.name, "y", 4, 1), (c.name, "y", 4, 0)], writes=[pskey(c, sa)])
            maskb = cst[:, C_MASK - 256:C_MASK - 256 + 128].rearrange("p (o c) -> p o c", o=1).to_broadcast([128, 8, 128])
            tt(attT.rearrange("p (b c) -> p b c", c=128), c.ps[sa][:, 0:TW].rearrange("p (b c) -> p b c", c=128), maskb, ALU.mult,
               reads=[pskey(c, sa), "cst"], writes=[(c.name, "y", 5, 0)])
            vhalf(1, "B")
            psMb = c.psM.bitcast(BF16)
            for bk in range(8):
                b0, b1 = bk * 128, (bk + 1) * 128
                P.add("pe", lambda e, b0=b0, b1=b1: e.transpose(psMb[:, b0:b1], ki[:, b0:b1], identb),
                      reads=[(c.name, "y", 4, 1), "cbf"], writes=c.keyM)
            act(kiT, psMb[:, 0:TW], AF.Copy, reads=c.keyM, writes=[(c.name, "y", 5, 1)])
            osets = ["A", "B"]
            dSb = ps[:, 2048:2560]
            for bk in range(8):
                b0, b1 = bk * 128, (bk + 1) * 128
                sbi = bk % 2
                for dvc in range(2):
                    mm(c.ps[osets[dvc]][:, b0:b1], v_bf[:, bk, dvc * 128:(dvc + 1) * 128], attT[:, b0:b1], True, False,
                       reads=[K(c, "y", 6), (c.name, "y", 5, 0)], writes=[pskey(c, osets[dvc])])
                    mm(c.ps[osets[dvc]][:, b0:b1], S_b[:, sbi, dvc * 128:(dvc + 1) * 128], qi[:, b0:b1], False, True,
                       reads=[("S_b", sbi), (c.name, "y", 4, 0)], writes=[pskey(c, osets[dvc])])
                mk = BK(4)
                dS = dSb[:, (bk % 2) * 256:(bk % 2) * 256 + 256]
                mm(dS, kiT[:, b0:b1], v_bf[:, bk, :], True, True, reads=[(c.name, "y", 5, 1), K(c, "y", 6)], writes=mk)
                tt(S_t[:, :], S_f[:, hd, :], dS, ALU.add, reads=[("S_f", hd)] + mk, writes=["S_t"])
                eql = T(c, 2)[:, b1 - 1:b1]
                ts(S_b[:, 1 - sbi, :], S_t[:, :], eql, None, ALU.mult, None, reads=["S_t", yk(2)], writes=[("S_b", 1 - sbi)])
                ts(S_f[:, hd, :], S_t[:, :], eql, None, ALU.mult, None, reads=["S_t", yk(2)], writes=[("S_f", hd)])
                pi = bk // 2
                dvc, hf = pi // 2, pi % 2
                sgT, sgk = (T(c, 0), yk(0)) if dvc == 0 else (T(c, 3), yk(3))
                for kc in range(4 * (bk % 2), 4 * (bk % 2) + 4):
                    mm(c.psM[:, 0:512], wv[:, kc, 256 + dvc * 128:256 + (dvc + 1) * 128], c.xn[:, kc, hf * 512:(hf + 1) * 512], kc == 0, kc == 7,
                       reads=[("w", slot_v), K(c, "xn", kc)], writes=c.keyM)
                if bk % 2 == 1:
                    act(sgT[:, hf * 512:(hf + 1) * 512], c.psM[:, 0:512], AF.Silu, reads=c.keyM, writes=[sgk])
            c.gla_og = "C"
            gla_gate_out(c, hd, slot_v, [c.ps["A"], c.ps["B"]], [pskey(c, "A"), pskey(c, "B")], T(c, 7), yk(7), None, None, T(c, 1), yk(1),
                         c.psM, c.keyM, sg_pre=[(T(c, 0), yk(0)), (T(c, 3), yk(3))])

        s_state = {"loaded": set(), "q": 0}

        def s_piece_load(q):
            if q >= 16 or q in s_state["loaded"]:
                return
            s_state["loaded"].add(q)
            hd, p = q // 4, q % 4
            i = q % 3
            P.add("sp", lambda e: e.dma_start(out=s_S[:, i, :, :], in_=d_s0[p * 4:(p + 1) * 4, hd, :, :].rearrange("b k v -> k b v")),
                  writes=[("s_S", i)], dma=("s_S_in", i))

        def gla_sample(c, slot_qk, slot_v, hd):
            wq = wv3(slot_qk, 8 * 256, 256)
            wv = wv3(slot_v, 8 * 512, 512)
            tm = lambda i: s_t[:, i, :]
            tk = lambda i: ("s_t", i)
            s_piece_load(hd * 4)
            s_piece_load(hd * 4 + 1)
            gla_decay(c, hd, s_lr, "s_lr", tm(0), tk(0))
            act(s_eg[:, :], tm(0), AF.Exp, reads=[tk(0)], writes=["s_eg"], scale=-1.0 / 16.0)
            sq_ = nextset(c)
            proj(c, sq_, lambda kc: wq[:, kc, 0:128], lambda kc: c.xn[:, kc, :], 8, lambda kc: [("w", slot_qk), K(c, "xn", kc)])
            ts(s_q[:, :], c.ps[sq_][:, 0:NS], float(DK) ** -0.5, None, ALU.mult, None, reads=[pskey(c, sq_)], writes=["s_q"])
            tt(s_qgb[:, :], s_q[:, :], s_eg[:, :], ALU.mult, reads=["s_q", "s_eg"], writes=["s_qgb"])
            sk_ = nextset(c)
            proj(c, sk_, lambda kc: wq[:, kc, 128:256], lambda kc: c.xn[:, kc, :], 8, lambda kc: [("w", slot_qk), K(c, "xn", kc)])
            tt(tm(1), c.ps[sk_][:, 0:NS], s_q[:, :], ALU.mult, reads=[pskey(c, sk_), "s_q"], writes=[tk(1)])
            qk_ps = ps[:, b7 + 48:b7 + 64]
            cp(s_pb[:, 0, :], tm(1), reads=[tk(1)], writes=["s_pb"])
            tt(s_pb[:, 1, :], tm(1), s_pb[:, 0, :], ALU.subtract, reads=[tk(1), "s_pb"], writes=["s_pb"])
            mm(qk_ps, ones1024, s_pb[:, 0, :], True, False, reads=["cbf", "s_pb"], writes=BK(7))
            mm(qk_ps, ones1024, s_pb[:, 1, :], False, True, reads=["cbf", "s_pb"], writes=BK(7))
            ts(tm(2), qk_ps, 1024.0, None, ALU.mult, None, reads=BK(7), writes=[tk(2)])
            for dvc in range(2):
                sv = nextset(c)
                proj(c, sv, lambda kc: wv[:, kc, dvc * 128:(dvc + 1) * 128], lambda kc: c.xn[:, kc, :], 8, lambda kc: [("w", slot_v), K(c, "xn", kc)])
                tt(s_vq[:, dvc, :], c.ps[sv][:, 0:NS], tm(2), ALU.mult, reads=[pskey(c, sv), tk(2)], writes=[("s_vq", dvc)])
            bor = cp_.ps["C"]
            bk_ = cp_.banks["C"]
            for kc in range(8):
                mm(bor[0:NS, 0:128], c.xn[:, kc, :], wq[:, kc, 128:256], kc == 0, kc == 7, reads=[("w", slot_qk), K(c, "xn", kc)], writes=bk_)
            act(s_ktm[:, :], bor[0:NS, 0:128], AF.Copy, reads=bk_, writes=["s_ktm"])
            for kc in range(8):
                mm(bor[0:NS, 128:128 + DV], c.xn[:, kc, :], wv[:, kc, 0:DV], kc == 0, kc == 7, reads=[("w", slot_v), K(c, "xn", kc)], writes=bk_)
            act(s_vtm[:, :], bor[0:NS, 128:128 + DV], AF.Copy, reads=bk_, writes=["s_vtm"])
            so = [bor[:, 512:512 + NS], bor[:, 512 + NS:512 + 2 * NS]]
            dSp = [ps[:, b6:b6 + 256], ps[:, b6 + 256:b6 + 512], ps[:, b7:b7 + 256], ps[:, b7 + 256:b7 + 512]]
            dSk = [BK(6), BK(6), BK(7), BK(7)]
            for p in range(4):
                q = hd * 4 + p
                i = q % 3
                s_piece_load(q + 2)
                act(s_Sb[:, 0, :, :], s_S[:, i, :, :], AF.Copy, reads=[("s_S", i)], writes=[("s_Sb", 0)])
                for bb in range(4):
                    b = p * 4 + bb
                    for dvc in range(2):
                        mm(so[dvc][:, b:b + 1], s_Sb[:, 0, bb, dvc * 128:(dvc + 1) * 128], s_qgb[:, b:b + 1], True, True,
                           reads=[("s_Sb", 0), "s_qgb"], writes=bk_)
                ktb = s_ktm[:, :].rearrange("t (o k) -> t o k", o=1).to_broadcast([NS, 4, 128])
                dlt = cst[0:NS, C_IDENT - 256 + p * 4:C_IDENT - 256 + p * 4 + 4].rearrange("t (b o) -> t b o", o=1).to_broadcast([NS, 4, 128])
                tt(s_km[:, 0, :, :], ktb, dlt, ALU.mult, reads=["s_ktm", "cst"], writes=[("s_km", 0)])
                for bb in range(4):
                    mm(dSp[bb], s_km[:, 0, bb, :], s_vtm[:, :], True, True, reads=[("s_km", 0), "s_vtm"], writes=dSk[bb])
                for bb in range(4):
                    b = p * 4 + bb
                    stt(s_S[:, i, bb, :], s_S[:, i, bb, :], s_eg[:, b:b + 1], dSp[bb], ALU.mult, ALU.add,
                        reads=[("s_S", i), "s_eg"] + dSk[bb], writes=[("s_S", i)])
                P.add("sp", lambda e, p=p, i=i: e.dma_start(out=o_ss[p * 4:(p + 1) * 4, hd, :, :].rearrange("b k v -> k b v"), in_=s_S[:, i, :, :]),
                      reads=[("s_S", i)], dma=("s_S_out", i))
            for dvc in range(2):
                tt(s_o[:, dvc, :], so[dvc], s_vq[:, dvc, :], ALU.add, reads=bk_ + [("s_vq", dvc)], writes=[("s_o", dvc)])
            c.gla_og = None
            gla_gate_out(c, hd, slot_v, [s_o[:, 0, :], s_o[:, 1, :]], [("s_o", 0), ("s_o", 1)], tm(7), tk(7), tm(8), tk(8), tm(9), tk(9),
                         c.psM, c.keyM)

        cp_.gla_avoid = ()
        cs_.gla_avoid = ()

        P.add("dve", lambda e: e.memset(halo[:, :, :], 0.0), writes=["halo"])
        P.add("dve", lambda e: e.memset(halo_bf[:, :, :], 0.0), writes=["halo_bf"])
        P.add("dve", lambda e: e.memset(hcar[:, :], 0.0), writes=["hcar"])
        P.add("dve", lambda e: e.memset(S_f[:, :, :], 0.0), writes=[("S_f", h) for h in range(NH)])
        P.add("sp", lambda e: e.dma_start(out=cs_.x[:, :, :], in_=d_xs), writes=[K(cs_, "x", ch) for ch in range(8)], dma="s_x_in")
        P.add("sp", lambda e: e.dma_start(out=s_h0[:, :, :], in_=d_h0), writes=["s_h0"], dma="s_h0")
        P.add("sp", lambda e: e.dma_start(out=s_c0[:, :, :, :], in_=d_c0), writes=["s_c0"], dma="s_c0")

        def load_x(tile, ch, eng="sp"):
            t0 = tile * TW
            P.add(eng, lambda e: e.dma_start(out=cp_.x[:, ch, :], in_=d_x[:, ch, t0:t0 + TW]), writes=[K(cp_, "x", ch)], dma=("xin_" + eng, ch))

        def store_y(tile, ch):
            t0 = tile * TW
            P.add("sp", lambda e: e.dma_start(out=o_y[:, ch, t0:t0 + TW], in_=cp_.x[:, ch, :]), reads=[K(cp_, "x", ch)], dma=("yout", ch))

        for ch in range(8):
            load_x(0, ch, "sp" if ch < 4 else "pool")
        deferred = []

        def run_deferred():
            while deferred:
                deferred.pop(0)()

        for tile in range(NT):
            ctxs = [cp_] + ([cs_] if tile == NT - 1 else [])
            bi = 0

            def ffn_body(gi_pre, bi):
                P.tag = "t%d:ffn%d_up" % (tile, gi_pre)
                for jb in range(NJ // 2):
                    slot = load_block(tile, bi); bi += 1
                    for c in ctxs:
                        if c is cs_:
                            run_deferred()
                        (s_ffn_up if c is cs_ else ffn_up)(c, slot, jb)
                P.tag = "t%d:ffn%d_dn" % (tile, gi_pre)
                for m in range(8):
                    slot = load_block(tile, bi); bi += 1
                    for c in ctxs:
                        (s_ffn_down if c is cs_ else ffn_down)(c, slot, m)
                return bi

            P.tag = "t%d:prenorm0" % tile
            for c in ctxs:
                if c is cs_:
                    deferred.append(lambda c=c: prenorm(c, 0))
                else:
                    prenorm(c, 0)
            bi = ffn_body(0, bi)
            P.tag = "t%d:epi1" % tile
            for c in ctxs:
                if c is cs_:
                    deferred.append(lambda c=c: epilogue_prenorm(c, 1, True, 2))
                else:
                    epilogue_prenorm(c, 1, True, 2)
            P.tag = "t%d:lr" % tile
            gla_lr(cp_, lr_bf[:, :], "lr_bf")
            if cs_ in ctxs:
                deferred.append(lambda: gla_lr(cs_, s_lr[:, :], "s_lr"))
            P.tag = "t%d:rnn" % tile
            prev = None
            for n in range(8):
                slot = load_block(tile, bi); bi += 1
                A1, A2, Ba, Bd = rnn_parts(cp_, slot, n, tile)
                if prev is not None:
                    prev[0]()
                A1()
                if cs_ in ctxs:
                    run_deferred()
                    rnn_sample_proj(cs_, slot, n)
                A2()
                if prev is not None:
                    prev[1]()
                prev = (Ba, Bd)
            prev[0]()
            prev[1]()
            if cs_ in ctxs:
                rnn_sample_tail(cs_, "a")
            P.tag = "t%d:gla" % tile
            for hd in range(NH):
                P.add("dve", lambda e: e.memset(S_b[:, 0, :], 0.0), writes=[("S_b", 0)]) if tile == 0 else None
                if tile > 0:
                    cp(S_b[:, 0, :], S_f[:, hd, :], reads=[("S_f", hd)], writes=[("S_b", 0)])
                slot_qk = load_block(tile, bi); bi += 1
                slot_v = load_block(tile, bi); bi += 1
                gla_prompt(cp_, slot_qk, slot_v, hd, tile)
                if cs_ in ctxs and hd == 0:
                    rnn_sample_tail(cs_, "b")
                if cs_ in ctxs:
                    gla_sample(cs_, slot_qk, slot_v, hd)
            if tile == NT - 1:
                P.tag = "t%d:stateout" % tile
                P.add("sp", lambda e: e.dma_start(out=o_hp, in_=hcar[:, :]), reads=["hcar"], dma="o_hp")
                P.add("sp", lambda e: e.dma_start(out=o_cp, in_=halo[:, :, :]), reads=["halo"], dma="o_cp")
                P.add("sp", lambda e: e.dma_start(out=o_sp, in_=S_f[:, :, :]), reads=[("S_f", h) for h in range(NH)], dma="o_sp")
                P.add("sp", lambda e: e.dma_start(out=o_hs, in_=s_hn[:, :, :]), reads=["s_hn"], dma="o_hs")
                cp(s_cn[:, :, :, 0:2], s_c0[:, :, :, 1:3], reads=["s_c0"], writes=["s_cn"])
                cp(s_cn[:, :, :, 2], s_xr[:, :, :], reads=["s_xr", "s_cn"], writes=["s_cn"])
                P.add("sp", lambda e: e.dma_start(out=o_cs, in_=s_cn[:, :, :, :]), reads=["s_cn"], dma="o_cs")
            P.tag = "t%d:merge" % tile
            for m in range(8):
                slot_g = load_block(tile, bi); bi += 1
                slot_b = load_block(tile, bi); bi += 1
                for c in ctxs:
                    (s_merge_m if c is cs_ else merge_m)(c, slot_g, slot_b, m)
            P.tag = "t%d:outproj" % tile
            for mp in range(4):
                slot = load_block(tile, bi); bi += 1
                for i in range(2):
                    for c in ctxs:
                        (s_outproj if c is cs_ else outproj)(c, slot, i, mp * 2 + i)
            P.tag = "t%d:epi3" % tile
            for c in ctxs:
                if c is cs_:
                    deferred.append(lambda c=c: epilogue_prenorm(c, 3, False, 4))
                else:
                    epilogue_prenorm(c, 3, False, 4)
            bi = ffn_body(4, bi)
            assert bi == len(PLAN)
            P.tag = "t%d:epi5" % tile
            pend_load = []

            def after5(ch):
                store_y(tile, ch)
                if tile + 1 < NT:
                    pend_load.append(ch)
                    if len(pend_load) > 1:
                        load_x(tile + 1, pend_load.pop(0))

            for c in ctxs:
                epilogue(c, 5, True, after_chunk=after5 if c is cp_ else None)
            while pend_load:
                load_x(tile + 1, pend_load.pop(0))

        P.add("sp", lambda e: e.dma_start(out=o_ys, in_=cs_.x[:, :, :]), reads=[K(cs_, "x", ch) for ch in range(8)], dma="o_ys")
        if debug:
            for name in debug:
                src, keys = DEBUG_SRC[name](locals())
                P.add("sp", lambda e, name=name, src=src: e.dma_start(out=dbg_out[name], in_=src), reads=keys, dma="dbg_" + name)
        P.emit()
    TAGMAP.clear()
    TAGMAP.update(P.tagmap)
    return nc


DEBUG_SRC = {}
TAGMAP = {}
_NC_CACHE = {}


def _prep_inputs(inp):
    inp = {k: np.asarray(v) for k, v in inp.items()}
    ws = _pack_weights(inp)
    par = _pack_params(inp)
    cst = _consts()
    w_in = inp["w_in"][0]
    wlrin = _kc(w_in, np.arange(OFF_LR, OFF_LR + 16))
    rgw = np.ascontiguousarray(np.stack([inp["rg_w_a"][0].transpose(1, 0, 2), inp["rg_w_x"][0].transpose(1, 0, 2)], axis=1)).astype(np.float32)
    wlr = np.ascontiguousarray(inp["gla_w_lr"][0])
    maps = []
    for c in range(NCORES):
        x = inp["x_prompt"][c]
        xT = np.ascontiguousarray(x.reshape(SEQ, 8, 128).transpose(2, 1, 0))
        sl = slice(c * NS, (c + 1) * NS)
        xs = inp["x_sample"][sl, 0, :]
        xsT = np.ascontiguousarray(xs.reshape(NS, 8, 128).transpose(2, 1, 0))
        h0 = np.ascontiguousarray(inp["state_rnn_h"][0, sl].reshape(NS, 8, 128).transpose(2, 1, 0))
        c0 = np.ascontiguousarray(inp["state_rnn_conv"][0, sl].reshape(NS, 3, 8, 128).transpose(3, 2, 0, 1))
        s0 = np.ascontiguousarray(inp["state_gla"][0, sl])
        maps.append({"xT": xT, "xsT": xsT, "h0": h0, "c0": c0, "s0": s0, "ws": ws, "par": par, "cst": cst,
                     "wlrin": wlrin, "rgw": rgw, "wlr": wlr})
    return maps


def _assemble(results):
    yp = np.empty((NCORES, SEQ, D), np.float32)
    ys = np.empty((NCORES * NS, 1, D), np.float32)
    hp = np.empty((1, NCORES, D), np.float32)
    cpo = np.empty((1, NCORES, 3, D), np.float32)
    spo = np.empty((1, NCORES, NH, 128, DV), np.float32)
    hs = np.empty((1, NCORES * NS, D), np.float32)
    cso = np.empty((1, NCORES * NS, 3, D), np.float32)
    sso = np.empty((1, NCORES * NS, NH, 128, DV), np.float32)
    for c, r in enumerate(results):
        sl = slice(c * NS, (c + 1) * NS)
        yp[c] = np.asarray(r["yT"]).transpose(2, 1, 0).reshape(SEQ, D)
        ys[sl, 0] = np.asarray(r["ysT"]).transpose(2, 1, 0).reshape(NS, D)
        hp[0, c] = np.asarray(r["hp"]).T.reshape(D)
        cpo[0, c] = np.asarray(r["cp"]).transpose(2, 1, 0).reshape(3, D)
        spo[0, c] = np.asarray(r["sp"]).transpose(1, 0, 2)
        hs[0, sl] = np.asarray(r["hs"]).transpose(2, 1, 0).reshape(NS, D)
        cso[0, sl] = np.asarray(r["cs"]).transpose(2, 3, 1, 0).reshape(NS, 3, D)
        sso[0, sl] = np.asarray(r["ss"])
    return (yp, ys, hp, cpo, spo, hs, cso, sso)


def kernel(**inputs):
    maps = _prep_inputs(inputs)
    if "nc" not in _NC_CACHE:
        _NC_CACHE["nc"] = build_program()
    nc = _NC_CACHE["nc"]
    res = run_bass_kernel_spmd(nc, maps, core_ids=list(range(NCORES)))
    return _assemble(res.results)
```

```python
import contextlib
import numpy as np
import concourse.bass as bass
import concourse.mybir as mybir
from concourse.bass_utils import run_bass_kernel_spmd

F32 = mybir.dt.float32
BF16 = mybir.dt.bfloat16
AF = mybir.ActivationFunctionType
ALU = mybir.AluOpType

NCORES = 8
D = 1024
SEQ = 2048
TW = 1024
NT = SEQ // TW
NS = 16
DFF = 2816
NJ = DFF // 128
EPS = 1e-6
DK = 128
DV = 256
NH = 4
OFF_XR, OFF_YR, OFF_Q, OFF_K, OFF_V, OFF_OG, OFF_LR, OFF_GA, OFF_GB = 0, 1024, 2048, 2560, 3072, 4096, 5120, 5136, 6160
SLOT = 4096
NSLOT = 3

P_GAIN = 0
P_CW = 48
P_CB = 80
P_BA = 88
P_BX = 96
P_LAM = 104
P_BLR = 112
P_GNG = 116
NPAR = 118
C_ONES1024 = 0
C_ONES256 = 128
C_IDENT = 256
C_MASK = 384
C_RMASK = 512
NCST = 1536

ENGS = ("pe", "act", "dve", "pool", "sp")


class _Op:
    __slots__ = ("eng", "idx", "fn", "deps", "dma", "sig", "sigval", "tag")

    def __init__(self, eng, idx, fn, dma):
        self.eng, self.idx, self.fn, self.dma = eng, idx, fn, dma
        self.tag = ""
        self.deps = {}
        self.sig = False
        self.sigval = None


class Prog:
    def __init__(self, nc):
        self.nc = nc
        self.ops = {e: [] for e in ENGS}
        self.last_write = {}
        self.readers = {}
        self.tag = ""
        self.tagmap = {}

    @staticmethod
    def _flat(keys):
        out = []
        for k in keys:
            if isinstance(k, list):
                out.extend(Prog._flat(k))
            else:
                out.append(k)
        return out

    def add(self, eng, fn, reads=(), writes=(), dma=None):
        op = _Op(eng, len(self.ops[eng]), fn, dma)
        op.tag = self.tag
        reads = self._flat(reads)
        writes = self._flat(writes)
        bk = [k for k in reads if isinstance(k, tuple) and k and k[0] == "bank"]
        if bk:
            reads = [k for k in reads if k not in bk]
            writes = writes + [k for k in bk if k not in writes]

        def dep(o):
            if o is None:
                return
            if o.dma is None and o.eng == "pe" and eng == "pe":
                return
            k = ("d", o.dma) if o.dma is not None else ("e", o.eng)
            cur = op.deps.get(k)
            if cur is None or o.idx > cur.idx:
                op.deps[k] = o

        for k in reads:
            dep(self.last_write.get(k))
        for k in writes:
            dep(self.last_write.get(k))
            for r in self.readers.get(k, ()):
                dep(r)
        for k in reads:
            self.readers.setdefault(k, []).append(op)
        for k in writes:
            self.last_write[k] = op
            self.readers[k] = []
        self.ops[eng].append(op)
        return op

    def emit(self, final_wait_eng="sp"):
        nc = self.nc
        for e in ENGS:
            for op in self.ops[e]:
                for d in op.deps.values():
                    d.sig = True
        for e in ENGS:
            for op in self.ops[e]:
                if op.dma is not None:
                    op.sig = True
        cnt = {}
        for e in ENGS:
            for op in self.ops[e]:
                if not op.sig:
                    continue
                k = ("d", op.dma) if op.dma is not None else ("e", op.eng)
                cnt[k] = cnt.get(k, 0) + (16 if op.dma is not None else 1)
                op.sigval = cnt[k]
        final = dict(cnt)
        with contextlib.ExitStack() as st:
            sems = {}
            for i, k in enumerate(cnt):
                sems[k] = st.enter_context(nc.semaphore("sem%d" % i))
            block = st.enter_context(nc.Block())
            engobj = {"pe": "tensor", "act": "scalar", "dve": "vector", "pool": "gpsimd", "sp": "sync"}

            def mk(e):
                def body(eng):
                    known = {}
                    for op in self.ops[e]:
                        for k, d in op.deps.items():
                            if known.get(k, 0) < d.sigval:
                                eng.wait_ge(sems[k], d.sigval)
                                known[k] = d.sigval
                        ins = op.fn(eng)
                        try:
                            self.tagmap[ins.ins.name] = op.tag
                        except Exception:
                            pass
                        if op.sig:
                            k = ("d", op.dma) if op.dma is not None else ("e", op.eng)
                            ins.then_inc(sems[k], 16 if op.dma is not None else 1)
                    if e == final_wait_eng:
                        for k, v in final.items():
                            if known.get(k, 0) < v:
                                eng.wait_ge(sems[k], v)
                return body

            for e in ENGS:
                if self.ops[e] or e == final_wait_eng:
                    getattr(block, engobj[e])(mk(e))


def _stream_plan():
    plan = []
    for f in (1,):
        pass
    def ffn(tag):
        out = []
        for jb in range(NJ // 2):
            out.append((tag + "gu", jb, 8 * 512))
        for m in range(8):
            out.append((tag + "dn", m, NJ * 128))
        return out
    plan += ffn("f1")
    for n in range(8):
        plan.append(("rnn", n, 8 * 256))
    for hd in range(NH):
        plan.append(("qk", hd, 8 * 256))
        plan.append(("vog", hd, 8 * 512))
    for m in range(8):
        plan.append(("gate", m, 8 * 256))
        plan.append(("br", m, 2 * 8 * 128))
    for mp in range(4):
        plan.append(("wout", mp, 2 * 8 * 128))
    plan += ffn("f2")
    offs = []
    o = 0
    for (_, _, n) in plan:
        offs.append(o)
        o += n
    return plan, offs, o


PLAN, PLAN_OFFS, WLEN = _stream_plan()


def _kc(w, cols):
    return np.ascontiguousarray(w[:, cols].reshape(8, 128, -1).transpose(1, 0, 2))


def _pack_weights(inp):
    wgu = {"f1": inp["ffn1_w_gu"][0], "f2": inp["ffn2_w_gu"][0]}
    wdn = {"f1": inp["ffn1_w_down"][0], "f2": inp["ffn2_w_down"][0]}
    w_in = inp["w_in"][0]
    wbr = inp["w_branch_rnn"][0]
    wbg = inp["w_branch_gla"][0]
    wout = inp["w_out"][0]
    ws = np.empty((128, WLEN), np.float32)
    ar = np.arange
    for (name, i, n), off in zip(PLAN, PLAN_OFFS):
        if name.endswith("gu"):
            w = wgu[name[:2]]
            cols = np.concatenate([ar(i * 256, i * 256 + 256), ar(DFF + i * 256, DFF + i * 256 + 256)])
            blk = _kc(w, cols)
        elif name.endswith("dn"):
            w = wdn[name[:2]]
            blk = w[:, i * 128:(i + 1) * 128].reshape(NJ, 128, 128).transpose(1, 0, 2)
        elif name == "rnn":
            cols = np.concatenate([ar(OFF_XR + i * 128, OFF_XR + i * 128 + 128), ar(OFF_YR + i * 128, OFF_YR + i * 128 + 128)])
            blk = _kc(w_in, cols)
        elif name == "qk":
            cols = np.concatenate([ar(OFF_Q + i * 128, OFF_Q + i * 128 + 128), ar(OFF_K + i * 128, OFF_K + i * 128 + 128)])
            blk = _kc(w_in, cols)
        elif name == "vog":
            cols = np.concatenate([ar(OFF_V + i * 256, OFF_V + i * 256 + 256), ar(OFF_OG + i * 256, OFF_OG + i * 256 + 256)])
            blk = _kc(w_in, cols)
        elif name == "gate":
            cols = np.concatenate([ar(OFF_GA + i * 128, OFF_GA + i * 128 + 128), ar(OFF_GB + i * 128, OFF_GB + i * 128 + 128)])
            blk = _kc(w_in, cols)
        elif name == "br":
            a = _kc(wbr, ar(i * 128, i * 128 + 128))
            b = _kc(wbg, ar(i * 128, i * 128 + 128))
            blk = np.stack([a, b], axis=1)
        elif name == "wout":
            a = _kc(wout, ar((2 * i) * 128, (2 * i) * 128 + 128))
            b = _kc(wout, ar((2 * i + 1) * 128, (2 * i + 1) * 128 + 128))
            blk = np.stack([a, b], axis=1)
        else:
            raise AssertionError(name)
        ws[:, off:off + n] = np.asarray(blk).reshape(128, n)
    return ws


def _fm(v):
    return np.ascontiguousarray(np.asarray(v).reshape(8, 128).T)


def _pack_params(inp):
    p = np.zeros((128, NPAR), np.float32)
    g = inp["norm_gains"][0]
    for i in range(6):
        p[:, P_GAIN + i * 8:P_GAIN + i * 8 + 8] = _fm(g[i])
    cw = inp["conv_w"][0]
    for j in range(4):
        p[:, P_CW + j * 8:P_CW + j * 8 + 8] = _fm(cw[j])
    p[:, P_CB:P_CB + 8] = _fm(inp["conv_b"][0])
    p[:, P_BA:P_BA + 8] = _fm(inp["rg_b_a"][0])
    p[:, P_BX:P_BX + 8] = _fm(inp["rg_b_x"][0])
    p[:, P_LAM:P_LAM + 8] = _fm(inp["rg_lambda"][0])
    p[:, P_BLR:P_BLR + 4] = np.asarray(inp["gla_b_lr"][0]).reshape(4, 128).T
    p[:, P_GNG:P_GNG + 2] = np.asarray(inp["gla_norm_g"][0]).reshape(2, 128).T
    return p


def _consts():
    c = np.zeros((128, NCST), np.float32)
    c[:, C_ONES1024:C_ONES1024 + 128] = 1.0 / 1024.0
    c[:, C_ONES256:C_ONES256 + 128] = 1.0 / 256.0
    c[:, C_IDENT:C_IDENT + 128] = np.eye(128, dtype=np.float32)
    s = np.arange(128)[:, None]
    cc = np.arange(128)[None, :]
    c[:, C_MASK:C_MASK + 128] = (s <= cc).astype(np.float32)
    rm = np.ones((1024,), np.float32)
    rm[::128] = 0.0
    c[:, C_RMASK:C_RMASK + 1024] = rm[None, :]
    return c


class Ctx:
    pass


def build_program(debug=None):
    nc = bass.Bass("TRN2", target_bir_lowering=False)
    di = lambda name, shape: nc.dram_tensor(name, shape, F32, kind="ExternalInput").ap()
    do = lambda name, shape: nc.dram_tensor(name, shape, F32, kind="ExternalOutput").ap()
    d_x = di("xT", [128, 8, SEQ])
    d_xs = di("xsT", [128, 8, NS])
    d_h0 = di("h0", [128, 8, NS])
    d_c0 = di("c0", [128, 8, NS, 3])
    d_s0 = di("s0", [NS, NH, 128, DV])
    d_ws = di("ws", [128, WLEN])
    d_par = di("par", [128, NPAR])
    d_cst = di("cst", [128, NCST])
    d_wlrin = di("wlrin", [128, 8, 16])
    d_rgw = di("rgw", [128, 2, 8, 128])
    d_wlr = di("wlr", [16, 512])
    o_y = do("yT", [128, 8, SEQ])
    o_ys = do("ysT", [128, 8, NS])
    o_hp = do("hp", [128, 8])
    o_cp = do("cp", [128, 8, 3])
    o_sp = do("sp", [128, NH, DV])
    o_hs = do("hs", [128, 8, NS])
    o_cs = do("cs", [128, 8, NS, 3])
    o_ss = do("ss", [NS, NH, 128, DV])
    dbg_out = {}
    if debug:
        for name, shape in debug.items():
            dbg_out[name] = do("dbg_" + name, shape)

    with contextlib.ExitStack() as st:
        def sb(name, shape, dt=F32):
            return st.enter_context(nc.sbuf_tensor("sb_" + name, shape, dt))

        P = Prog(nc)
        par = sb("par", [128, NPAR])
        cst = sb("cst", [128, NCST - 256])
        cbf = sb("cbf", [128, 512], BF16)
        wlrin = sb("wlrin", [128, 8, 16], BF16)
        rgw = sb("rgw", [128, 2, 8, 128], BF16)
        wlr = sb("wlr", [16, 512], BF16)
        der = sb("der", [128, 64])
        DC, DC2, DNB, DEPS, DEPS4, DHC, DHBA, DHBX, DQ25 = 0, 8, 16, 20, 21, 24, 32, 40, 48
        wring = [sb("wring%d" % i, [128, SLOT], BF16) for i in range(NSLOT)]
        ps = st.enter_context(nc.psum_tensor("ps", [128, 4096], F32))

        P.add("sp", lambda e: e.dma_start(out=par[:], in_=d_par), writes=["par"], dma="par")
        P.add("sp", lambda e: e.dma_start(out=cst[:], in_=d_cst[:, 256:NCST]), writes=["cst"], dma="cst")
        P.add("pool", lambda e: e.dma_start(out=cbf[:], in_=d_cst[:, 0:512]), writes=["cbf"], dma="cbf")
        P.add("pool", lambda e: e.dma_start(out=wlrin[:], in_=d_wlrin), writes=["wlrin"], dma="wlrin")
        P.add("pool", lambda e: e.dma_start(out=rgw[:], in_=d_rgw), writes=["rgw"], dma="rgw")
        P.add("pool", lambda e: e.dma_start(out=wlr[:], in_=d_wlr), writes=["wlr"], dma="wlr")
        ones1024 = cbf[:, 0:128]
        ones256 = cbf[:, 128:256]
        identb = cbf[:, 256:384]
        P.add("act", lambda e: e.activation(out=der[:, DC:DC + 8], in_=par[:, P_LAM:P_LAM + 8], func=AF.Exp, scale=-1.0),
              reads=["par"], writes=["der"])
        P.add("act", lambda e: e.activation(out=der[:, DC:DC + 8], in_=der[:, DC:DC + 8], func=AF.Ln, bias=1.0),
              reads=["der"], writes=["der"])
        P.add("dve", lambda e: e.tensor_scalar(out=der[:, DC2:DC2 + 8], in0=der[:, DC:DC + 8], scalar1=-16.0, scalar2=None, op0=ALU.mult),
              reads=["der"], writes=["der"])
        P.add("dve", lambda e: e.tensor_scalar(out=der[:, DC:DC + 8], in0=der[:, DC:DC + 8], scalar1=-8.0, scalar2=None, op0=ALU.mult),
              reads=["der"], writes=["der"])
        P.add("dve", lambda e: e.tensor_scalar(out=der[:, DNB:DNB + 4], in0=par[:, P_BLR:P_BLR + 4], scalar1=-1.0, scalar2=None, op0=ALU.mult),
              reads=["par", "der"], writes=["der"])
        P.add("dve", lambda e: e.tensor_scalar(out=der[:, DHC:DHC + 8], in0=der[:, DC:DC + 8], scalar1=0.5, scalar2=None, op0=ALU.mult),
              reads=["der"], writes=["der"])
        P.add("dve", lambda e: e.tensor_scalar(out=der[:, DHBA:DHBA + 8], in0=par[:, P_BA:P_BA + 8], scalar1=0.5, scalar2=None, op0=ALU.mult),
              reads=["par", "der"], writes=["der"])
        P.add("dve", lambda e: e.tensor_scalar(out=der[:, DHBX:DHBX + 8], in0=par[:, P_BX:P_BX + 8], scalar1=0.5, scalar2=None, op0=ALU.mult),
              reads=["par", "der"], writes=["der"])
        P.add("dve", lambda e: e.memset(der[:, DQ25:DQ25 + 1], 0.25), reads=["der"], writes=["der"])
        P.add("dve", lambda e: e.memset(der[:, DEPS:DEPS + 1], EPS), reads=["der"], writes=["der"])
        P.add("dve", lambda e: e.memset(der[:, DEPS4:DEPS4 + 1], 4.0 * EPS), reads=["der"], writes=["der"])

        def make_ctx(name, W, psA, psB, psC, psM, keyM, banks, psS=None):
            c = Ctx()
            c.name, c.W = name, W
            c.groups = [(g, min(g + 512, W)) for g in range(0, W, 512)]
            c.x = sb(name + "_x", [128, 8, W])
            c.xn = sb(name + "_xn", [128, 8, W], BF16)
            c.hreg = sb(name + "_h", [128, 24, W], BF16)
            c.y = sb(name + "_y", [128, 8, W])
            c.sq = sb(name + "_sq", [128, 2, W], BF16)
            c.rstd = sb(name + "_rstd", [128, W])
            c.ps = {"A": psA, "B": psB}
            if psC is not None:
                c.ps["C"] = psC
            c.banks = banks
            c.pend = None
            c.statset = "C"
            if psS is not None:
                c.ps["S"] = psS
                c.statset = "S"
            c.psM = psM
            c.keyM = keyM
            c.rr = 0
            c.sqi = 0
            return c

        BK = lambda *i: [("bank", j) for j in i]
        cp_ = make_ctx("p", TW, ps[:, 0:1024], ps[:, 1024:2048], ps[:, 2048:3072], ps[:, 2560:3072], BK(5),
                       {"A": BK(0, 1), "B": BK(2, 3), "C": BK(4, 5)})
        cp_.order = ["A", "B", "C"]
        b6, b7 = 3072, 3584
        cs_ = make_ctx("s", NS, ps[:, b6:b6 + 16], ps[:, b7:b7 + 16], None, ps[:, b6 + 16:b6 + 32], BK(6),
                       {"A": BK(6), "B": BK(7), "S": BK(6)}, psS=ps[:, b6 + 16:b6 + 32])
        cs_.order = ["A", "B"]
        cs_.sqall = sb("s_sqall", [128, 8, NS], BF16)

        def K(c, what, i=None):
            if what == "y" and i in (4, 5, 6):
                return [(c.name, "y", i, 0), (c.name, "y", i, 1)]
            return (c.name, what, i)

        def pskey(c, s):
            return c.banks[s]

        def nextset(c, avoid=()):
            order = c.order
            for _ in range(len(order) + 1):
                s = order[c.rr % len(order)]
                c.rr += 1
                if s not in avoid:
                    return s
            raise AssertionError

        def T(c, i):
            return c.y[:, i, :]

        halo_bf = sb("halo_bf", [128, 8, 3], BF16)
        halo = sb("halo", [128, 8, 3])
        hcar = sb("hcar", [128, 8])
        S_f = sb("S_f", [128, NH, DV])
        S_b = sb("S_b", [128, 2, DV], BF16)
        S_t = sb("S_t", [128, DV])
        lr_bf = sb("lr_bf", [16, TW], BF16)
        s_h0 = sb("s_h0", [128, 8, NS])
        s_c0 = sb("s_c0", [128, 8, NS, 3])
        s_cn = sb("s_cn", [128, 8, NS, 3])
        s_hn = sb("s_hn", [128, 8, NS])
        s_xr = sb("s_xr", [128, 8, NS])
        s_t = sb("s_t", [128, 12, NS])
        s_xcb3 = sb("s_xcb3", [128, 8, NS], BF16)
        s_u = sb("s_u", [128, 6, 8, NS])
        s_lr = sb("s_lr", [16, NS], BF16)
        s_ktm = sb("s_ktm", [16, 128], BF16)
        s_vtm = sb("s_vtm", [16, DV], BF16)
        s_km = sb("s_km", [16, 1, 4, 128], BF16)
        s_Sb = sb("s_Sb", [128, 1, 4, DV], BF16)
        s_S = sb("s_S", [128, 3, 4, DV])
        s_q = sb("s_q", [128, NS])
        s_qgb = sb("s_qgb", [128, NS], BF16)
        s_pb = sb("s_pb", [128, 2, NS], BF16)
        s_eg = sb("s_eg", [128, NS])
        s_vq = sb("s_vq", [128, 2, NS])
        s_o = sb("s_o", [128, 2, NS])

        wstate = {"i": 0}

        def load_block(tile, bi):
            name, idx, n = PLAN[bi]
            gi = wstate["i"]
            wstate["i"] += 1
            slot = gi % NSLOT
            off = PLAN_OFFS[bi]
            P.add("pool", lambda e, slot=slot, off=off, n=n: e.dma_start(out=wring[slot][:, 0:n], in_=d_ws[:, off:off + n]),
                  writes=[("w", slot)], dma=("w", slot))
            return slot

        def wv3(slot, n, inner):
            return wring[slot][:, 0:n].rearrange("p (k c) -> p k c", c=inner)

        def wv4(slot):
            return wring[slot][:, 0:2048].rearrange("p (a k c) -> p a k c", a=2, k=8)

        def mm(out, lhsT, rhs, start, stop, reads, writes):
            P.add("pe", lambda e: e.matmul(out, lhsT=lhsT, rhs=rhs, start=start, stop=stop), reads=reads, writes=writes)

        def act(out, in_, func, reads, writes, bias=None, scale=None):
            kw = {}
            if bias is not None:
                kw["bias"] = bias
            if scale is not None:
                kw["scale"] = scale
            P.add("act", lambda e: e.activation(out=out, in_=in_, func=func, **kw), reads=reads, writes=writes)

        def tt(out, in0, in1, op, reads, writes, eng="dve"):
            P.add(eng, lambda e: e.tensor_tensor(out=out, in0=in0, in1=in1, op=op), reads=reads, writes=writes)

        def stt(out, in0, scalar, in1, op0, op1, reads, writes, eng="dve"):
            P.add(eng, lambda e: e.scalar_tensor_tensor(out=out, in0=in0, scalar=scalar, in1=in1, op0=op0, op1=op1), reads=reads, writes=writes)

        def ts(out, in0, s1, s2, op0, op1, reads, writes, eng="dve"):
            if s2 is None:
                P.add(eng, lambda e: e.tensor_scalar(out=out, in0=in0, scalar1=s1, scalar2=None, op0=op0), reads=reads, writes=writes)
            else:
                P.add(eng, lambda e: e.tensor_scalar(out=out, in0=in0, scalar1=s1, scalar2=s2, op0=op0, op1=op1), reads=reads, writes=writes)

        def cp(out, in_, reads, writes, eng="dve"):
            P.add(eng, lambda e: e.tensor_copy(out=out, in_=in_), reads=reads, writes=writes)

        def proj(c, s, lhsT_of_kc, rhs_chunks, nk, reads, M=128):
            for kc in range(nk):
                for (g0, g1) in c.groups:
                    mm(c.ps[s][0:M, g0:g1], lhsT_of_kc(kc), rhs_chunks(kc)[:, g0:g1], kc == 0, kc == nk - 1,
                       reads=reads(kc), writes=[pskey(c, s)])

        def stat_acc(c, src, src_reads, first, last, ones, statset=None):
            statset = statset or c.statset
            i = c.sqi % 2
            c.sqi += 1
            act(c.sq[:, i, :], src, AF.Square, reads=src_reads, writes=[K(c, "sq", i)])
            for (g0, g1) in c.groups:
                mm(c.ps[statset][:, g0:g1], ones, c.sq[:, i, g0:g1], first, last,
                   reads=[K(c, "sq", i), "cbf"], writes=[pskey(c, statset)])

        def rstd_from_stat(c, half, statset=None):
            statset = statset or c.statset
            flush_stat(c)
            if half:
                act(c.rstd[:, :], c.ps[statset][:, 0:c.W], AF.Ln, reads=[pskey(c, statset), "der"], writes=[K(c, "rstd")],
                    bias=der[:, DEPS4:DEPS4 + 1], scale=4.0)
            else:
                act(c.rstd[:, :], c.ps[statset][:, 0:c.W], AF.Ln, reads=[pskey(c, statset), "der"], writes=[K(c, "rstd")],
                    bias=der[:, DEPS:DEPS + 1], scale=1.0)
            act(c.rstd[:, :], c.rstd[:, :], AF.Exp, reads=[K(c, "rstd")], writes=[K(c, "rstd")], scale=-0.5)

        def flush_stat(c):
            if c.pend is not None:
                i, first, last = c.pend
                c.pend = None
                for (g0, g1) in c.groups:
                    mm(c.ps[c.statset][:, g0:g1], ones1024, c.sq[:, i, g0:g1], first, last,
                       reads=[K(c, "sq", i), "cbf"], writes=[pskey(c, c.statset)])

        def stat_all(c, src3, src_keys):
            act(c.sqall[:, :, :], src3, AF.Square, reads=src_keys, writes=[K(c, "sqall")])
            for ch in range(8):
                mm(c.ps["S"][:, 0:c.W], ones1024, c.sqall[:, ch, :], ch == 0, ch == 7, reads=[K(c, "sqall"), "cbf"], writes=[pskey(c, "S")])

        def prenorm_finish(c, gi):
            rstd_from_stat(c, False)
            for ch in range(8):
                stt(c.xn[:, ch, :], c.x[:, ch, :], par[:, P_GAIN + gi * 8 + ch:P_GAIN + gi * 8 + ch + 1], c.rstd[:, :], ALU.mult, ALU.mult,
                    reads=[K(c, "x", ch), K(c, "rstd"), "par"], writes=[K(c, "xn", ch)])

        def xn_pre(c, gi, ch):
            act(c.xn[:, ch, :], c.x[:, ch, :], AF.Identity, reads=[K(c, "x", ch), "par"], writes=[K(c, "xn", ch)],
                scale=par[:, P_GAIN + gi * 8 + ch:P_GAIN + gi * 8 + ch + 1])

        def prenorm(c, gi):
            if c is cs_:
                s_prenorm(c, gi)
                return
            for ch in range(8):
                stat_acc(c, c.x[:, ch, :], [K(c, "x", ch)], ch == 0, ch == 7, ones1024)
                if gi in (0, 4):
                    xn_pre(c, gi, ch)
            if gi in (0, 4):
                rstd_from_stat(c, False)
            else:
                prenorm_finish(c, gi)

        def epilogue(c, gi, half, after_chunk=None):
            if c is cs_:
                s_epilogue(c, gi, half)
                return
            rstd_from_stat(c, half)
            for ch in range(8):
                stt(c.y[:, ch, :], c.y[:, ch, :], par[:, P_GAIN + gi * 8 + ch:P_GAIN + gi * 8 + ch + 1], c.rstd[:, :], ALU.mult, ALU.mult,
                    reads=[K(c, "y", ch), K(c, "rstd"), "par"], writes=[K(c, "y", ch)])
                tt(c.x[:, ch, :], c.x[:, ch, :], c.y[:, ch, :], ALU.add, reads=[K(c, "x", ch), K(c, "y", ch)], writes=[K(c, "x", ch)])
                if after_chunk is not None:
                    after_chunk(ch)

        def epilogue_prenorm(c, gi_post, half, gi_pre):
            if c is cs_:
                epilogue(c, gi_post, half)
                prenorm(c, gi_pre)
                return
            if gi_pre in (0, 4):
                def after(ch):
                    stat_acc(c, c.x[:, ch, :], [K(c, "x", ch)], ch == 0, ch == 7, ones1024)
                    xn_pre(c, gi_pre, ch)
                epilogue(c, gi_post, half, after_chunk=after)
                rstd_from_stat(c, False)
                return
            epilogue(c, gi_post, half, after_chunk=lambda ch: stat_acc(c, c.x[:, ch, :], [K(c, "x", ch)], ch == 0, ch == 7, ones1024))
            prenorm_finish(c, gi_pre)

        def out_chunk(c, s, m, first, last):
            act(c.y[:, m, :], c.ps[s][:, 0:c.W], AF.Copy, reads=[pskey(c, s)], writes=[K(c, "y", m)])
            if c is cs_:
                return
            i = c.sqi % 2
            c.sqi += 1
            act(c.sq[:, i, :], c.ps[s][:, 0:c.W], AF.Square, reads=[pskey(c, s)], writes=[K(c, "sq", i)])
            c.pend = (i, first, last)

        def ffn_up(c, slot, jb):
            w = wv3(slot, 8 * 512, 512)
            for jj in range(2):
                j = jb * 2 + jj
                sg_, su_ = nextset(c), nextset(c)
                proj(c, sg_, lambda kc: w[:, kc, jj * 128:(jj + 1) * 128], lambda kc: c.xn[:, kc, :], 8,
                     lambda kc: [("w", slot), K(c, "xn", kc)])
                proj(c, su_, lambda kc: w[:, kc, 256 + jj * 128:256 + (jj + 1) * 128], lambda kc: c.xn[:, kc, :], 8,
                     lambda kc: [("w", slot), K(c, "xn", kc)])
                ta, tb = (4, 5) if j % 2 == 0 else (6, 7)
                rk = K(c, "rstd")
                tt(T(c, ta), c.ps[sg_][:, 0:c.W], c.rstd[:, :], ALU.mult, reads=[pskey(c, sg_), rk], writes=[K(c, "y", ta)])
                act(T(c, ta), T(c, ta), AF.Silu, reads=[K(c, "y", ta)], writes=[K(c, "y", ta)])
                tt(T(c, tb), c.ps[su_][:, 0:c.W], c.rstd[:, :], ALU.mult, reads=[pskey(c, su_), rk], writes=[K(c, "y", tb)])
                tt(c.hreg[:, j, :], T(c, ta), T(c, tb), ALU.mult, reads=[K(c, "y", ta), K(c, "y", tb)], writes=[K(c, "h", j)])

        def ffn_down(c, slot, m):
            w = wv3(slot, NJ * 128, 128)
            s = nextset(c, avoid=("C",))
            proj(c, s, lambda j: w[:, j, :], lambda j: c.hreg[:, j, :], NJ, lambda j: [("w", slot), K(c, "h", j)])
            flush_stat(c)
            out_chunk(c, s, m, m == 0, m == 7)

        def merge_m(c, slot_g, slot_b, m):
            wg = wv3(slot_g, 8 * 256, 256)
            wb = wv4(slot_b)
            sA = nextset(c)
            proj(c, sA, lambda kc: wg[:, kc, 0:128], lambda kc: c.xn[:, kc, :], 8, lambda kc: [("w", slot_g), K(c, "xn", kc)])
            act(T(c, 0), c.ps[sA][:, 0:c.W], AF.Sigmoid, reads=[pskey(c, sA)], writes=[K(c, "y", 0)])
            sY = nextset(c)
            proj(c, sY, lambda n: wb[:, 0, n, :], lambda n: c.hreg[:, n, :], 8, lambda n: [("w", slot_b), K(c, "h", n)])
            tt(T(c, 0), T(c, 0), c.ps[sY][:, 0:c.W], ALU.mult, reads=[K(c, "y", 0), pskey(c, sY)], writes=[K(c, "y", 0)])
            sB = nextset(c)
            proj(c, sB, lambda kc: wg[:, kc, 128:256], lambda kc: c.xn[:, kc, :], 8, lambda kc: [("w", slot_g), K(c, "xn", kc)])
            act(T(c, 1), c.ps[sB][:, 0:c.W], AF.Sigmoid, reads=[pskey(c, sB)], writes=[K(c, "y", 1)])
            sY2 = nextset(c)
            proj(c, sY2, lambda n: wb[:, 1, n, :], lambda n: c.hreg[:, 8 + n, :], 8, lambda n: [("w", slot_b), K(c, "h", 8 + n)])
            tt(T(c, 1), T(c, 1), c.ps[sY2][:, 0:c.W], ALU.mult, reads=[K(c, "y", 1), pskey(c, sY2)], writes=[K(c, "y", 1)])
            tt(c.hreg[:, 16 + m, :], T(c, 0), T(c, 1), ALU.add, reads=[K(c, "y", 0), K(c, "y", 1)], writes=[K(c, "h", 16 + m)])

        def outproj(c, slot, i, m2):
            w = wv4(slot)
            s = nextset(c, avoid=("C",))
            proj(c, s, lambda m: w[:, i, m, :], lambda m: c.hreg[:, 16 + m, :], 8, lambda m: [("w", slot), K(c, "h", 16 + m)])
            flush_stat(c)
            out_chunk(c, s, m2, m2 == 0, m2 == 7)

        s_flat = lambda: s_u[:, :, :, :].rearrange("p a n t -> p (a n t)")
        g3 = lambda gi: par[:, P_GAIN + gi * 8:P_GAIN + gi * 8 + 8].rearrange("p (n o) -> p n o", o=1).to_broadcast([128, 8, NS])
        r3 = lambda c: c.rstd[:, :].rearrange("p (o t) -> p o t", o=1).to_broadcast([128, 8, NS])

        def s_prenorm(c, gi):
            stat_all(c, c.x[:, :, :], [K(c, "x", ch) for ch in range(8)])
            rstd_from_stat(c, False)
            tt(s_u[:, 5, :, :], c.x[:, :, :], g3(gi), ALU.mult, reads=[K(c, "x", ch) for ch in range(8)] + ["par"], writes=[("s_u", 5)])
            tt(c.xn[:, :, :], s_u[:, 5, :, :], r3(c), ALU.mult, reads=[("s_u", 5), K(c, "rstd")], writes=[K(c, "xn", ch) for ch in range(8)])

        def s_epilogue(c, gi, half):
            yk_ = [K(c, "y", ch) for ch in range(8)]
            stat_all(c, c.y[:, :, :], yk_)
            rstd_from_stat(c, half)
            tt(c.y[:, :, :], c.y[:, :, :], g3(gi), ALU.mult, reads=yk_ + ["par"], writes=yk_)
            tt(c.y[:, :, :], c.y[:, :, :], r3(c), ALU.mult, reads=yk_ + [K(c, "rstd")], writes=yk_)
            tt(c.x[:, :, :], c.x[:, :, :], c.y[:, :, :], ALU.add, reads=yk_ + [K(c, "x", ch) for ch in range(8)], writes=[K(c, "x", ch) for ch in range(8)])

        def s_ffn_up(c, slot, jb):
            w = wv3(slot, 8 * 512, 512)
            for jj in range(2):
                j = jb * 2 + jj
                for kc in range(8):
                    mm(ps[:, b6 + j * NS:b6 + (j + 1) * NS], w[:, kc, jj * 128:(jj + 1) * 128], c.xn[:, kc, :], kc == 0, kc == 7,
                       reads=[("w", slot), K(c, "xn", kc)], writes=BK(6))
                for kc in range(8):
                    mm(ps[:, b7 + j * NS:b7 + (j + 1) * NS], w[:, kc, 256 + jj * 128:256 + (jj + 1) * 128], c.xn[:, kc, :], kc == 0, kc == 7,
                       reads=[("w", slot), K(c, "xn", kc)], writes=BK(7))
            if jb == NJ // 2 - 1:
                sg = s_flat()[:, 0:NJ * NS]
                allu = [("s_u", i) for i in range(6)]
                act(sg, ps[:, b6:b6 + NJ * NS], AF.Silu, reads=BK(6), writes=allu)
                tt(c.hreg[:, 0:NJ, :].rearrange("p j t -> p (j t)"), sg, ps[:, b7:b7 + NJ * NS], ALU.mult, reads=allu + BK(7),
                   writes=[K(c, "h", j) for j in range(NJ)])

        def s_outchunks(c, slot_reads, lhs_of, rhs_of, nk, m, last):
            for k in range(nk):
                mm(ps[:, b6 + m * NS:b6 + (m + 1) * NS], lhs_of(k), rhs_of(k), k == 0, k == nk - 1, reads=slot_reads(k), writes=BK(6))
            if last:
                act(c.y[:, :, :], ps[:, b6:b6 + 8 * NS].rearrange("p (m t) -> p m t", t=NS), AF.Copy, reads=BK(6), writes=[K(c, "y", ch) for ch in range(8)])

        def s_ffn_down(c, slot, m):
            w = wv3(slot, NJ * 128, 128)
            s_outchunks(c, lambda j: [("w", slot), K(c, "h", j)], lambda j: w[:, j, :], lambda j: c.hreg[:, j, :], NJ, m, m == 7)

        def s_outproj(c, slot, i, m2):
            w = wv4(slot)
            s_outchunks(c, lambda m: [("w", slot), K(c, "h", 16 + m)], lambda m: w[:, i, m, :], lambda m: c.hreg[:, 16 + m, :], 8, m2, m2 == 7)

        def s_merge_m(c, slot_g, slot_b, m):
            wg = wv3(slot_g, 8 * 256, 256)
            wb = wv4(slot_b)
            cs0, cs1 = m * NS, (m + 1) * NS
            for kc in range(8):
                mm(ps[:, b6 + cs0:b6 + cs1], wg[:, kc, 0:128], c.xn[:, kc, :], kc == 0, kc == 7, reads=[("w", slot_g), K(c, "xn", kc)], writes=BK(6))
            for n in range(8):
                mm(ps[:, b6 + 128 + cs0:b6 + 128 + cs1], wb[:, 0, n, :], c.hreg[:, n, :], n == 0, n == 7, reads=[("w", slot_b), K(c, "h", n)], writes=BK(6))
            for kc in range(8):
                mm(ps[:, b7 + cs0:b7 + cs1], wg[:, kc, 128:256], c.xn[:, kc, :], kc == 0, kc == 7, reads=[("w", slot_g), K(c, "xn", kc)], writes=BK(7))
            for n in range(8):
                mm(ps[:, b7 + 128 + cs0:b7 + 128 + cs1], wb[:, 1, n, :], c.hreg[:, 8 + n, :], n == 0, n == 7, reads=[("w", slot_b), K(c, "h", 8 + n)], writes=BK(7))
            if m == 7:
                f = s_flat()
                allu = [("s_u", i) for i in range(6)]
                act(f[:, 0:128], ps[:, b6:b6 + 128], AF.Sigmoid, reads=BK(6), writes=allu)
                act(f[:, 128:256], ps[:, b7:b7 + 128], AF.Sigmoid, reads=BK(7), writes=allu)
                tt(f[:, 0:128], f[:, 0:128], ps[:, b6 + 128:b6 + 256], ALU.mult, reads=allu + BK(6), writes=allu)
                tt(f[:, 128:256], f[:, 128:256], ps[:, b7 + 128:b7 + 256], ALU.mult, reads=allu + BK(7), writes=allu)
                tt(c.hreg[:, 16:24, :].rearrange("p m t -> p (m t)"), f[:, 0:128], f[:, 128:256], ALU.add, reads=allu,
                   writes=[K(c, "h", 16 + mm_) for mm_ in range(8)])

        pcol = lambda base, i: par[:, base + i:base + i + 1]

        def Hs(c, k):
            ap = c.hreg[:, 16 + 2 * k:18 + 2 * k, :].rearrange("p a w -> p (a w)").bitcast(F32)
            return ap, [K(c, "h", 16 + 2 * k), K(c, "h", 17 + 2 * k)]

        def rnn_slots(c, p):
            if p == 0:
                d = {"gl": (T(c, 0), K(c, "y", 0)), "r": (T(c, 1), K(c, "y", 1)), "i": (T(c, 2), K(c, "y", 2)),
                     "a": (T(c, 3), K(c, "y", 3)), "m": (T(c, 7), K(c, "y", 7)), "xc": Hs(c, 2)}
            else:
                d = {"gl": (T(c, 4), K(c, "y", 4)), "r": (T(c, 5), K(c, "y", 5)), "i": (T(c, 6), K(c, "y", 6)),
                     "a": Hs(c, 0), "m": Hs(c, 1), "xc": Hs(c, 3)}
            return d

        def rnn_parts(c, slot, n, tile):
            p = n % 2
            R = rnn_slots(c, p)
            xcb_p = c.hreg[:, 8 + p, :]
            xcbk = K(c, "h", 8 + p)
            xrb = c.hreg[:, 10 + 2 * p:12 + 2 * p, :].rearrange("p a w -> p (a w)")
            xk = [K(c, "h", 10 + 2 * p), K(c, "h", 11 + 2 * p)]
            dgp = c.hreg[:, 14 + p, :]
            dgk = K(c, "h", 14 + p)
            st = {}

            def A1():
                w = wv3(slot, 8 * 256, 256)
                sx = nextset(c)
                proj(c, sx, lambda kc: w[:, kc, 0:128], lambda kc: c.xn[:, kc, :], 8, lambda kc: [("w", slot), K(c, "xn", kc)])
                sy = nextset(c)
                proj(c, sy, lambda kc: w[:, kc, 128:256], lambda kc: c.xn[:, kc, :], 8, lambda kc: [("w", slot), K(c, "xn", kc)])
                cp(xrb[:, 0:3], halo_bf[:, n, :], reads=["halo_bf"], writes=[xk])
                cp(xrb[:, 3:TW + 3], c.ps[sx][:, 0:TW], reads=[pskey(c, sx)], writes=[xk])
                cp(halo[:, n, :], c.ps[sx][:, TW - 3:TW], reads=[pskey(c, sx)], writes=["halo"])
                cp(halo_bf[:, n, :], xrb[:, TW:TW + 3], reads=[xk], writes=["halo_bf"])
                idb = cst[:, C_IDENT - 256:C_IDENT - 256 + 128].rearrange("p (o c) -> p o c", o=1).to_broadcast([128, 4, 128])
                cwb = par[:, P_CW + n:P_CW + n + 25:8].rearrange("p (j o) -> p j o", o=1).to_broadcast([128, 4, 128])
                tt(dgp[:, 0:512].rearrange("p (j c) -> p j c", c=128), idb, cwb, ALU.mult, reads=["cst", "par"], writes=[dgk])
                act(R["gl"][0], c.ps[sy][:, 0:TW], AF.Gelu_apprx_tanh, reads=[pskey(c, sy)], writes=[R["gl"][1]])
                sc = nextset(c)
                for j in range(4):
                    for (g0, g1) in c.groups:
                        mm(c.ps[sc][:, g0:g1], dgp[:, j * 128:(j + 1) * 128], xrb[:, j + g0:j + g1], j == 0, j == 3, reads=[dgk, xk], writes=[pskey(c, sc)])
                st["sc"] = sc

            def A2():
                sc = st["sc"]
                act(xcb_p, c.ps[sc][:, 0:TW], AF.Identity, reads=[pskey(c, sc), "par"], writes=[xcbk], bias=pcol(P_CB, n))
                ts(R["xc"][0], c.ps[sc][:, 0:TW], pcol(P_CB, n), None, ALU.add, None, reads=[pskey(c, sc), "par"], writes=[R["xc"][1]])
                s1 = nextset(c)
                for (g0, g1) in c.groups:
                    mm(c.ps[s1][:, g0:g1], rgw[:, 0, n, :], xcb_p[:, g0:g1], True, True, reads=["rgw", xcbk], writes=[pskey(c, s1)])
                s2 = nextset(c)
                for (g0, g1) in c.groups:
                    mm(c.ps[s2][:, g0:g1], rgw[:, 1, n, :], xcb_p[:, g0:g1], True, True, reads=["rgw", xcbk], writes=[pskey(c, s2)])
                act(R["r"][0], c.ps[s1][:, 0:TW], AF.Tanh, reads=[pskey(c, s1), "der"], writes=[R["r"][1]], scale=0.5, bias=der[:, DHBA + n:DHBA + n + 1])
                act(R["i"][0], c.ps[s2][:, 0:TW], AF.Tanh, reads=[pskey(c, s2), "der"], writes=[R["i"][1]], scale=0.5, bias=der[:, DHBX + n:DHBX + n + 1])

            def B_act():
                act(R["a"][0], R["r"][0], AF.Exp, reads=[R["r"][1], "der"], writes=[R["a"][1]], scale=der[:, DHC + n:DHC + n + 1], bias=der[:, DHC + n:DHC + n + 1])
                act(R["m"][0], R["a"][0], AF.Square, reads=[R["a"][1]], writes=[R["m"][1]])
                act(R["m"][0], R["m"][0], AF.Sqrt, reads=[R["m"][1], "der"], writes=[R["m"][1]], scale=-0.25, bias=der[:, DQ25:DQ25 + 1])

            def B_dve():
                if tile == 0:
                    P.add("dve", lambda e: e.memset(R["m"][0][:, 0:1], 0.5), reads=[R["m"][1]], writes=[R["m"][1]])
                stt(R["i"][0], R["i"][0], 1.0, R["m"][0], ALU.add, ALU.mult, reads=[R["i"][1], R["m"][1]], writes=[R["i"][1]])
                tt(R["i"][0], R["i"][0], R["xc"][0], ALU.mult, reads=[R["i"][1], R["xc"][1]], writes=[R["i"][1]])
                P.add("dve", lambda e: e.tensor_tensor_scan(out=R["r"][0], data0=R["a"][0], data1=R["i"][0], initial=hcar[:, n:n + 1], op0=ALU.mult, op1=ALU.add),
                      reads=[R["a"][1], R["i"][1], R["r"][1], "hcar"], writes=[R["r"][1]])
                cp(hcar[:, n:n + 1], R["r"][0][:, TW - 1:TW], reads=[R["r"][1]], writes=["hcar"])
                tt(c.hreg[:, n, :], R["r"][0], R["gl"][0], ALU.mult, reads=[R["r"][1], R["gl"][1]], writes=[K(c, "h", n)])

            return A1, A2, B_act, B_dve

        s_xrp = ps[:, b6 + 128:b6 + 256]
        s_yrp = ps[:, b7 + 128:b7 + 256]

        def rnn_sample_proj(c, slot, n):
            w = wv3(slot, 8 * 256, 256)
            for kc in range(8):
                mm(s_xrp[:, n * NS:(n + 1) * NS], w[:, kc, 0:128], c.xn[:, kc, :], kc == 0, kc == 7, reads=[("w", slot), K(c, "xn", kc)], writes=BK(6))
            for kc in range(8):
                mm(s_yrp[:, n * NS:(n + 1) * NS], w[:, kc, 128:256], c.xn[:, kc, :], kc == 0, kc == 7, reads=[("w", slot), K(c, "xn", kc)], writes=BK(7))

        def rnn_sample_tail(c, part):
            v3 = lambda ap: ap.rearrange("p (n t) -> p n t", t=NS)
            bc = lambda col: par[:, col:col + 8].rearrange("p (n o) -> p n o", o=1).to_broadcast([128, 8, NS])
            dbc = lambda col: der[:, col:col + 8].rearrange("p (n o) -> p n o", o=1).to_broadcast([128, 8, NS])
            U = lambda i: s_u[:, i, :, :]
            uk = lambda i: ("s_u", i)
            if part == "a":
                act(s_xr[:, :, :], v3(s_xrp), AF.Copy, reads=BK(6), writes=["s_xr"])
                act(U(0), v3(s_yrp), AF.Gelu_apprx_tanh, reads=BK(7), writes=[uk(0)])
                tt(U(1), s_c0[:, :, :, 0], bc(P_CW + 0), ALU.mult, reads=["s_c0", "par"], writes=[uk(1)])
                for j in (1, 2):
                    tt(U(2), s_c0[:, :, :, j], bc(P_CW + j * 8), ALU.mult, reads=["s_c0", "par"], writes=[uk(2)])
                    tt(U(1), U(1), U(2), ALU.add, reads=[uk(1), uk(2)], writes=[uk(1)])
                tt(U(2), s_xr[:, :, :], bc(P_CW + 3 * 8), ALU.mult, reads=["s_xr", "par"], writes=[uk(2)])
                tt(U(1), U(1), U(2), ALU.add, reads=[uk(1), uk(2)], writes=[uk(1)])
                tt(U(1), U(1), bc(P_CB), ALU.add, reads=[uk(1), "par"], writes=[uk(1)])
                act(s_xcb3[:, :, :], U(1), AF.Copy, reads=[uk(1)], writes=["s_xcb3"])
                return
            for n in range(8):
                mm(s_xrp[:, n * NS:(n + 1) * NS], rgw[:, 0, n, :], s_xcb3[:, n, :], True, True, reads=["rgw", "s_xcb3"], writes=BK(6))
                mm(s_yrp[:, n * NS:(n + 1) * NS], rgw[:, 1, n, :], s_xcb3[:, n, :], True, True, reads=["rgw", "s_xcb3"], writes=BK(7))
            tt(U(2), v3(s_xrp), bc(P_BA), ALU.add, reads=BK(6) + ["par"], writes=[uk(2)])
            tt(U(3), v3(s_yrp), bc(P_BX), ALU.add, reads=BK(7) + ["par"], writes=[uk(3)])
            act(U(2), U(2), AF.Sigmoid, reads=[uk(2)], writes=[uk(2)])
            act(U(3), U(3), AF.Sigmoid, reads=[uk(3)], writes=[uk(3)])
            tt(U(2), U(2), dbc(DC), ALU.mult, reads=[uk(2), "der"], writes=[uk(2)])
            act(U(4), U(2), AF.Exp, reads=[uk(2)], writes=[uk(4)])
            act(U(5), U(2), AF.Exp, reads=[uk(2)], writes=[uk(5)], scale=2.0)
            act(U(5), U(5), AF.Sqrt, reads=[uk(5)], writes=[uk(5)], scale=-1.0, bias=1.0)
            tt(U(3), U(3), U(5), ALU.mult, reads=[uk(3), uk(5)], writes=[uk(3)])
            tt(U(3), U(3), U(1), ALU.mult, reads=[uk(3), uk(1)], writes=[uk(3)])
            tt(U(4), U(4), s_h0[:, :, :], ALU.mult, reads=[uk(4), "s_h0"], writes=[uk(4)])
            tt(s_hn[:, :, :], U(4), U(3), ALU.add, reads=[uk(4), uk(3)], writes=["s_hn"])
            tt(c.hreg[:, 0:8, :], s_hn[:, :, :], U(0), ALU.mult, reads=["s_hn", uk(0)], writes=[K(c, "h", n) for n in range(8)])

        def gla_lr(c, lrb, lrkey):
            s = nextset(c)
            proj(c, s, lambda kc: wlrin[:, kc, :], lambda kc: c.xn[:, kc, :], 8, lambda kc: ["wlrin", K(c, "xn", kc)], M=16)
            act(lrb, c.ps[s][0:16, 0:c.W], AF.Copy, reads=[pskey(c, s)], writes=[lrkey])

        def gla_decay(c, hd, lrb, lrkey, Tl, kl, s=None):
            s = s or nextset(c)
            for (g0, g1) in c.groups:
                mm(c.ps[s][:, g0:g1], wlr[0:16, hd * 128:(hd + 1) * 128], lrb[0:16, g0:g1], True, True,
                   reads=["wlr", lrkey], writes=[pskey(c, s)])
            act(Tl, c.ps[s][:, 0:c.W], AF.Exp, reads=[pskey(c, s), "der"], writes=[kl], scale=-1.0, bias=der[:, DNB + hd:DNB + hd + 1])
            act(Tl, Tl, AF.Ln, reads=[kl], writes=[kl], bias=1.0)

        def gla_gate_out(c, hd, slot_v, o_aps, o_keys, Trs, krs, Tsg, ksg, Tt, ktt, stat_ps, stat_key, sg_pre=None):
            wv = wv3(slot_v, 8 * 512, 512)
            for (g0, g1) in c.groups:
                for dvc in range(2):
                    i = c.sqi % 2
                    c.sqi += 1
                    act(c.sq[:, i, g0:g1], o_aps[dvc][:, g0:g1], AF.Square, reads=[o_keys[dvc]], writes=[K(c, "sq", i)])
                    mm(stat_ps[:, 0:g1 - g0], ones256, c.sq[:, i, g0:g1], dvc == 0, dvc == 1, reads=[K(c, "sq", i), "cbf"], writes=stat_key)
                act(Trs[:, g0:g1], stat_ps[:, 0:g1 - g0], AF.Ln, reads=stat_key + ["der"], writes=[krs], bias=der[:, DEPS:DEPS + 1], scale=1.0)
            act(Trs, Trs, AF.Exp, reads=[krs], writes=[krs], scale=-0.5)
            for dvc in range(2):
                if sg_pre is not None:
                    Tsg_, ksg_ = sg_pre[dvc]
                else:
                    s = c.gla_og or nextset(c)
                    proj(c, s, lambda kc: wv[:, kc, 256 + dvc * 128:256 + (dvc + 1) * 128], lambda kc: c.xn[:, kc, :], 8,
                         lambda kc: [("w", slot_v), K(c, "xn", kc)])
                    act(Tsg, c.ps[s][:, 0:c.W], AF.Silu, reads=[pskey(c, s)], writes=[ksg])
                    Tsg_, ksg_ = Tsg, ksg
                stt(Tt, o_aps[dvc][:, 0:c.W], pcol(P_GNG, dvc), Trs, ALU.mult, ALU.mult, reads=[o_keys[dvc], "par", krs], writes=[ktt])
                tt(c.hreg[:, 8 + hd * 2 + dvc, :], Tt, Tsg_, ALU.mult, reads=[ktt, ksg_], writes=[K(c, "h", 8 + hd * 2 + dvc)])

        def gla_prompt(c, slot_qk, slot_v, hd, tile):
            wq = wv3(slot_qk, 8 * 256, 256)
            wv = wv3(slot_v, 8 * 512, 512)
            yk = lambda i: K(c, "y", i)
            gla_decay(c, hd, lr_bf, "lr_bf", T(c, 0), yk(0), s="C")
            P.add("dve", lambda e: e.tensor_tensor_scan(out=T(c, 1), data0=cst[:, C_RMASK - 256:C_RMASK - 256 + TW], data1=T(c, 0), initial=0.0, op0=ALU.mult, op1=ALU.add),
                  reads=[yk(0), "cst"], writes=[yk(1)])
            act(T(c, 2), T(c, 1), AF.Exp, reads=[yk(1)], writes=[yk(2)], scale=-1.0 / 16.0)
            act(T(c, 3), T(c, 1), AF.Exp, reads=[yk(1)], writes=[yk(3)], scale=1.0 / 16.0)
            qi = c.y[:, 4, :].bitcast(BF16)[:, 0:TW]
            ki = c.y[:, 4, :].bitcast(BF16)[:, TW:2 * TW]
            attT = c.y[:, 5, :].bitcast(BF16)[:, 0:TW]
            kiT = c.y[:, 5, :].bitcast(BF16)[:, TW:2 * TW]
            v_bf = c.y[:, 6, :].bitcast(BF16).rearrange("p (b v) -> p b v", v=DV)
            def vhalf(hf, sv):
                for bb in range(4):
                    bk = hf * 4 + bb
                    for kc in range(8):
                        mm(c.ps[sv][:, bb * DV:(bb + 1) * DV], c.xn[:, kc, bk * 128:(bk + 1) * 128], wv[:, kc, 0:DV], kc == 0, kc == 7,
                           reads=[("w", slot_v), K(c, "xn", kc)], writes=[pskey(c, sv)])
                act(c.y[:, 6, :].bitcast(BF16)[:, hf * TW:(hf + 1) * TW], c.ps[sv][:, 0:TW], AF.Copy, reads=[pskey(c, sv)], writes=[(c.name, "y", 6, hf)])
            sq_ = "A"
            proj(c, sq_, lambda kc: wq[:, kc, 0:128], lambda kc: c.xn[:, kc, :], 8, lambda kc: [("w", slot_qk), K(c, "xn", kc)])
            sk_ = "B"
            proj(c, sk_, lambda kc: wq[:, kc, 128:256], lambda kc: c.xn[:, kc, :], 8, lambda kc: [("w", slot_qk), K(c, "xn", kc)])
            stt(qi, c.ps[sq_][:, 0:TW], float(DK) ** -0.5, T(c, 2), ALU.mult, ALU.mult, reads=[pskey(c, sq_), yk(2)], writes=[(c.name, "y", 4, 0)])
            tt(ki, c.ps[sk_][:, 0:TW], T(c, 3), ALU.mult, reads=[pskey(c, sk_), yk(3)], writes=[(c.name, "y", 4, 1)])
            vhalf(0, "C")
            sa = "A"
            for bk in range(8):
                b0, b1 = bk * 128, (bk + 1) * 128
                mm(c.ps[sa][:, b0:b1], ki[:, b0:b1], qi[:, b0:b1], True, True, reads=[(c.name, "y", 4, 1), (c.name, "y", 4, 0)], writes=[pskey(c, sa)])
            maskb = cst[:, C_MASK - 256:C_MASK - 256 + 128].rearrange("p (o c) -> p o c", o=1).to_broadcast([128, 8, 128])
            tt(attT.rearrange("p (b c) -> p b c", c=128), c.ps[sa][:, 0:TW].rearrange("p (b c) -> p b c", c=128), maskb, ALU.mult,
               reads=[pskey(c, sa), "cst"], writes=[(c.name, "y", 5, 0)])
            vhalf(1, "B")
            psMb = c.psM.bitcast(BF16)
            for bk in range(8):
                b0, b1 = bk * 128, (bk + 1) * 128
                P.add("pe", lambda e, b0=b0, b1=b1: e.transpose(psMb[:, b0:b1], ki[:, b0:b1], identb),
                      reads=[(c.name, "y", 4, 1), "cbf"], writes=c.keyM)
            act(kiT, psMb[:, 0:TW], AF.Copy, reads=c.keyM, writes=[(c.name, "y", 5, 1)])
            osets = ["A", "B"]
            dSb = ps[:, 2048:2560]
            for bk in range(8):
                b0, b1 = bk * 128, (bk + 1) * 128
                sbi = bk % 2
                for dvc in range(2):
                    mm(c.ps[osets[dvc]][:, b0:b1], v_bf[:, bk, dvc * 128:(dvc + 1) * 128], attT[:, b0:b1], True, False,
                       reads=[K(c, "y", 6), (c.name, "y", 5, 0)], writes=[pskey(c, osets[dvc])])
                    mm(c.ps[osets[dvc]][:, b0:b1], S_b[:, sbi, dvc * 128:(dvc + 1) * 128], qi[:, b0:b1], False, True,
                       reads=[("S_b", sbi), (c.name, "y", 4, 0)], writes=[pskey(c, osets[dvc])])
                mk = BK(4)
                dS = dSb[:, (bk % 2) * 256:(bk % 2) * 256 + 256]
                mm(dS, kiT[:, b0:b1], v_bf[:, bk, :], True, True, reads=[(c.name, "y", 5, 1), K(c, "y", 6)], writes=mk)
                tt(S_t[:, :], S_f[:, hd, :], dS, ALU.add, reads=[("S_f", hd)] + mk, writes=["S_t"])
                eql = T(c, 2)[:, b1 - 1:b1]
                ts(S_b[:, 1 - sbi, :], S_t[:, :], eql, None, ALU.mult, None, reads=["S_t", yk(2)], writes=[("S_b", 1 - sbi)])
                ts(S_f[:, hd, :], S_t[:, :], eql, None, ALU.mult, None, reads=["S_t", yk(2)], writes=[("S_f", hd)])
                pi = bk // 2
                dvc, hf = pi // 2, pi % 2
                sgT, sgk = (T(c, 0), yk(0)) if dvc == 0 else (T(c, 3), yk(3))
                for kc in range(4 * (bk % 2), 4 * (bk % 2) + 4):
                    mm(c.psM[:, 0:512], wv[:, kc, 256 + dvc * 128:256 + (dvc + 1) * 128], c.xn[:, kc, hf * 512:(hf + 1) * 512], kc == 0, kc == 7,
                       reads=[("w", slot_v), K(c, "xn", kc)], writes=c.keyM)
                if bk % 2 == 1:
                    act(sgT[:, hf * 512:(hf + 1) * 512], c.psM[:, 0:512], AF.Silu, reads=c.keyM, writes=[sgk])
            c.gla_og = "C"
            gla_gate_out(c, hd, slot_v, [c.ps["A"], c.ps["B"]], [pskey(c, "A"), pskey(c, "B")], T(c, 7), yk(7), None, None, T(c, 1), yk(1),
                         c.psM, c.keyM, sg_pre=[(T(c, 0), yk(0)), (T(c, 3), yk(3))])

        s_state = {"loaded": set(), "q": 0}

        def s_piece_load(q):
            if q >= 16 or q in s_state["loaded"]:
                return
            s_state["loaded"].add(q)
            hd, p = q // 4, q % 4
            i = q % 3
            P.add("sp", lambda e: e.dma_start(out=s_S[:, i, :, :], in_=d_s0[p * 4:(p + 1) * 4, hd, :, :].rearrange("b k v -> k b v")),
                  writes=[("s_S", i)], dma=("s_S_in", i))

        def gla_sample(c, slot_qk, slot_v, hd):
            wq = wv3(slot_qk, 8 * 256, 256)
            wv = wv3(slot_v, 8 * 512, 512)
            tm = lambda i: s_t[:, i, :]
            tk = lambda i: ("s_t", i)
            s_piece_load(hd * 4)
            s_piece_load(hd * 4 + 1)
            gla_decay(c, hd, s_lr, "s_lr", tm(0), tk(0))
            act(s_eg[:, :], tm(0), AF.Exp, reads=[tk(0)], writes=["s_eg"], scale=-1.0 / 16.0)
            sq_ = nextset(c)
            proj(c, sq_, lambda kc: wq[:, kc, 0:128], lambda kc: c.xn[:, kc, :], 8, lambda kc: [("w", slot_qk), K(c, "xn", kc)])
            ts(s_q[:, :], c.ps[sq_][:, 0:NS], float(DK) ** -0.5, None, ALU.mult, None, reads=[pskey(c, sq_)], writes=["s_q"])
            tt(s_qgb[:, :], s_q[:, :], s_eg[:, :], ALU.mult, reads=["s_q", "s_eg"], writes=["s_qgb"])
            sk_ = nextset(c)
            proj(c, sk_, lambda kc: wq[:, kc, 128:256], lambda kc: c.xn[:, kc, :], 8, lambda kc: [("w", slot_qk), K(c, "xn", kc)])
            tt(tm(1), c.ps[sk_][:, 0:NS], s_q[:, :], ALU.mult, reads=[pskey(c, sk_), "s_q"], writes=[tk(1)])
            qk_ps = ps[:, b7 + 48:b7 + 64]
            cp(s_pb[:, 0, :], tm(1), reads=[tk(1)], writes=["s_pb"])
            tt(s_pb[:, 1, :], tm(1), s_pb[:, 0, :], ALU.subtract, reads=[tk(1), "s_pb"], writes=["s_pb"])
            mm(qk_ps, ones1024, s_pb[:, 0, :], True, False, reads=["cbf", "s_pb"], writes=BK(7))
            mm(qk_ps, ones1024, s_pb[:, 1, :], False, True, reads=["cbf", "s_pb"], writes=BK(7))
            ts(tm(2), qk_ps, 1024.0, None, ALU.mult, None, reads=BK(7), writes=[tk(2)])
            for dvc in range(2):
                sv = nextset(c)
                proj(c, sv, lambda kc: wv[:, kc, dvc * 128:(dvc + 1) * 128], lambda kc: c.xn[:, kc, :], 8, lambda kc: [("w", slot_v), K(c, "xn", kc)])
                tt(s_vq[:, dvc, :], c.ps[sv][:, 0:NS], tm(2), ALU.mult, reads=[pskey(c, sv), tk(2)], writes=[("s_vq", dvc)])
            bor = cp_.ps["C"]
            bk_ = cp_.banks["C"]
            for kc in range(8):
                mm(bor[0:NS, 0:128], c.xn[:, kc, :], wq[:, kc, 128:256], kc == 0, kc == 7, reads=[("w", slot_qk), K(c, "xn", kc)], writes=bk_)
            act(s_ktm[:, :], bor[0:NS, 0:128], AF.Copy, reads=bk_, writes=["s_ktm"])
            for kc in range(8):
                mm(bor[0:NS, 128:128 + DV], c.xn[:, kc, :], wv[:, kc, 0:DV], kc == 0, kc == 7, reads=[("w", slot_v), K(c, "xn", kc)], writes=bk_)
            act(s_vtm[:, :], bor[0:NS, 128:128 + DV], AF.Copy, reads=bk_, writes=["s_vtm"])
            so = [bor[:, 512:512 + NS], bor[:, 512 + NS:512 + 2 * NS]]
            dSp = [ps[:, b6 + 256:b6 + 512], ps[:, b7 + 256:b7 + 512]]
            dSk = [BK(6), BK(7)]
            for p in range(4):
                q = hd * 4 + p
                i = q % 3
                s_piece_load(q + 2)
                act(s_Sb[:, 0, :, :], s_S[:, i, :, :], AF.Copy, reads=[("s_S", i)], writes=[("s_Sb", 0)])
                for bb in range(4):
                    b = p * 4 + bb
                    for dvc in range(2):
                        mm(so[dvc][:, b:b + 1], s_Sb[:, 0, bb, dvc * 128:(dvc + 1) * 128], s_qgb[:, b:b + 1], True, True,
                           reads=[("s_Sb", 0), "s_qgb"], writes=bk_)
                ktb = s_ktm[:, :].rearrange("t (o k) -> t o k", o=1).to_broadcast([NS, 4, 128])
                dlt = cst[0:NS, C_IDENT - 256 + p * 4:C_IDENT - 256 + p * 4 + 4].rearrange("t (b o) -> t b o", o=1).to_broadcast([NS, 4, 128])
                tt(s_km[:, 0, :, :], ktb, dlt, ALU.mult, reads=["s_ktm", "cst"], writes=[("s_km", 0)])
                for bb in range(4):
                    b = p * 4 + bb
                    kmi = b % 2
                    mm(dSp[kmi], s_km[:, 0, bb, :], s_vtm[:, :], True, True, reads=[("s_km", 0), "s_vtm"], writes=dSk[kmi])
                    stt(s_S[:, i, bb, :], s_S[:, i, bb, :], s_eg[:, b:b + 1], dSp[kmi], ALU.mult, ALU.add,
                        reads=[("s_S", i), "s_eg"] + dSk[kmi], writes=[("s_S", i)])
                P.add("sp", lambda e, p=p, i=i: e.dma_start(out=o_ss[p * 4:(p + 1) * 4, hd, :, :].rearrange("b k v -> k b v"), in_=s_S[:, i, :, :]),
                      reads=[("s_S", i)], dma=("s_S_out", i))
            for dvc in range(2):
                tt(s_o[:, dvc, :], so[dvc], s_vq[:, dvc, :], ALU.add, reads=bk_ + [("s_vq", dvc)], writes=[("s_o", dvc)])
            c.gla_og = None
            gla_gate_out(c, hd, slot_v, [s_o[:, 0, :], s_o[:, 1, :]], [("s_o", 0), ("s_o", 1)], tm(7), tk(7), tm(8), tk(8), tm(9), tk(9),
                         c.psM, c.keyM)

        cp_.gla_avoid = ()
        cs_.gla_avoid = ()

        P.add("dve", lambda e: e.memset(halo[:, :, :], 0.0), writes=["halo"])
        P.add("dve", lambda e: e.memset(halo_bf[:, :, :], 0.0), writes=["halo_bf"])
        P.add("dve", lambda e: e.memset(hcar[:, :], 0.0), writes=["hcar"])
        P.add("dve", lambda e: e.memset(S_f[:, :, :], 0.0), writes=[("S_f", h) for h in range(NH)])
        P.add("sp", lambda e: e.dma_start(out=cs_.x[:, :, :], in_=d_xs), writes=[K(cs_, "x", ch) for ch in range(8)], dma="s_x_in")
        P.add("sp", lambda e: e.dma_start(out=s_h0[:, :, :], in_=d_h0), writes=["s_h0"], dma="s_h0")
        P.add("sp", lambda e: e.dma_start(out=s_c0[:, :, :, :], in_=d_c0), writes=["s_c0"], dma="s_c0")

        def load_x(tile, ch, eng="sp"):
            t0 = tile * TW
            P.add(eng, lambda e: e.dma_start(out=cp_.x[:, ch, :], in_=d_x[:, ch, t0:t0 + TW]), writes=[K(cp_, "x", ch)], dma=("xin_" + eng, ch))

        def store_y(tile, ch):
            t0 = tile * TW
            P.add("sp", lambda e: e.dma_start(out=o_y[:, ch, t0:t0 + TW], in_=cp_.x[:, ch, :]), reads=[K(cp_, "x", ch)], dma=("yout", ch))

        for ch in range(8):
            load_x(0, ch, "sp" if ch < 4 else "pool")
        deferred = []

        def run_deferred():
            while deferred:
                deferred.pop(0)()

        for tile in range(NT):
            ctxs = [cp_] + ([cs_] if tile == NT - 1 else [])
            bi = 0

            def ffn_body(gi_pre, bi):
                P.tag = "t%d:ffn%d_up" % (tile, gi_pre)
                for jb in range(NJ // 2):
                    slot = load_block(tile, bi); bi += 1
                    for c in ctxs:
                        if c is cs_:
                            run_deferred()
                        (s_ffn_up if c is cs_ else ffn_up)(c, slot, jb)
                P.tag = "t%d:ffn%d_dn" % (tile, gi_pre)
                for m in range(8):
                    slot = load_block(tile, bi); bi += 1
                    for c in ctxs:
                        (s_ffn_down if c is cs_ else ffn_down)(c, slot, m)
                return bi

            P.tag = "t%d:prenorm0" % tile
            for c in ctxs:
                if c is cs_:
                    deferred.append(lambda c=c: prenorm(c, 0))
                else:
                    prenorm(c, 0)
            bi = ffn_body(0, bi)
            P.tag = "t%d:epi1" % tile
            for c in ctxs:
                if c is cs_:
                    deferred.append(lambda c=c: epilogue_prenorm(c, 1, True, 2))
                else:
                    epilogue_prenorm(c, 1, True, 2)
            P.tag = "t%d:lr" % tile
            gla_lr(cp_, lr_bf[:, :], "lr_bf")
            if cs_ in ctxs:
                deferred.append(lambda: gla_lr(cs_, s_lr[:, :], "s_lr"))
            P.tag = "t%d:rnn" % tile
            prev = None
            for n in range(8):
                slot = load_block(tile, bi); bi += 1
                A1, A2, Ba, Bd = rnn_parts(cp_, slot, n, tile)
                if prev is not None:
                    prev[0]()
                A1()
                if cs_ in ctxs:
                    run_deferred()
                    rnn_sample_proj(cs_, slot, n)
                A2()
                if prev is not None:
                    prev[1]()
                prev = (Ba, Bd)
            prev[0]()
            prev[1]()
            if cs_ in ctxs:
                rnn_sample_tail(cs_, "a")
            P.tag = "t%d:gla" % tile
            for hd in range(NH):
                P.add("dve", lambda e: e.memset(S_b[:, 0, :], 0.0), writes=[("S_b", 0)]) if tile == 0 else None
                if tile > 0:
                    cp(S_b[:, 0, :], S_f[:, hd, :], reads=[("S_f", hd)], writes=[("S_b", 0)])
                slot_qk = load_block(tile, bi); bi += 1
                slot_v = load_block(tile, bi); bi += 1
                gla_prompt(cp_, slot_qk, slot_v, hd, tile)
                if cs_ in ctxs and hd == 0:
                    rnn_sample_tail(cs_, "b")
                if cs_ in ctxs:
                    gla_sample(cs_, slot_qk, slot_v, hd)
            P.tag = "t%d:merge" % tile
            for m in range(8):
                slot_g = load_block(tile, bi); bi += 1
                slot_b = load_block(tile, bi); bi += 1
                for c in ctxs:
                    (s_merge_m if c is cs_ else merge_m)(c, slot_g, slot_b, m)
            P.tag = "t%d:outproj" % tile
            for mp in range(4):
                slot = load_block(tile, bi); bi += 1
                for i in range(2):
                    for c in ctxs:
                        (s_outproj if c is cs_ else outproj)(c, slot, i, mp * 2 + i)
            P.tag = "t%d:epi3" % tile
            for c in ctxs:
                if c is cs_:
                    deferred.append(lambda c=c: epilogue_prenorm(c, 3, False, 4))
                else:
                    epilogue_prenorm(c, 3, False, 4)
            bi = ffn_body(4, bi)
            assert bi == len(PLAN)
            P.tag = "t%d:epi5" % tile
            pend_load = []

            def after5(ch):
                store_y(tile, ch)
                if tile + 1 < NT:
                    pend_load.append(ch)
                    if len(pend_load) > 1:
                        load_x(tile + 1, pend_load.pop(0))

            for c in ctxs:
                epilogue(c, 5, True, after_chunk=after5 if c is cp_ else None)
            while pend_load:
                load_x(tile + 1, pend_load.pop(0))

        P.add("sp", lambda e: e.dma_start(out=o_ys, in_=cs_.x[:, :, :]), reads=[K(cs_, "x", ch) for ch in range(8)], dma="o_ys")
        P.add("sp", lambda e: e.dma_start(out=o_hp, in_=hcar[:, :]), reads=["hcar"], dma="o_hp")
        P.add("sp", lambda e: e.dma_start(out=o_cp, in_=halo[:, :, :]), reads=["halo"], dma="o_cp")
        P.add("sp", lambda e: e.dma_start(out=o_sp, in_=S_f[:, :, :]), reads=[("S_f", h) for h in range(NH)], dma="o_sp")
        P.add("sp", lambda e: e.dma_start(out=o_hs, in_=s_hn[:, :, :]), reads=["s_hn"], dma="o_hs")
        cp(s_cn[:, :, :, 0:2], s_c0[:, :, :, 1:3], reads=["s_c0"], writes=["s_cn"])
        cp(s_cn[:, :, :, 2], s_xr[:, :, :], reads=["s_xr", "s_cn"], writes=["s_cn"])
        P.add("sp", lambda e: e.dma_start(out=o_cs, in_=s_cn[:, :, :, :]), reads=["s_cn"], dma="o_cs")
        if debug:
            for name in debug:
                src, keys = DEBUG_SRC[name](locals())
                P.add("sp", lambda e, name=name, src=src: e.dma_start(out=dbg_out[name], in_=src), reads=keys, dma="dbg_" + name)
        P.emit()
    TAGMAP.clear()
    TAGMAP.update(P.tagmap)
    return nc


DEBUG_SRC = {}
TAGMAP = {}
_NC_CACHE = {}


def _prep_inputs(inp):
    inp = {k: np.asarray(v) for k, v in inp.items()}
    ws = _pack_weights(inp)
    par = _pack_params(inp)
    cst = _consts()
    w_in = inp["w_in"][0]
    wlrin = _kc(w_in, np.arange(OFF_LR, OFF_LR + 16))
    rgw = np.ascontiguousarray(np.stack([inp["rg_w_a"][0].transpose(1, 0, 2), inp["rg_w_x"][0].transpose(1, 0, 2)], axis=1)).astype(np.float32)
    wlr = np.ascontiguousarray(inp["gla_w_lr"][0])
    maps = []
    for c in range(NCORES):
        x = inp["x_prompt"][c]
        xT = np.ascontiguousarray(x.reshape(SEQ, 8, 128).transpose(2, 1, 0))
        sl = slice(c * NS, (c + 1) * NS)
        xs = inp["x_sample"][sl, 0, :]
        xsT = np.ascontiguousarray(xs.reshape(NS, 8, 128).transpose(2, 1, 0))
        h0 = np.ascontiguousarray(inp["state_rnn_h"][0, sl].reshape(NS, 8, 128).transpose(2, 1, 0))
        c0 = np.ascontiguousarray(inp["state_rnn_conv"][0, sl].reshape(NS, 3, 8, 128).transpose(3, 2, 0, 1))
        s0 = np.ascontiguousarray(inp["state_gla"][0, sl])
        maps.append({"xT": xT, "xsT": xsT, "h0": h0, "c0": c0, "s0": s0, "ws": ws, "par": par, "cst": cst,
                     "wlrin": wlrin, "rgw": rgw, "wlr": wlr})
    return maps


def _assemble(results):
    yp = np.empty((NCORES, SEQ, D), np.float32)
    ys = np.empty((NCORES * NS, 1, D), np.float32)
    hp = np.empty((1, NCORES, D), np.float32)
    cpo = np.empty((1, NCORES, 3, D), np.float32)
    spo = np.empty((1, NCORES, NH, 128, DV), np.float32)
    hs = np.empty((1, NCORES * NS, D), np.float32)
    cso = np.empty((1, NCORES * NS, 3, D), np.float32)
    sso = np.empty((1, NCORES * NS, NH, 128, DV), np.float32)
    for c, r in enumerate(results):
        sl = slice(c * NS, (c + 1) * NS)
        yp[c] = np.asarray(r["yT"]).transpose(2, 1, 0).reshape(SEQ, D)
        ys[sl, 0] = np.asarray(r["ysT"]).transpose(2, 1, 0).reshape(NS, D)
        hp[0, c] = np.asarray(r["hp"]).T.reshape(D)
        cpo[0, c] = np.asarray(r["cp"]).transpose(2, 1, 0).reshape(3, D)
        spo[0, c] = np.asarray(r["sp"]).transpose(1, 0, 2)
        hs[0, sl] = np.asarray(r["hs"]).transpose(2, 1, 0).reshape(NS, D)
        cso[0, sl] = np.asarray(r["cs"]).transpose(2, 3, 1, 0).reshape(NS, 3, D)
        sso[0, sl] = np.asarray(r["ss"])
    return (yp, ys, hp, cpo, spo, hs, cso, sso)


def kernel(**inputs):
    maps = _prep_inputs(inputs)
    if "nc" not in _NC_CACHE:
        _NC_CACHE["nc"] = build_program()
    nc = _NC_CACHE["nc"]
    res = run_bass_kernel_spmd(nc, maps, core_ids=list(range(NCORES)))
    return _assemble(res.results)
```

```python
import contextlib
import numpy as np
import concourse.bass as bass
import concourse.mybir as mybir
from concourse.bass_utils import run_bass_kernel_spmd

F32 = mybir.dt.float32
BF16 = mybir.dt.bfloat16
AF = mybir.ActivationFunctionType
ALU = mybir.AluOpType

NCORES = 8
D = 1024
SEQ = 2048
TW = 1024
NT = SEQ // TW
NS = 16
DFF = 2816
NJ = DFF // 128
EPS = 1e-6
DK = 128
DV = 256
NH = 4
OFF_XR, OFF_YR, OFF_Q, OFF_K, OFF_V, OFF_OG, OFF_LR, OFF_GA, OFF_GB = 0, 1024, 2048, 2560, 3072, 4096, 5120, 5136, 6160
SLOT = 4096
NSLOT = 3

P_GAIN = 0
P_CW = 48
P_CB = 80
P_BA = 88
P_BX = 96
P_LAM = 104
P_BLR = 112
P_GNG = 116
NPAR = 118
C_ONES1024 = 0
C_ONES256 = 128
C_IDENT = 256
C_MASK = 384
C_RMASK = 512
NCST = 1536

ENGS = ("pe", "act", "dve", "pool", "sp")


class _Op:
    __slots__ = ("eng", "idx", "fn", "deps", "dma", "sig", "sigval", "tag")

    def __init__(self, eng, idx, fn, dma):
        self.eng, self.idx, self.fn, self.dma = eng, idx, fn, dma
        self.tag = ""
        self.deps = {}
        self.sig = False
        self.sigval = None


class Prog:
    def __init__(self, nc):
        self.nc = nc
        self.ops = {e: [] for e in ENGS}
        self.last_write = {}
        self.readers = {}
        self.tag = ""
        self.tagmap = {}

    @staticmethod
    def _flat(keys):
        out = []
        for k in keys:
            if isinstance(k, list):
                out.extend(Prog._flat(k))
            else:
                out.append(k)
        return out

    def add(self, eng, fn, reads=(), writes=(), dma=None):
        op = _Op(eng, len(self.ops[eng]), fn, dma)
        op.tag = self.tag
        reads = self._flat(reads)
        writes = self._flat(writes)
        bk = [k for k in reads if isinstance(k, tuple) and k and k[0] == "bank"]
        if bk:
            reads = [k for k in reads if k not in bk]
            writes = writes + [k for k in bk if k not in writes]

        def dep(o):
            if o is None:
                return
            if o.dma is None and o.eng == "pe" and eng == "pe":
                return
            k = ("d", o.dma) if o.dma is not None else ("e", o.eng)
            cur = op.deps.get(k)
            if cur is None or o.idx > cur.idx:
                op.deps[k] = o

        for k in reads:
            dep(self.last_write.get(k))
        for k in writes:
            dep(self.last_write.get(k))
            for r in self.readers.get(k, ()):
                dep(r)
        for k in reads:
            self.readers.setdefault(k, []).append(op)
        for k in writes:
            self.last_write[k] = op
            self.readers[k] = []
        self.ops[eng].append(op)
        return op

    def emit(self, final_wait_eng="sp"):
        nc = self.nc
        for e in ENGS:
            for op in self.ops[e]:
                for d in op.deps.values():
                    d.sig = True
        for e in ENGS:
            for op in self.ops[e]:
                if op.dma is not None:
                    op.sig = True
        cnt = {}
        for e in ENGS:
            for op in self.ops[e]:
                if not op.sig:
                    continue
                k = ("d", op.dma) if op.dma is not None else ("e", op.eng)
                cnt[k] = cnt.get(k, 0) + (16 if op.dma is not None else 1)
                op.sigval = cnt[k]
        final = dict(cnt)
        with contextlib.ExitStack() as st:
            sems = {}
            for i, k in enumerate(cnt):
                sems[k] = st.enter_context(nc.semaphore("sem%d" % i))
            block = st.enter_context(nc.Block())
            engobj = {"pe": "tensor", "act": "scalar", "dve": "vector", "pool": "gpsimd", "sp": "sync"}

            def mk(e):
                def body(eng):
                    known = {}
                    for op in self.ops[e]:
                        for k, d in op.deps.items():
                            if known.get(k, 0) < d.sigval:
                                eng.wait_ge(sems[k], d.sigval)
                                known[k] = d.sigval
                        ins = op.fn(eng)
                        try:
                            self.tagmap[ins.ins.name] = op.tag
                        except Exception:
                            pass
                        if op.sig:
                            k = ("d", op.dma) if op.dma is not None else ("e", op.eng)
                            ins.then_inc(sems[k], 16 if op.dma is not None else 1)
                    if e == final_wait_eng:
                        for k, v in final.items():
                            if known.get(k, 0) < v:
                                eng.wait_ge(sems[k], v)
                return body

            for e in ENGS:
                if self.ops[e] or e == final_wait_eng:
                    getattr(block, engobj[e])(mk(e))


def _stream_plan():
    plan = []
    for f in (1,):
        pass
    def ffn(tag):
        out = []
        for jb in range(NJ // 2):
            out.append((tag + "gu", jb, 8 * 512))
        for m in range(8):
            out.append((tag + "dn", m, NJ * 128))
        return out
    plan += ffn("f1")
    for n in range(8):
        plan.append(("rnn", n, 8 * 256))
    for hd in range(NH):
        plan.append(("qk", hd, 8 * 256))
        plan.append(("vog", hd, 8 * 512))
    for m in range(8):
        plan.append(("gate", m, 8 * 256))
        plan.append(("br", m, 2 * 8 * 128))
    for mp in range(4):
        plan.append(("wout", mp, 2 * 8 * 128))
    plan += ffn("f2")
    offs = []
    o = 0
    for (_, _, n) in plan:
        offs.append(o)
        o += n
    return plan, offs, o


PLAN, PLAN_OFFS, WLEN = _stream_plan()


def _kc(w, cols):
    return np.ascontiguousarray(w[:, cols].reshape(8, 128, -1).transpose(1, 0, 2))


def _pack_weights(inp):
    wgu = {"f1": inp["ffn1_w_gu"][0], "f2": inp["ffn2_w_gu"][0]}
    wdn = {"f1": inp["ffn1_w_down"][0], "f2": inp["ffn2_w_down"][0]}
    w_in = inp["w_in"][0]
    wbr = inp["w_branch_rnn"][0]
    wbg = inp["w_branch_gla"][0]
    wout = inp["w_out"][0]
    ws = np.empty((128, WLEN), np.float32)
    ar = np.arange
    for (name, i, n), off in zip(PLAN, PLAN_OFFS):
        if name.endswith("gu"):
            w = wgu[name[:2]]
            cols = np.concatenate([ar(i * 256, i * 256 + 256), ar(DFF + i * 256, DFF + i * 256 + 256)])
            blk = _kc(w, cols)
        elif name.endswith("dn"):
            w = wdn[name[:2]]
            blk = w[:, i * 128:(i + 1) * 128].reshape(NJ, 128, 128).transpose(1, 0, 2)
        elif name == "rnn":
            cols = np.concatenate([ar(OFF_XR + i * 128, OFF_XR + i * 128 + 128), ar(OFF_YR + i * 128, OFF_YR + i * 128 + 128)])
            blk = _kc(w_in, cols)
        elif name == "qk":
            cols = np.concatenate([ar(OFF_Q + i * 128, OFF_Q + i * 128 + 128), ar(OFF_K + i * 128, OFF_K + i * 128 + 128)])
            blk = _kc(w_in, cols)
        elif name == "vog":
            cols = np.concatenate([ar(OFF_V + i * 256, OFF_V + i * 256 + 256), ar(OFF_OG + i * 256, OFF_OG + i * 256 + 256)])
            blk = _kc(w_in, cols)
        elif name == "gate":
            cols = np.concatenate([ar(OFF_GA + i * 128, OFF_GA + i * 128 + 128), ar(OFF_GB + i * 128, OFF_GB + i * 128 + 128)])
            blk = _kc(w_in, cols)
        elif name == "br":
            a = _kc(wbr, ar(i * 128, i * 128 + 128))
            b = _kc(wbg, ar(i * 128, i * 128 + 128))
            blk = np.stack([a, b], axis=1)
        elif name == "wout":
            a = _kc(wout, ar((2 * i) * 128, (2 * i) * 128 + 128))
            b = _kc(wout, ar((2 * i + 1) * 128, (2 * i + 1) * 128 + 128))
            blk = np.stack([a, b], axis=1)
        else:
            raise AssertionError(name)
        ws[:, off:off + n] = np.asarray(blk).reshape(128, n)
    return ws


def _fm(v):
    return np.ascontiguousarray(np.asarray(v).reshape(8, 128).T)


def _pack_params(inp):
    p = np.zeros((128, NPAR), np.float32)
    g = inp["norm_gains"][0]
    for i in range(6):
        p[:, P_GAIN + i * 8:P_GAIN + i * 8 + 8] = _fm(g[i])
    cw = inp["conv_w"][0]
    for j in range(4):
        p[:, P_CW + j * 8:P_CW + j * 8 + 8] = _fm(cw[j])
    p[:, P_CB:P_CB + 8] = _fm(inp["conv_b"][0])
    p[:, P_BA:P_BA + 8] = _fm(inp["rg_b_a"][0])
    p[:, P_BX:P_BX + 8] = _fm(inp["rg_b_x"][0])
    p[:, P_LAM:P_LAM + 8] = _fm(inp["rg_lambda"][0])
    p[:, P_BLR:P_BLR + 4] = np.asarray(inp["gla_b_lr"][0]).reshape(4, 128).T
    p[:, P_GNG:P_GNG + 2] = np.asarray(inp["gla_norm_g"][0]).reshape(2, 128).T
    return p


def _consts():
    c = np.zeros((128, NCST), np.float32)
    c[:, C_ONES1024:C_ONES1024 + 128] = 1.0 / 1024.0
    c[:, C_ONES256:C_ONES256 + 128] = 1.0 / 256.0
    c[:, C_IDENT:C_IDENT + 128] = np.eye(128, dtype=np.float32)
    s = np.arange(128)[:, None]
    cc = np.arange(128)[None, :]
    c[:, C_MASK:C_MASK + 128] = (s <= cc).astype(np.float32)
    rm = np.ones((1024,), np.float32)
    rm[::128] = 0.0
    c[:, C_RMASK:C_RMASK + 1024] = rm[None, :]
    return c


class Ctx:
    pass


def build_program(debug=None):
    nc = bass.Bass("TRN2", target_bir_lowering=False)
    di = lambda name, shape: nc.dram_tensor(name, shape, F32, kind="ExternalInput").ap()
    do = lambda name, shape: nc.dram_tensor(name, shape, F32, kind="ExternalOutput").ap()
    d_x = di("xT", [128, 8, SEQ])
    d_xs = di("xsT", [128, 8, NS])
    d_h0 = di("h0", [128, 8, NS])
    d_c0 = di("c0", [128, 8, NS, 3])
    d_s0 = di("s0", [NS, NH, 128, DV])
    d_ws = di("ws", [128, WLEN])
    d_par = di("par", [128, NPAR])
    d_cst = di("cst", [128, NCST])
    d_wlrin = di("wlrin", [128, 8, 16])
    d_rgw = di("rgw", [128, 2, 8, 128])
    d_wlr = di("wlr", [16, 512])
    o_y = do("yT", [128, 8, SEQ])
    o_ys = do("ysT", [128, 8, NS])
    o_hp = do("hp", [128, 8])
    o_cp = do("cp", [128, 8, 3])
    o_sp = do("sp", [128, NH, DV])
    o_hs = do("hs", [128, 8, NS])
    o_cs = do("cs", [128, 8, NS, 3])
    o_ss = do("ss", [NS, NH, 128, DV])
    dbg_out = {}
    if debug:
        for name, shape in debug.items():
            dbg_out[name] = do("dbg_" + name, shape)

    with contextlib.ExitStack() as st:
        def sb(name, shape, dt=F32):
            return st.enter_context(nc.sbuf_tensor("sb_" + name, shape, dt))

        P = Prog(nc)
        par = sb("par", [128, NPAR])
        cst = sb("cst", [128, NCST - 256])
        cbf = sb("cbf", [128, 512], BF16)
        wlrin = sb("wlrin", [128, 8, 16], BF16)
        rgw = sb("rgw", [128, 2, 8, 128], BF16)
        wlr = sb("wlr", [16, 512], BF16)
        der = sb("der", [128, 64])
        DC, DC2, DNB, DEPS, DEPS4, DHC, DHBA, DHBX, DQ25 = 0, 8, 16, 20, 21, 24, 32, 40, 48
        wring = [sb("wring%d" % i, [128, SLOT], BF16) for i in range(NSLOT)]
        ps = st.enter_context(nc.psum_tensor("ps", [128, 4096], F32))

        P.add("sp", lambda e: e.dma_start(out=par[:], in_=d_par), writes=["par"], dma="par")
        P.add("sp", lambda e: e.dma_start(out=cst[:], in_=d_cst[:, 256:NCST]), writes=["cst"], dma="cst")
        P.add("pool", lambda e: e.dma_start(out=cbf[:], in_=d_cst[:, 0:512]), writes=["cbf"], dma="cbf")
        P.add("pool", lambda e: e.dma_start(out=wlrin[:], in_=d_wlrin), writes=["wlrin"], dma="wlrin")
        P.add("pool", lambda e: e.dma_start(out=rgw[:], in_=d_rgw), writes=["rgw"], dma="rgw")
        P.add("pool", lambda e: e.dma_start(out=wlr[:], in_=d_wlr), writes=["wlr"], dma="wlr")
        ones1024 = cbf[:, 0:128]
        ones256 = cbf[:, 128:256]
        identb = cbf[:, 256:384]
        P.add("act", lambda e: e.activation(out=der[:, DC:DC + 8], in_=par[:, P_LAM:P_LAM + 8], func=AF.Exp, scale=-1.0),
              reads=["par"], writes=["der"])
        P.add("act", lambda e: e.activation(out=der[:, DC:DC + 8], in_=der[:, DC:DC + 8], func=AF.Ln, bias=1.0),
              reads=["der"], writes=["der"])
        P.add("dve", lambda e: e.tensor_scalar(out=der[:, DC2:DC2 + 8], in0=der[:, DC:DC + 8], scalar1=-16.0, scalar2=None, op0=ALU.mult),
              reads=["der"], writes=["der"])
        P.add("dve", lambda e: e.tensor_scalar(out=der[:, DC:DC + 8], in0=der[:, DC:DC + 8], scalar1=-8.0, scalar2=None, op0=ALU.mult),
              reads=["der"], writes=["der"])
        P.add("dve", lambda e: e.tensor_scalar(out=der[:, DNB:DNB + 4], in0=par[:, P_BLR:P_BLR + 4], scalar1=-1.0, scalar2=None, op0=ALU.mult),
              reads=["par", "der"], writes=["der"])
        P.add("dve", lambda e: e.tensor_scalar(out=der[:, DHC:DHC + 8], in0=der[:, DC:DC + 8], scalar1=0.5, scalar2=None, op0=ALU.mult),
              reads=["der"], writes=["der"])
        P.add("dve", lambda e: e.tensor_scalar(out=der[:, DHBA:DHBA + 8], in0=par[:, P_BA:P_BA + 8], scalar1=0.5, scalar2=None, op0=ALU.mult),
              reads=["par", "der"], writes=["der"])
        P.add("dve", lambda e: e.tensor_scalar(out=der[:, DHBX:DHBX + 8], in0=par[:, P_BX:P_BX + 8], scalar1=0.5, scalar2=None, op0=ALU.mult),
              reads=["par", "der"], writes=["der"])
        P.add("dve", lambda e: e.memset(der[:, DQ25:DQ25 + 1], 0.25), reads=["der"], writes=["der"])
        P.add("dve", lambda e: e.memset(der[:, DEPS:DEPS + 1], EPS), reads=["der"], writes=["der"])
        P.add("dve", lambda e: e.memset(der[:, DEPS4:DEPS4 + 1], 4.0 * EPS), reads=["der"], writes=["der"])

        def make_ctx(name, W, psA, psB, psC, psM, keyM, banks, psS=None):
            c = Ctx()
            c.name, c.W = name, W
            c.groups = [(g, min(g + 512, W)) for g in range(0, W, 512)]
            c.x = sb(name + "_x", [128, 8, W])
            c.xn = sb(name + "_xn", [128, 8, W], BF16)
            c.hreg = sb(name + "_h", [128, 24, W], BF16)
            c.y = sb(name + "_y", [128, 8, W])
            c.sq = sb(name + "_sq", [128, 2, W], BF16)
            c.rstd = sb(name + "_rstd", [128, W])
            c.ps = {"A": psA, "B": psB}
            if psC is not None:
                c.ps["C"] = psC
            c.banks = banks
            c.pend = None
            c.statset = "C"
            if psS is not None:
                c.ps["S"] = psS
                c.statset = "S"
            c.psM = psM
            c.keyM = keyM
            c.rr = 0
            c.sqi = 0
            return c

        BK = lambda *i: [("bank", j) for j in i]
        cp_ = make_ctx("p", TW, ps[:, 0:1024], ps[:, 1024:2048], ps[:, 2048:3072], ps[:, 2560:3072], BK(5),
                       {"A": BK(0, 1), "B": BK(2, 3), "C": BK(4, 5)})
        cp_.order = ["A", "B", "C"]
        b6, b7 = 3072, 3584
        cs_ = make_ctx("s", NS, ps[:, b6:b6 + 16], ps[:, b7:b7 + 16], None, ps[:, b6 + 16:b6 + 32], BK(6),
                       {"A": BK(6), "B": BK(7), "S": BK(6)}, psS=ps[:, b6 + 16:b6 + 32])
        cs_.order = ["A", "B"]
        cs_.sqall = sb("s_sqall", [128, 8, NS], BF16)

        def K(c, what, i=None):
            if what == "y" and i in (4, 5, 6):
                return [(c.name, "y", i, 0), (c.name, "y", i, 1)]
            return (c.name, what, i)

        def pskey(c, s):
            return c.banks[s]

        def nextset(c, avoid=()):
            order = c.order
            for _ in range(len(order) + 1):
                s = order[c.rr % len(order)]
                c.rr += 1
                if s not in avoid:
                    return s
            raise AssertionError

        def T(c, i):
            return c.y[:, i, :]

        halo_bf = sb("halo_bf", [128, 8, 3], BF16)
        halo = sb("halo", [128, 8, 3])
        hcar = sb("hcar", [128, 8])
        S_f = sb("S_f", [128, NH, DV])
        S_b = sb("S_b", [128, 2, DV], BF16)
        S_t = sb("S_t", [128, DV])
        lr_bf = sb("lr_bf", [16, TW], BF16)
        s_h0 = sb("s_h0", [128, 8, NS])
        s_c0 = sb("s_c0", [128, 8, NS, 3])
        s_cn = sb("s_cn", [128, 8, NS, 3])
        s_hn = sb("s_hn", [128, 8, NS])
        s_xr = sb("s_xr", [128, 8, NS])
        s_t = sb("s_t", [128, 12, NS])
        s_xcb3 = sb("s_xcb3", [128, 8, NS], BF16)
        s_u = sb("s_u", [128, 6, 8, NS])
        s_lr = sb("s_lr", [16, NS], BF16)
        s_ktm = sb("s_ktm", [16, 128], BF16)
        s_vtm = sb("s_vtm", [16, DV], BF16)
        s_km = sb("s_km", [16, 1, 4, 128], BF16)
        s_Sb = sb("s_Sb", [128, 1, 4, DV], BF16)
        s_S = sb("s_S", [128, 3, 4, DV])
        s_q = sb("s_q", [128, NS])
        s_qgb = sb("s_qgb", [128, NS], BF16)
        s_pb = sb("s_pb", [128, 2, NS], BF16)
        s_eg = sb("s_eg", [128, NS])
        s_vq = sb("s_vq", [128, 2, NS])
        s_o = sb("s_o", [128, 2, NS])

        wstate = {"i": 0}

        def load_block(tile, bi):
            name, idx, n = PLAN[bi]
            gi = wstate["i"]
            wstate["i"] += 1
            slot = gi % NSLOT
            off = PLAN_OFFS[bi]
            P.add("pool", lambda e, slot=slot, off=off, n=n: e.dma_start(out=wring[slot][:, 0:n], in_=d_ws[:, off:off + n]),
                  writes=[("w", slot)], dma=("w", slot))
            return slot

        def wv3(slot, n, inner):
            return wring[slot][:, 0:n].rearrange("p (k c) -> p k c", c=inner)

        def wv4(slot):
            return wring[slot][:, 0:2048].rearrange("p (a k c) -> p a k c", a=2, k=8)

        def mm(out, lhsT, rhs, start, stop, reads, writes):
            P.add("pe", lambda e: e.matmul(out, lhsT=lhsT, rhs=rhs, start=start, stop=stop), reads=reads, writes=writes)

        def act(out, in_, func, reads, writes, bias=None, scale=None):
            kw = {}
            if bias is not None:
                kw["bias"] = bias
            if scale is not None:
                kw["scale"] = scale
            P.add("act", lambda e: e.activation(out=out, in_=in_, func=func, **kw), reads=reads, writes=writes)

        def tt(out, in0, in1, op, reads, writes, eng="dve"):
            P.add(eng, lambda e: e.tensor_tensor(out=out, in0=in0, in1=in1, op=op), reads=reads, writes=writes)

        def stt(out, in0, scalar, in1, op0, op1, reads, writes, eng="dve"):
            P.add(eng, lambda e: e.scalar_tensor_tensor(out=out, in0=in0, scalar=scalar, in1=in1, op0=op0, op1=op1), reads=reads, writes=writes)

        def ts(out, in0, s1, s2, op0, op1, reads, writes, eng="dve"):
            if s2 is None:
                P.add(eng, lambda e: e.tensor_scalar(out=out, in0=in0, scalar1=s1, scalar2=None, op0=op0), reads=reads, writes=writes)
            else:
                P.add(eng, lambda e: e.tensor_scalar(out=out, in0=in0, scalar1=s1, scalar2=s2, op0=op0, op1=op1), reads=reads, writes=writes)

        def cp(out, in_, reads, writes, eng="dve"):
            P.add(eng, lambda e: e.tensor_copy(out=out, in_=in_), reads=reads, writes=writes)

        def proj(c, s, lhsT_of_kc, rhs_chunks, nk, reads, M=128):
            for kc in range(nk):
                for (g0, g1) in c.groups:
                    mm(c.ps[s][0:M, g0:g1], lhsT_of_kc(kc), rhs_chunks(kc)[:, g0:g1], kc == 0, kc == nk - 1,
                       reads=reads(kc), writes=[pskey(c, s)])

        def stat_acc(c, src, src_reads, first, last, ones, statset=None):
            statset = statset or c.statset
            i = c.sqi % 2
            c.sqi += 1
            act(c.sq[:, i, :], src, AF.Square, reads=src_reads, writes=[K(c, "sq", i)])
            for (g0, g1) in c.groups:
                mm(c.ps[statset][:, g0:g1], ones, c.sq[:, i, g0:g1], first, last,
                   reads=[K(c, "sq", i), "cbf"], writes=[pskey(c, statset)])

        def rstd_from_stat(c, half, statset=None):
            statset = statset or c.statset
            flush_stat(c)
            if half:
                act(c.rstd[:, :], c.ps[statset][:, 0:c.W], AF.Ln, reads=[pskey(c, statset), "der"], writes=[K(c, "rstd")],
                    bias=der[:, DEPS4:DEPS4 + 1], scale=4.0)
            else:
                act(c.rstd[:, :], c.ps[statset][:, 0:c.W], AF.Ln, reads=[pskey(c, statset), "der"], writes=[K(c, "rstd")],
                    bias=der[:, DEPS:DEPS + 1], scale=1.0)
            act(c.rstd[:, :], c.rstd[:, :], AF.Exp, reads=[K(c, "rstd")], writes=[K(c, "rstd")], scale=-0.5)

        def flush_stat(c):
            if c.pend is not None:
                i, first, last = c.pend
                c.pend = None
                for (g0, g1) in c.groups:
                    mm(c.ps[c.statset][:, g0:g1], ones1024, c.sq[:, i, g0:g1], first, last,
                       reads=[K(c, "sq", i), "cbf"], writes=[pskey(c, c.statset)])

        def stat_all(c, src3, src_keys):
            act(c.sqall[:, :, :], src3, AF.Square, reads=src_keys, writes=[K(c, "sqall")])
            for ch in range(8):
                mm(c.ps["S"][:, 0:c.W], ones1024, c.sqall[:, ch, :], ch == 0, ch == 7, reads=[K(c, "sqall"), "cbf"], writes=[pskey(c, "S")])

        def prenorm_finish(c, gi):
            rstd_from_stat(c, False)
            for ch in range(8):
                stt(c.xn[:, ch, :], c.x[:, ch, :], par[:, P_GAIN + gi * 8 + ch:P_GAIN + gi * 8 + ch + 1], c.rstd[:, :], ALU.mult, ALU.mult,
                    reads=[K(c, "x", ch), K(c, "rstd"), "par"], writes=[K(c, "xn", ch)])

        def xn_pre(c, gi, ch, eng="act"):
            g = par[:, P_GAIN + gi * 8 + ch:P_GAIN + gi * 8 + ch + 1]
            if eng == "act":
                act(c.xn[:, ch, :], c.x[:, ch, :], AF.Identity, reads=[K(c, "x", ch), "par"], writes=[K(c, "xn", ch)], scale=g)
            else:
                ts(c.xn[:, ch, :], c.x[:, ch, :], g, None, ALU.mult, None, reads=[K(c, "x", ch), "par"], writes=[K(c, "xn", ch)])

        def prenorm(c, gi):
            if c is cs_:
                s_prenorm(c, gi)
                return
            for ch in range(8):
                stat_acc(c, c.x[:, ch, :], [K(c, "x", ch)], ch == 0, ch == 7, ones1024)
                if gi in (0, 4):
                    xn_pre(c, gi, ch, eng="dve")
            if gi in (0, 4):
                rstd_from_stat(c, False)
            else:
                prenorm_finish(c, gi)

        def epilogue(c, gi, half, after_chunk=None):
            if c is cs_:
                s_epilogue(c, gi, half)
                return
            rstd_from_stat(c, half)
            for ch in range(8):
                stt(c.y[:, ch, :], c.y[:, ch, :], par[:, P_GAIN + gi * 8 + ch:P_GAIN + gi * 8 + ch + 1], c.rstd[:, :], ALU.mult, ALU.mult,
                    reads=[K(c, "y", ch), K(c, "rstd"), "par"], writes=[K(c, "y", ch)])
                tt(c.x[:, ch, :], c.x[:, ch, :], c.y[:, ch, :], ALU.add, reads=[K(c, "x", ch), K(c, "y", ch)], writes=[K(c, "x", ch)])
                if after_chunk is not None:
                    after_chunk(ch)

        def epilogue_prenorm(c, gi_post, half, gi_pre):
            if c is cs_:
                epilogue(c, gi_post, half)
                prenorm(c, gi_pre)
                return
            if gi_pre in (0, 4):
                def after(ch):
                    stat_acc(c, c.x[:, ch, :], [K(c, "x", ch)], ch == 0, ch == 7, ones1024)
                    xn_pre(c, gi_pre, ch)
                epilogue(c, gi_post, half, after_chunk=after)
                rstd_from_stat(c, False)
                return
            epilogue(c, gi_post, half, after_chunk=lambda ch: stat_acc(c, c.x[:, ch, :], [K(c, "x", ch)], ch == 0, ch == 7, ones1024))
            prenorm_finish(c, gi_pre)

        def out_chunk(c, s, m, first, last):
            act(c.y[:, m, :], c.ps[s][:, 0:c.W], AF.Copy, reads=[pskey(c, s)], writes=[K(c, "y", m)])
            if c is cs_:
                return
            i = c.sqi % 2
            c.sqi += 1
            act(c.sq[:, i, :], c.ps[s][:, 0:c.W], AF.Square, reads=[pskey(c, s)], writes=[K(c, "sq", i)])
            c.pend = (i, first, last)

        def ffn_up(c, slot, jb):
            w = wv3(slot, 8 * 512, 512)
            for jj in range(2):
                j = jb * 2 + jj
                sg_, su_ = nextset(c), nextset(c)
                proj(c, sg_, lambda kc: w[:, kc, jj * 128:(jj + 1) * 128], lambda kc: c.xn[:, kc, :], 8,
                     lambda kc: [("w", slot), K(c, "xn", kc)])
                proj(c, su_, lambda kc: w[:, kc, 256 + jj * 128:256 + (jj + 1) * 128], lambda kc: c.xn[:, kc, :], 8,
                     lambda kc: [("w", slot), K(c, "xn", kc)])
                ta, tb = (4, 5) if j % 2 == 0 else (6, 7)
                rk = K(c, "rstd")
                tt(T(c, ta), c.ps[sg_][:, 0:c.W], c.rstd[:, :], ALU.mult, reads=[pskey(c, sg_), rk], writes=[K(c, "y", ta)])
                act(T(c, ta), T(c, ta), AF.Silu, reads=[K(c, "y", ta)], writes=[K(c, "y", ta)])
                tt(T(c, tb), c.ps[su_][:, 0:c.W], c.rstd[:, :], ALU.mult, reads=[pskey(c, su_), rk], writes=[K(c, "y", tb)])
                tt(c.hreg[:, j, :], T(c, ta), T(c, tb), ALU.mult, reads=[K(c, "y", ta), K(c, "y", tb)], writes=[K(c, "h", j)])

        def ffn_down(c, slot, m):
            w = wv3(slot, NJ * 128, 128)
            s = nextset(c, avoid=("C",))
            proj(c, s, lambda j: w[:, j, :], lambda j: c.hreg[:, j, :], NJ, lambda j: [("w", slot), K(c, "h", j)])
            flush_stat(c)
            out_chunk(c, s, m, m == 0, m == 7)

        def merge_m(c, slot_g, slot_b, m):
            wg = wv3(slot_g, 8 * 256, 256)
            wb = wv4(slot_b)
            sA = nextset(c)
            proj(c, sA, lambda kc: wg[:, kc, 0:128], lambda kc: c.xn[:, kc, :], 8, lambda kc: [("w", slot_g), K(c, "xn", kc)])
            act(T(c, 0), c.ps[sA][:, 0:c.W], AF.Sigmoid, reads=[pskey(c, sA)], writes=[K(c, "y", 0)])
            sY = nextset(c)
            proj(c, sY, lambda n: wb[:, 0, n, :], lambda n: c.hreg[:, n, :], 8, lambda n: [("w", slot_b), K(c, "h", n)])
            tt(T(c, 0), T(c, 0), c.ps[sY][:, 0:c.W], ALU.mult, reads=[K(c, "y", 0), pskey(c, sY)], writes=[K(c, "y", 0)])
            sB = nextset(c)
            proj(c, sB, lambda kc: wg[:, kc, 128:256], lambda kc: c.xn[:, kc, :], 8, lambda kc: [("w", slot_g), K(c, "xn", kc)])
            act(T(c, 1), c.ps[sB][:, 0:c.W], AF.Sigmoid, reads=[pskey(c, sB)], writes=[K(c, "y", 1)])
            sY2 = nextset(c)
            proj(c, sY2, lambda n: wb[:, 1, n, :], lambda n: c.hreg[:, 8 + n, :], 8, lambda n: [("w", slot_b), K(c, "h", 8 + n)])
            tt(T(c, 1), T(c, 1), c.ps[sY2][:, 0:c.W], ALU.mult, reads=[K(c, "y", 1), pskey(c, sY2)], writes=[K(c, "y", 1)])
            tt(c.hreg[:, 16 + m, :], T(c, 0), T(c, 1), ALU.add, reads=[K(c, "y", 0), K(c, "y", 1)], writes=[K(c, "h", 16 + m)])

        def outproj(c, slot, i, m2):
            w = wv4(slot)
            s = nextset(c, avoid=("C",))
            proj(c, s, lambda m: w[:, i, m, :], lambda m: c.hreg[:, 16 + m, :], 8, lambda m: [("w", slot), K(c, "h", 16 + m)])
            flush_stat(c)
            out_chunk(c, s, m2, m2 == 0, m2 == 7)

        s_flat = lambda: s_u[:, :, :, :].rearrange("p a n t -> p (a n t)")
        g3 = lambda gi: par[:, P_GAIN + gi * 8:P_GAIN + gi * 8 + 8].rearrange("p (n o) -> p n o", o=1).to_broadcast([128, 8, NS])
        r3 = lambda c: c.rstd[:, :].rearrange("p (o t) -> p o t", o=1).to_broadcast([128, 8, NS])

        def s_prenorm(c, gi):
            stat_all(c, c.x[:, :, :], [K(c, "x", ch) for ch in range(8)])
            rstd_from_stat(c, False)
            tt(s_u[:, 5, :, :], c.x[:, :, :], g3(gi), ALU.mult, reads=[K(c, "x", ch) for ch in range(8)] + ["par"], writes=[("s_u", 5)])
            tt(c.xn[:, :, :], s_u[:, 5, :, :], r3(c), ALU.mult, reads=[("s_u", 5), K(c, "rstd")], writes=[K(c, "xn", ch) for ch in range(8)])

        def s_epilogue(c, gi, half):
            yk_ = [K(c, "y", ch) for ch in range(8)]
            stat_all(c, c.y[:, :, :], yk_)
            rstd_from_stat(c, half)
            tt(c.y[:, :, :], c.y[:, :, :], g3(gi), ALU.mult, reads=yk_ + ["par"], writes=yk_)
            tt(c.y[:, :, :], c.y[:, :, :], r3(c), ALU.mult, reads=yk_ + [K(c, "rstd")], writes=yk_)
            tt(c.x[:, :, :], c.x[:, :, :], c.y[:, :, :], ALU.add, reads=yk_ + [K(c, "x", ch) for ch in range(8)], writes=[K(c, "x", ch) for ch in range(8)])

        def s_ffn_up(c, slot, jb):
            w = wv3(slot, 8 * 512, 512)
            for jj in range(2):
                j = jb * 2 + jj
                for kc in range(8):
                    mm(ps[:, b6 + j * NS:b6 + (j + 1) * NS], w[:, kc, jj * 128:(jj + 1) * 128], c.xn[:, kc, :], kc == 0, kc == 7,
                       reads=[("w", slot), K(c, "xn", kc)], writes=BK(6))
                for kc in range(8):
                    mm(ps[:, b7 + j * NS:b7 + (j + 1) * NS], w[:, kc, 256 + jj * 128:256 + (jj + 1) * 128], c.xn[:, kc, :], kc == 0, kc == 7,
                       reads=[("w", slot), K(c, "xn", kc)], writes=BK(7))
            if jb == NJ // 2 - 1:
                sg = s_flat()[:, 0:NJ * NS]
                allu = [("s_u", i) for i in range(6)]
                act(sg, ps[:, b6:b6 + NJ * NS], AF.Silu, reads=BK(6), writes=allu)
                tt(c.hreg[:, 0:NJ, :].rearrange("p j t -> p (j t)"), sg, ps[:, b7:b7 + NJ * NS], ALU.mult, reads=allu + BK(7),
                   writes=[K(c, "h", j) for j in range(NJ)])

        def s_outchunks(c, slot_reads, lhs_of, rhs_of, nk, m, last):
            for k in range(nk):
                mm(ps[:, b6 + m * NS:b6 + (m + 1) * NS], lhs_of(k), rhs_of(k), k == 0, k == nk - 1, reads=slot_reads(k), writes=BK(6))
            if last:
                act(c.y[:, :, :], ps[:, b6:b6 + 8 * NS].rearrange("p (m t) -> p m t", t=NS), AF.Copy, reads=BK(6), writes=[K(c, "y", ch) for ch in range(8)])

        def s_ffn_down(c, slot, m):
            w = wv3(slot, NJ * 128, 128)
            s_outchunks(c, lambda j: [("w", slot), K(c, "h", j)], lambda j: w[:, j, :], lambda j: c.hreg[:, j, :], NJ, m, m == 7)

        def s_outproj(c, slot, i, m2):
            w = wv4(slot)
            s_outchunks(c, lambda m: [("w", slot), K(c, "h", 16 + m)], lambda m: w[:, i, m, :], lambda m: c.hreg[:, 16 + m, :], 8, m2, m2 == 7)

        def s_merge_m(c, slot_g, slot_b, m):
            wg = wv3(slot_g, 8 * 256, 256)
            wb = wv4(slot_b)
            cs0, cs1 = m * NS, (m + 1) * NS
            for kc in range(8):
                mm(ps[:, b6 + cs0:b6 + cs1], wg[:, kc, 0:128], c.xn[:, kc, :], kc == 0, kc == 7, reads=[("w", slot_g), K(c, "xn", kc)], writes=BK(6))
            for n in range(8):
                mm(ps[:, b6 + 128 + cs0:b6 + 128 + cs1], wb[:, 0, n, :], c.hreg[:, n, :], n == 0, n == 7, reads=[("w", slot_b), K(c, "h", n)], writes=BK(6))
            for kc in range(8):
                mm(ps[:, b7 + cs0:b7 + cs1], wg[:, kc, 128:256], c.xn[:, kc, :], kc == 0, kc == 7, reads=[("w", slot_g), K(c, "xn", kc)], writes=BK(7))
            for n in range(8):
                mm(ps[:, b7 + 128 + cs0:b7 + 128 + cs1], wb[:, 1, n, :], c.hreg[:, 8 + n, :], n == 0, n == 7, reads=[("w", slot_b), K(c, "h", 8 + n)], writes=BK(7))
            if m == 7:
                f = s_flat()
                allu = [("s_u", i) for i in range(6)]
                act(f[:, 0:128], ps[:, b6:b6 + 128], AF.Sigmoid, reads=BK(6), writes=allu)
                act(f[:, 128:256], ps[:, b7:b7 + 128], AF.Sigmoid, reads=BK(7), writes=allu)
                tt(f[:, 0:128], f[:, 0:128], ps[:, b6 + 128:b6 + 256], ALU.mult, reads=allu + BK(6), writes=allu)
                tt(f[:, 128:256], f[:, 128:256], ps[:, b7 + 128:b7 + 256], ALU.mult, reads=allu + BK(7), writes=allu)
                tt(c.hreg[:, 16:24, :].rearrange("p m t -> p (m t)"), f[:, 0:128], f[:, 128:256], ALU.add, reads=allu,
                   writes=[K(c, "h", 16 + mm_) for mm_ in range(8)])

        pcol = lambda base, i: par[:, base + i:base + i + 1]

        def Hs(c, k):
            ap = c.hreg[:, 16 + 2 * k:18 + 2 * k, :].rearrange("p a w -> p (a w)").bitcast(F32)
            return ap, [K(c, "h", 16 + 2 * k), K(c, "h", 17 + 2 * k)]

        def rnn_slots(c, p):
            if p == 0:
                d = {"gl": (T(c, 0), K(c, "y", 0)), "r": (T(c, 1), K(c, "y", 1)), "i": (T(c, 2), K(c, "y", 2)),
                     "a": (T(c, 3), K(c, "y", 3)), "m": (T(c, 7), K(c, "y", 7)), "xc": Hs(c, 2)}
            else:
                d = {"gl": (T(c, 4), K(c, "y", 4)), "r": (T(c, 5), K(c, "y", 5)), "i": (T(c, 6), K(c, "y", 6)),
                     "a": Hs(c, 0), "m": Hs(c, 1), "xc": Hs(c, 3)}
            return d

        def rnn_parts(c, slot, n, tile):
            p = n % 2
            R = rnn_slots(c, p)
            xcb_p = c.hreg[:, 8 + p, :]
            xcbk = K(c, "h", 8 + p)
            xrb = c.hreg[:, 10 + 2 * p:12 + 2 * p, :].rearrange("p a w -> p (a w)")
            xk = [K(c, "h", 10 + 2 * p), K(c, "h", 11 + 2 * p)]
            dgp = c.hreg[:, 14 + p, :]
            dgk = K(c, "h", 14 + p)
            st = {}

            def A1():
                w = wv3(slot, 8 * 256, 256)
                sx = nextset(c)
                proj(c, sx, lambda kc: w[:, kc, 0:128], lambda kc: c.xn[:, kc, :], 8, lambda kc: [("w", slot), K(c, "xn", kc)])
                sy = nextset(c)
                proj(c, sy, lambda kc: w[:, kc, 128:256], lambda kc: c.xn[:, kc, :], 8, lambda kc: [("w", slot), K(c, "xn", kc)])
                cp(xrb[:, 0:3], halo_bf[:, n, :], reads=["halo_bf"], writes=[xk])
                cp(xrb[:, 3:TW + 3], c.ps[sx][:, 0:TW], reads=[pskey(c, sx)], writes=[xk])
                cp(halo[:, n, :], c.ps[sx][:, TW - 3:TW], reads=[pskey(c, sx)], writes=["halo"])
                cp(halo_bf[:, n, :], xrb[:, TW:TW + 3], reads=[xk], writes=["halo_bf"])
                idb = cst[:, C_IDENT - 256:C_IDENT - 256 + 128].rearrange("p (o c) -> p o c", o=1).to_broadcast([128, 4, 128])
                cwb = par[:, P_CW + n:P_CW + n + 25:8].rearrange("p (j o) -> p j o", o=1).to_broadcast([128, 4, 128])
                tt(dgp[:, 0:512].rearrange("p (j c) -> p j c", c=128), idb, cwb, ALU.mult, reads=["cst", "par"], writes=[dgk])
                act(R["gl"][0], c.ps[sy][:, 0:TW], AF.Gelu_apprx_tanh, reads=[pskey(c, sy)], writes=[R["gl"][1]])
                sc = nextset(c)
                for j in range(4):
                    for (g0, g1) in c.groups:
                        mm(c.ps[sc][:, g0:g1], dgp[:, j * 128:(j + 1) * 128], xrb[:, j + g0:j + g1], j == 0, j == 3, reads=[dgk, xk], writes=[pskey(c, sc)])
                st["sc"] = sc

            def A2():
                sc = st["sc"]
                act(xcb_p, c.ps[sc][:, 0:TW], AF.Identity, reads=[pskey(c, sc), "par"], writes=[xcbk], bias=pcol(P_CB, n))
                ts(R["xc"][0], c.ps[sc][:, 0:TW], pcol(P_CB, n), None, ALU.add, None, reads=[pskey(c, sc), "par"], writes=[R["xc"][1]])
                s1 = nextset(c)
                for (g0, g1) in c.groups:
                    mm(c.ps[s1][:, g0:g1], rgw[:, 0, n, :], xcb_p[:, g0:g1], True, True, reads=["rgw", xcbk], writes=[pskey(c, s1)])
                s2 = nextset(c)
                for (g0, g1) in c.groups:
                    mm(c.ps[s2][:, g0:g1], rgw[:, 1, n, :], xcb_p[:, g0:g1], True, True, reads=["rgw", xcbk], writes=[pskey(c, s2)])
                act(R["r"][0], c.ps[s1][:, 0:TW], AF.Tanh, reads=[pskey(c, s1), "der"], writes=[R["r"][1]], scale=0.5, bias=der[:, DHBA + n:DHBA + n + 1])
                act(R["i"][0], c.ps[s2][:, 0:TW], AF.Tanh, reads=[pskey(c, s2), "der"], writes=[R["i"][1]], scale=0.5, bias=der[:, DHBX + n:DHBX + n + 1])

            def B_act():
                act(R["a"][0], R["r"][0], AF.Exp, reads=[R["r"][1], "der"], writes=[R["a"][1]], scale=der[:, DHC + n:DHC + n + 1], bias=der[:, DHC + n:DHC + n + 1])
                act(R["m"][0], R["a"][0], AF.Square, reads=[R["a"][1]], writes=[R["m"][1]])
                act(R["m"][0], R["m"][0], AF.Sqrt, reads=[R["m"][1], "der"], writes=[R["m"][1]], scale=-0.25, bias=der[:, DQ25:DQ25 + 1])

            def B_dve():
                if tile == 0:
                    P.add("dve", lambda e: e.memset(R["m"][0][:, 0:1], 0.5), reads=[R["m"][1]], writes=[R["m"][1]])
                stt(R["i"][0], R["i"][0], 1.0, R["m"][0], ALU.add, ALU.mult, reads=[R["i"][1], R["m"][1]], writes=[R["i"][1]])
                tt(R["i"][0], R["i"][0], R["xc"][0], ALU.mult, reads=[R["i"][1], R["xc"][1]], writes=[R["i"][1]])
                P.add("dve", lambda e: e.tensor_tensor_scan(out=R["r"][0], data0=R["a"][0], data1=R["i"][0], initial=hcar[:, n:n + 1], op0=ALU.mult, op1=ALU.add),
                      reads=[R["a"][1], R["i"][1], R["r"][1], "hcar"], writes=[R["r"][1]])
                cp(hcar[:, n:n + 1], R["r"][0][:, TW - 1:TW], reads=[R["r"][1]], writes=["hcar"])
                tt(c.hreg[:, n, :], R["r"][0], R["gl"][0], ALU.mult, reads=[R["r"][1], R["gl"][1]], writes=[K(c, "h", n)])

            return A1, A2, B_act, B_dve

        s_xrp = ps[:, b6 + 128:b6 + 256]
        s_yrp = ps[:, b7 + 128:b7 + 256]

        def rnn_sample_proj(c, slot, n):
            w = wv3(slot, 8 * 256, 256)
            for kc in range(8):
                mm(s_xrp[:, n * NS:(n + 1) * NS], w[:, kc, 0:128], c.xn[:, kc, :], kc == 0, kc == 7, reads=[("w", slot), K(c, "xn", kc)], writes=BK(6))
            for kc in range(8):
                mm(s_yrp[:, n * NS:(n + 1) * NS], w[:, kc, 128:256], c.xn[:, kc, :], kc == 0, kc == 7, reads=[("w", slot), K(c, "xn", kc)], writes=BK(7))

        def rnn_sample_tail(c, part):
            v3 = lambda ap: ap.rearrange("p (n t) -> p n t", t=NS)
            bc = lambda col: par[:, col:col + 8].rearrange("p (n o) -> p n o", o=1).to_broadcast([128, 8, NS])
            dbc = lambda col: der[:, col:col + 8].rearrange("p (n o) -> p n o", o=1).to_broadcast([128, 8, NS])
            U = lambda i: s_u[:, i, :, :]
            uk = lambda i: ("s_u", i)
            if part == "a":
                act(s_xr[:, :, :], v3(s_xrp), AF.Copy, reads=BK(6), writes=["s_xr"])
                act(U(0), v3(s_yrp), AF.Gelu_apprx_tanh, reads=BK(7), writes=[uk(0)])
                tt(U(1), s_c0[:, :, :, 0], bc(P_CW + 0), ALU.mult, reads=["s_c0", "par"], writes=[uk(1)])
                for j in (1, 2):
                    tt(U(2), s_c0[:, :, :, j], bc(P_CW + j * 8), ALU.mult, reads=["s_c0", "par"], writes=[uk(2)])
                    tt(U(1), U(1), U(2), ALU.add, reads=[uk(1), uk(2)], writes=[uk(1)])
                tt(U(2), s_xr[:, :, :], bc(P_CW + 3 * 8), ALU.mult, reads=["s_xr", "par"], writes=[uk(2)])
                tt(U(1), U(1), U(2), ALU.add, reads=[uk(1), uk(2)], writes=[uk(1)])
                tt(U(1), U(1), bc(P_CB), ALU.add, reads=[uk(1), "par"], writes=[uk(1)])
                act(s_xcb3[:, :, :], U(1), AF.Copy, reads=[uk(1)], writes=["s_xcb3"])
                return
            for n in range(8):
                mm(s_xrp[:, n * NS:(n + 1) * NS], rgw[:, 0, n, :], s_xcb3[:, n, :], True, True, reads=["rgw", "s_xcb3"], writes=BK(6))
                mm(s_yrp[:, n * NS:(n + 1) * NS], rgw[:, 1, n, :], s_xcb3[:, n, :], True, True, reads=["rgw", "s_xcb3"], writes=BK(7))
            tt(U(2), v3(s_xrp), bc(P_BA), ALU.add, reads=BK(6) + ["par"], writes=[uk(2)])
            tt(U(3), v3(s_yrp), bc(P_BX), ALU.add, reads=BK(7) + ["par"], writes=[uk(3)])
            act(U(2), U(2), AF.Sigmoid, reads=[uk(2)], writes=[uk(2)])
            act(U(3), U(3), AF.Sigmoid, reads=[uk(3)], writes=[uk(3)])
            tt(U(2), U(2), dbc(DC), ALU.mult, reads=[uk(2), "der"], writes=[uk(2)])
            act(U(4), U(2), AF.Exp, reads=[uk(2)], writes=[uk(4)])
            act(U(5), U(2), AF.Exp, reads=[uk(2)], writes=[uk(5)], scale=2.0)
            act(U(5), U(5), AF.Sqrt, reads=[uk(5)], writes=[uk(5)], scale=-1.0, bias=1.0)
            tt(U(3), U(3), U(5), ALU.mult, reads=[uk(3), uk(5)], writes=[uk(3)])
            tt(U(3), U(3), U(1), ALU.mult, reads=[uk(3), uk(1)], writes=[uk(3)])
            tt(U(4), U(4), s_h0[:, :, :], ALU.mult, reads=[uk(4), "s_h0"], writes=[uk(4)])
            tt(s_hn[:, :, :], U(4), U(3), ALU.add, reads=[uk(4), uk(3)], writes=["s_hn"])
            tt(c.hreg[:, 0:8, :], s_hn[:, :, :], U(0), ALU.mult, reads=["s_hn", uk(0)], writes=[K(c, "h", n) for n in range(8)])

        def gla_lr(c, lrb, lrkey):
            s = nextset(c)
            proj(c, s, lambda kc: wlrin[:, kc, :], lambda kc: c.xn[:, kc, :], 8, lambda kc: ["wlrin", K(c, "xn", kc)], M=16)
            act(lrb, c.ps[s][0:16, 0:c.W], AF.Copy, reads=[pskey(c, s)], writes=[lrkey])

        def gla_decay(c, hd, lrb, lrkey, Tl, kl, s=None):
            s = s or nextset(c)
            for (g0, g1) in c.groups:
                mm(c.ps[s][:, g0:g1], wlr[0:16, hd * 128:(hd + 1) * 128], lrb[0:16, g0:g1], True, True,
                   reads=["wlr", lrkey], writes=[pskey(c, s)])
            act(Tl, c.ps[s][:, 0:c.W], AF.Exp, reads=[pskey(c, s), "der"], writes=[kl], scale=-1.0, bias=der[:, DNB + hd:DNB + hd + 1])
            act(Tl, Tl, AF.Ln, reads=[kl], writes=[kl], bias=1.0)

        def gla_gate_out(c, hd, slot_v, o_aps, o_keys, Trs, krs, Tsg, ksg, Tt, ktt, stat_ps, stat_key, sg_pre=None):
            wv = wv3(slot_v, 8 * 512, 512)
            for (g0, g1) in c.groups:
                for dvc in range(2):
                    i = c.sqi % 2
                    c.sqi += 1
                    act(c.sq[:, i, g0:g1], o_aps[dvc][:, g0:g1], AF.Square, reads=[o_keys[dvc]], writes=[K(c, "sq", i)])
                    mm(stat_ps[:, 0:g1 - g0], ones256, c.sq[:, i, g0:g1], dvc == 0, dvc == 1, reads=[K(c, "sq", i), "cbf"], writes=stat_key)
                act(Trs[:, g0:g1], stat_ps[:, 0:g1 - g0], AF.Ln, reads=stat_key + ["der"], writes=[krs], bias=der[:, DEPS:DEPS + 1], scale=1.0)
            act(Trs, Trs, AF.Exp, reads=[krs], writes=[krs], scale=-0.5)
            for dvc in range(2):
                if sg_pre is not None:
                    Tsg_, ksg_ = sg_pre[dvc]
                else:
                    s = c.gla_og or nextset(c)
                    proj(c, s, lambda kc: wv[:, kc, 256 + dvc * 128:256 + (dvc + 1) * 128], lambda kc: c.xn[:, kc, :], 8,
                         lambda kc: [("w", slot_v), K(c, "xn", kc)])
                    act(Tsg, c.ps[s][:, 0:c.W], AF.Silu, reads=[pskey(c, s)], writes=[ksg])
                    Tsg_, ksg_ = Tsg, ksg
                stt(Tt, o_aps[dvc][:, 0:c.W], pcol(P_GNG, dvc), Trs, ALU.mult, ALU.mult, reads=[o_keys[dvc], "par", krs], writes=[ktt])
                tt(c.hreg[:, 8 + hd * 2 + dvc, :], Tt, Tsg_, ALU.mult, reads=[ktt, ksg_], writes=[K(c, "h", 8 + hd * 2 + dvc)])

        def gla_prompt(c, slot_qk, slot_v, hd, tile):
            wq = wv3(slot_qk, 8 * 256, 256)
            wv = wv3(slot_v, 8 * 512, 512)
            yk = lambda i: K(c, "y", i)
            gla_decay(c, hd, lr_bf, "lr_bf", T(c, 0), yk(0), s="C")
            P.add("dve", lambda e: e.tensor_tensor_scan(out=T(c, 1), data0=cst[:, C_RMASK - 256:C_RMASK - 256 + TW], data1=T(c, 0), initial=0.0, op0=ALU.mult, op1=ALU.add),
                  reads=[yk(0), "cst"], writes=[yk(1)])
            act(T(c, 2), T(c, 1), AF.Exp, reads=[yk(1)], writes=[yk(2)], scale=-1.0 / 16.0)
            act(T(c, 3), T(c, 1), AF.Exp, reads=[yk(1)], writes=[yk(3)], scale=1.0 / 16.0)
            qi = c.y[:, 4, :].bitcast(BF16)[:, 0:TW]
            ki = c.y[:, 4, :].bitcast(BF16)[:, TW:2 * TW]
            attT = c.y[:, 5, :].bitcast(BF16)[:, 0:TW]
            kiT = c.y[:, 5, :].bitcast(BF16)[:, TW:2 * TW]
            v_bf = c.y[:, 6, :].bitcast(BF16).rearrange("p (b v) -> p b v", v=DV)
            def vhalf(hf, sv):
                for bb in range(4):
                    bk = hf * 4 + bb
                    for kc in range(8):
                        mm(c.ps[sv][:, bb * DV:(bb + 1) * DV], c.xn[:, kc, bk * 128:(bk + 1) * 128], wv[:, kc, 0:DV], kc == 0, kc == 7,
                           reads=[("w", slot_v), K(c, "xn", kc)], writes=[pskey(c, sv)])
                act(c.y[:, 6, :].bitcast(BF16)[:, hf * TW:(hf + 1) * TW], c.ps[sv][:, 0:TW], AF.Copy, reads=[pskey(c, sv)], writes=[(c.name, "y", 6, hf)])
            sq_ = "A"
            proj(c, sq_, lambda kc: wq[:, kc, 0:128], lambda kc: c.xn[:, kc, :], 8, lambda kc: [("w", slot_qk), K(c, "xn", kc)])
            sk_ = "B"
            proj(c, sk_, lambda kc: wq[:, kc, 128:256], lambda kc: c.xn[:, kc, :], 8, lambda kc: [("w", slot_qk), K(c, "xn", kc)])
            stt(qi, c.ps[sq_][:, 0:TW], float(DK) ** -0.5, T(c, 2), ALU.mult, ALU.mult, reads=[pskey(c, sq_), yk(2)], writes=[(c.name, "y", 4, 0)])
            tt(ki, c.ps[sk_][:, 0:TW], T(c, 3), ALU.mult, reads=[pskey(c, sk_), yk(3)], writes=[(c.name, "y", 4, 1)])
            vhalf(0, "C")
            sa = "A"
            for bk in range(8):
                b0, b1 = bk * 128, (bk + 1) * 128
                mm(c.ps[sa][:, b0:b1], ki[:, b0:b1], qi[:, b0:b1], True, True, reads=[(c.name, "y", 4, 1), (c.name, "y", 4, 0)], writes=[pskey(c, sa)])
            maskb = cst[:, C_MASK - 256:C_MASK - 256 + 128].rearrange("p (o c) -> p o c", o=1).to_broadcast([128, 8, 128])
            tt(attT.rearrange("p (b c) -> p b c", c=128), c.ps[sa][:, 0:TW].rearrange("p (b c) -> p b c", c=128), maskb, ALU.mult,
               reads=[pskey(c, sa), "cst"], writes=[(c.name, "y", 5, 0)])
            vhalf(1, "B")
            psMb = c.psM.bitcast(BF16)
            for bk in range(8):
                b0, b1 = bk * 128, (bk + 1) * 128
                P.add("pe", lambda e, b0=b0, b1=b1: e.transpose(psMb[:, b0:b1], ki[:, b0:b1], identb),
                      reads=[(c.name, "y", 4, 1), "cbf"], writes=c.keyM)
            act(kiT, psMb[:, 0:TW], AF.Copy, reads=c.keyM, writes=[(c.name, "y", 5, 1)])
            osets = ["A", "B"]
            dSb = ps[:, 2048:2560]
            for bk in range(8):
                b0, b1 = bk * 128, (bk + 1) * 128
                sbi = bk % 2
                for dvc in range(2):
                    mm(c.ps[osets[dvc]][:, b0:b1], v_bf[:, bk, dvc * 128:(dvc + 1) * 128], attT[:, b0:b1], True, False,
                       reads=[K(c, "y", 6), (c.name, "y", 5, 0)], writes=[pskey(c, osets[dvc])])
                    mm(c.ps[osets[dvc]][:, b0:b1], S_b[:, sbi, dvc * 128:(dvc + 1) * 128], qi[:, b0:b1], False, True,
                       reads=[("S_b", sbi), (c.name, "y", 4, 0)], writes=[pskey(c, osets[dvc])])
                mk = BK(4)
                dS = dSb[:, (bk % 2) * 256:(bk % 2) * 256 + 256]
                mm(dS, kiT[:, b0:b1], v_bf[:, bk, :], True, True, reads=[(c.name, "y", 5, 1), K(c, "y", 6)], writes=mk)
                tt(S_t[:, :], S_f[:, hd, :], dS, ALU.add, reads=[("S_f", hd)] + mk, writes=["S_t"])
                eql = T(c, 2)[:, b1 - 1:b1]
                ts(S_b[:, 1 - sbi, :], S_t[:, :], eql, None, ALU.mult, None, reads=["S_t", yk(2)], writes=[("S_b", 1 - sbi)])
                ts(S_f[:, hd, :], S_t[:, :], eql, None, ALU.mult, None, reads=["S_t", yk(2)], writes=[("S_f", hd)])
                pi = bk // 2
                dvc, hf = pi // 2, pi % 2
                sgT, sgk = (T(c, 0), yk(0)) if dvc == 0 else (T(c, 3), yk(3))
                for kc in range(4 * (bk % 2), 4 * (bk % 2) + 4):
                    mm(c.psM[:, 0:512], wv[:, kc, 256 + dvc * 128:256 + (dvc + 1) * 128], c.xn[:, kc, hf * 512:(hf + 1) * 512], kc == 0, kc == 7,
                       reads=[("w", slot_v), K(c, "xn", kc)], writes=c.keyM)
                if bk % 2 == 1:
                    act(sgT[:, hf * 512:(hf + 1) * 512], c.psM[:, 0:512], AF.Silu, reads=c.keyM, writes=[sgk])
            c.gla_og = "C"
            gla_gate_out(c, hd, slot_v, [c.ps["A"], c.ps["B"]], [pskey(c, "A"), pskey(c, "B")], T(c, 7), yk(7), None, None, T(c, 1), yk(1),
                         c.psM, c.keyM, sg_pre=[(T(c, 0), yk(0)), (T(c, 3), yk(3))])

        s_state = {"loaded": set(), "q": 0}

        def s_piece_load(q):
            if q >= 16 or q in s_state["loaded"]:
                return
            s_state["loaded"].add(q)
            hd, p = q // 4, q % 4
            i = q % 3
            P.add("sp", lambda e: e.dma_start(out=s_S[:, i, :, :], in_=d_s0[p * 4:(p + 1) * 4, hd, :, :].rearrange("b k v -> k b v")),
                  writes=[("s_S", i)], dma=("s_S_in", i))

        def gla_sample(c, slot_qk, slot_v, hd):
            wq = wv3(slot_qk, 8 * 256, 256)
            wv = wv3(slot_v, 8 * 512, 512)
            tm = lambda i: s_t[:, i, :]
            tk = lambda i: ("s_t", i)
            s_piece_load(hd * 4)
            s_piece_load(hd * 4 + 1)
            gla_decay(c, hd, s_lr, "s_lr", tm(0), tk(0))
            act(s_eg[:, :], tm(0), AF.Exp, reads=[tk(0)], writes=["s_eg"], scale=-1.0 / 16.0)
            sq_ = nextset(c)
            proj(c, sq_, lambda kc: wq[:, kc, 0:128], lambda kc: c.xn[:, kc, :], 8, lambda kc: [("w", slot_qk), K(c, "xn", kc)])
            ts(s_q[:, :], c.ps[sq_][:, 0:NS], float(DK) ** -0.5, None, ALU.mult, None, reads=[pskey(c, sq_)], writes=["s_q"])
            tt(s_qgb[:, :], s_q[:, :], s_eg[:, :], ALU.mult, reads=["s_q", "s_eg"], writes=["s_qgb"])
            sk_ = nextset(c)
            proj(c, sk_, lambda kc: wq[:, kc, 128:256], lambda kc: c.xn[:, kc, :], 8, lambda kc: [("w", slot_qk), K(c, "xn", kc)])
            tt(tm(1), c.ps[sk_][:, 0:NS], s_q[:, :], ALU.mult, reads=[pskey(c, sk_), "s_q"], writes=[tk(1)])
            qk_ps = ps[:, b7 + 48:b7 + 64]
            cp(s_pb[:, 0, :], tm(1), reads=[tk(1)], writes=["s_pb"])
            tt(s_pb[:, 1, :], tm(1), s_pb[:, 0, :], ALU.subtract, reads=[tk(1), "s_pb"], writes=["s_pb"])
            mm(qk_ps, ones1024, s_pb[:, 0, :], True, False, reads=["cbf", "s_pb"], writes=BK(7))
            mm(qk_ps, ones1024, s_pb[:, 1, :], False, True, reads=["cbf", "s_pb"], writes=BK(7))
            ts(tm(2), qk_ps, 1024.0, None, ALU.mult, None, reads=BK(7), writes=[tk(2)])
            for dvc in range(2):
                sv = nextset(c)
                proj(c, sv, lambda kc: wv[:, kc, dvc * 128:(dvc + 1) * 128], lambda kc: c.xn[:, kc, :], 8, lambda kc: [("w", slot_v), K(c, "xn", kc)])
                tt(s_vq[:, dvc, :], c.ps[sv][:, 0:NS], tm(2), ALU.mult, reads=[pskey(c, sv), tk(2)], writes=[("s_vq", dvc)])
            bor = cp_.ps["C"]
            bk_ = cp_.banks["C"]
            for kc in range(8):
                mm(bor[0:NS, 0:128], c.xn[:, kc, :], wq[:, kc, 128:256], kc == 0, kc == 7, reads=[("w", slot_qk), K(c, "xn", kc)], writes=bk_)
            act(s_ktm[:, :], bor[0:NS, 0:128], AF.Copy, reads=bk_, writes=["s_ktm"])
            for kc in range(8):
                mm(bor[0:NS, 128:128 + DV], c.xn[:, kc, :], wv[:, kc, 0:DV], kc == 0, kc == 7, reads=[("w", slot_v), K(c, "xn", kc)], writes=bk_)
            act(s_vtm[:, :], bor[0:NS, 128:128 + DV], AF.Copy, reads=bk_, writes=["s_vtm"])
            so = [bor[:, 512:512 + NS], bor[:, 512 + NS:512 + 2 * NS]]
            dSp = [ps[:, b6 + 256:b6 + 512], ps[:, b7 + 256:b7 + 512]]
            dSk = [BK(6), BK(7)]
            for p in range(4):
                q = hd * 4 + p
                i = q % 3
                s_piece_load(q + 2)
                act(s_Sb[:, 0, :, :], s_S[:, i, :, :], AF.Copy, reads=[("s_S", i)], writes=[("s_Sb", 0)])
                for bb in range(4):
                    b = p * 4 + bb
                    for dvc in range(2):
                        mm(so[dvc][:, b:b + 1], s_Sb[:, 0, bb, dvc * 128:(dvc + 1) * 128], s_qgb[:, b:b + 1], True, True,
                           reads=[("s_Sb", 0), "s_qgb"], writes=bk_)
                ktb = s_ktm[:, :].rearrange("t (o k) -> t o k", o=1).to_broadcast([NS, 4, 128])
                dlt = cst[0:NS, C_IDENT - 256 + p * 4:C_IDENT - 256 + p * 4 + 4].rearrange("t (b o) -> t b o", o=1).to_broadcast([NS, 4, 128])
                tt(s_km[:, 0, :, :], ktb, dlt, ALU.mult, reads=["s_ktm", "cst"], writes=[("s_km", 0)])
                for bb in range(4):
                    b = p * 4 + bb
                    kmi = b % 2
                    mm(dSp[kmi], s_km[:, 0, bb, :], s_vtm[:, :], True, True, reads=[("s_km", 0), "s_vtm"], writes=dSk[kmi])
                    stt(s_S[:, i, bb, :], s_S[:, i, bb, :], s_eg[:, b:b + 1], dSp[kmi], ALU.mult, ALU.add,
                        reads=[("s_S", i), "s_eg"] + dSk[kmi], writes=[("s_S", i)])
                P.add("sp", lambda e, p=p, i=i: e.dma_start(out=o_ss[p * 4:(p + 1) * 4, hd, :, :].rearrange("b k v -> k b v"), in_=s_S[:, i, :, :]),
                      reads=[("s_S", i)], dma=("s_S_out", i))
            for dvc in range(2):
                tt(s_o[:, dvc, :], so[dvc], s_vq[:, dvc, :], ALU.add, reads=bk_ + [("s_vq", dvc)], writes=[("s_o", dvc)])
            c.gla_og = None
            gla_gate_out(c, hd, slot_v, [s_o[:, 0, :], s_o[:, 1, :]], [("s_o", 0), ("s_o", 1)], tm(7), tk(7), tm(8), tk(8), tm(9), tk(9),
                         c.psM, c.keyM)

        cp_.gla_avoid = ()
        cs_.gla_avoid = ()

        P.add("dve", lambda e: e.memset(halo[:, :, :], 0.0), writes=["halo"])
        P.add("dve", lambda e: e.memset(halo_bf[:, :, :], 0.0), writes=["halo_bf"])
        P.add("dve", lambda e: e.memset(hcar[:, :], 0.0), writes=["hcar"])
        P.add("dve", lambda e: e.memset(S_f[:, :, :], 0.0), writes=[("S_f", h) for h in range(NH)])
        P.add("sp", lambda e: e.dma_start(out=cs_.x[:, :, :], in_=d_xs), writes=[K(cs_, "x", ch) for ch in range(8)], dma="s_x_in")
        P.add("sp", lambda e: e.dma_start(out=s_h0[:, :, :], in_=d_h0), writes=["s_h0"], dma="s_h0")
        P.add("sp", lambda e: e.dma_start(out=s_c0[:, :, :, :], in_=d_c0), writes=["s_c0"], dma="s_c0")

        def load_x(tile, ch, eng="sp"):
            t0 = tile * TW
            P.add(eng, lambda e: e.dma_start(out=cp_.x[:, ch, :], in_=d_x[:, ch, t0:t0 + TW]), writes=[K(cp_, "x", ch)], dma=("xin_" + eng, ch))

        def store_y(tile, ch):
            t0 = tile * TW
            P.add("sp", lambda e: e.dma_start(out=o_y[:, ch, t0:t0 + TW], in_=cp_.x[:, ch, :]), reads=[K(cp_, "x", ch)], dma=("yout", ch))

        for ch in range(8):
            load_x(0, ch, "sp" if ch < 4 else "pool")
        deferred = []

        def run_deferred():
            while deferred:
                deferred.pop(0)()

        for tile in range(NT):
            ctxs = [cp_] + ([cs_] if tile == NT - 1 else [])
            bi = 0

            def ffn_body(gi_pre, bi):
                P.tag = "t%d:ffn%d_up" % (tile, gi_pre)
                for jb in range(NJ // 2):
                    slot = load_block(tile, bi); bi += 1
                    for c in ctxs:
                        if c is cs_:
                            run_deferred()
                        (s_ffn_up if c is cs_ else ffn_up)(c, slot, jb)
                P.tag = "t%d:ffn%d_dn" % (tile, gi_pre)
                for m in range(8):
                    slot = load_block(tile, bi); bi += 1
                    for c in ctxs:
                        (s_ffn_down if c is cs_ else ffn_down)(c, slot, m)
                return bi

            P.tag = "t%d:prenorm0" % tile
            for c in ctxs:
                if c is cs_:
                    deferred.append(lambda c=c: prenorm(c, 0))
                else:
                    prenorm(c, 0)
            bi = ffn_body(0, bi)
            P.tag = "t%d:epi1" % tile
            for c in ctxs:
                if c is cs_:
                    deferred.append(lambda c=c: epilogue_prenorm(c, 1, True, 2))
                else:
                    epilogue_prenorm(c, 1, True, 2)
            P.tag = "t%d:lr" % tile
            gla_lr(cp_, lr_bf[:, :], "lr_bf")
            if cs_ in ctxs:
                deferred.append(lambda: gla_lr(cs_, s_lr[:, :], "s_lr"))
            P.tag = "t%d:rnn" % tile
            prev = None
            for n in range(8):
                slot = load_block(tile, bi); bi += 1
                A1, A2, Ba, Bd = rnn_parts(cp_, slot, n, tile)
                if prev is not None:
                    prev[0]()
                A1()
                if cs_ in ctxs:
                    run_deferred()
                    rnn_sample_proj(cs_, slot, n)
                A2()
                if prev is not None:
                    prev[1]()
                prev = (Ba, Bd)
            prev[0]()
            prev[1]()
            if cs_ in ctxs:
                rnn_sample_tail(cs_, "a")
            P.tag = "t%d:gla" % tile
            for hd in range(NH):
                P.add("dve", lambda e: e.memset(S_b[:, 0, :], 0.0), writes=[("S_b", 0)]) if tile == 0 else None
                if tile > 0:
                    cp(S_b[:, 0, :], S_f[:, hd, :], reads=[("S_f", hd)], writes=[("S_b", 0)])
                slot_qk = load_block(tile, bi); bi += 1
                slot_v = load_block(tile, bi); bi += 1
                gla_prompt(cp_, slot_qk, slot_v, hd, tile)
                if cs_ in ctxs and hd == 0:
                    rnn_sample_tail(cs_, "b")
                if cs_ in ctxs:
                    gla_sample(cs_, slot_qk, slot_v, hd)
            P.tag = "t%d:merge" % tile
            for m in range(8):
                slot_g = load_block(tile, bi); bi += 1
                slot_b = load_block(tile, bi); bi += 1
                for c in ctxs:
                    (s_merge_m if c is cs_ else merge_m)(c, slot_g, slot_b, m)
            P.tag = "t%d:outproj" % tile
            for mp in range(4):
                slot = load_block(tile, bi); bi += 1
                for i in range(2):
                    for c in ctxs:
                        (s_outproj if c is cs_ else outproj)(c, slot, i, mp * 2 + i)
            P.tag = "t%d:epi3" % tile
            for c in ctxs:
                if c is cs_:
                    deferred.append(lambda c=c: epilogue_prenorm(c, 3, False, 4))
                else:
                    epilogue_prenorm(c, 3, False, 4)
            bi = ffn_body(4, bi)
            assert bi == len(PLAN)
            P.tag = "t%d:epi5" % tile
            pend_load = []

            def after5(ch):
                store_y(tile, ch)
                if tile + 1 < NT:
                    pend_load.append(ch)
                    if len(pend_load) > 1:
                        load_x(tile + 1, pend_load.pop(0))

            for c in ctxs:
                epilogue(c, 5, True, after_chunk=after5 if c is cp_ else None)
            while pend_load:
                load_x(tile + 1, pend_load.pop(0))

        P.add("sp", lambda e: e.dma_start(out=o_ys, in_=cs_.x[:, :, :]), reads=[K(cs_, "x", ch) for ch in range(8)], dma="o_ys")
        P.add("sp", lambda e: e.dma_start(out=o_hp, in_=hcar[:, :]), reads=["hcar"], dma="o_hp")
        P.add("sp", lambda e: e.dma_start(out=o_cp, in_=halo[:, :, :]), reads=["halo"], dma="o_cp")
        P.add("sp", lambda e: e.dma_start(out=o_sp, in_=S_f[:, :, :]), reads=[("S_f", h) for h in range(NH)], dma="o_sp")
        P.add("sp", lambda e: e.dma_start(out=o_hs, in_=s_hn[:, :, :]), reads=["s_hn"], dma="o_hs")
        cp(s_cn[:, :, :, 0:2], s_c0[:, :, :, 1:3], reads=["s_c0"], writes=["s_cn"])
        cp(s_cn[:, :, :, 2], s_xr[:, :, :], reads=["s_xr", "s_cn"], writes=["s_cn"])
        P.add("sp", lambda e: e.dma_start(out=o_cs, in_=s_cn[:, :, :, :]), reads=["s_cn"], dma="o_cs")
        if debug:
            for name in debug:
                src, keys = DEBUG_SRC[name](locals())
                P.add("sp", lambda e, name=name, src=src: e.dma_start(out=dbg_out[name], in_=src), reads=keys, dma="dbg_" + name)
        P.emit()
    TAGMAP.clear()
    TAGMAP.update(P.tagmap)
    return nc


DEBUG_SRC = {}
TAGMAP = {}
_NC_CACHE = {}


def _prep_inputs(inp):
    inp = {k: np.asarray(v) for k, v in inp.items()}
    ws = _pack_weights(inp)
    par = _pack_params(inp)
    cst = _consts()
    w_in = inp["w_in"][0]
    wlrin = _kc(w_in, np.arange(OFF_LR, OFF_LR + 16))
    rgw = np.ascontiguousarray(np.stack([inp["rg_w_a"][0].transpose(1, 0, 2), inp["rg_w_x"][0].transpose(1, 0, 2)], axis=1)).astype(np.float32)
    wlr = np.ascontiguousarray(inp["gla_w_lr"][0])
    maps = []
    for c in range(NCORES):
        x = inp["x_prompt"][c]
        xT = np.ascontiguousarray(x.reshape(SEQ, 8, 128).transpose(2, 1, 0))
        sl = slice(c * NS, (c + 1) * NS)
        xs = inp["x_sample"][sl, 0, :]
        xsT = np.ascontiguousarray(xs.reshape(NS, 8, 128).transpose(2, 1, 0))
        h0 = np.ascontiguousarray(inp["state_rnn_h"][0, sl].reshape(NS, 8, 128).transpose(2, 1, 0))
        c0 = np.ascontiguousarray(inp["state_rnn_conv"][0, sl].reshape(NS, 3, 8, 128).transpose(3, 2, 0, 1))
        s0 = np.ascontiguousarray(inp["state_gla"][0, sl])
        maps.append({"xT": xT, "xsT": xsT, "h0": h0, "c0": c0, "s0": s0, "ws": ws, "par": par, "cst": cst,
                     "wlrin": wlrin, "rgw": rgw, "wlr": wlr})
    return maps


def _assemble(results):
    yp = np.empty((NCORES, SEQ, D), np.float32)
    ys = np.empty((NCORES * NS, 1, D), np.float32)
    hp = np.empty((1, NCORES, D), np.float32)
    cpo = np.empty((1, NCORES, 3, D), np.float32)
    spo = np.empty((1, NCORES, NH, 128, DV), np.float32)
    hs = np.empty((1, NCORES * NS, D), np.float32)
    cso = np.empty((1, NCORES * NS, 3, D), np.float32)
    sso = np.empty((1, NCORES * NS, NH, 128, DV), np.float32)
    for c, r in enumerate(results):
        sl = slice(c * NS, (c + 1) * NS)
        yp[c] = np.asarray(r["yT"]).transpose(2, 1, 0).reshape(SEQ, D)
        ys[sl, 0] = np.asarray(r["ysT"]).transpose(2, 1, 0).reshape(NS, D)
        hp[0, c] = np.asarray(r["hp"]).T.reshape(D)
        cpo[0, c] = np.asarray(r["cp"]).transpose(2, 1, 0).reshape(3, D)
        spo[0, c] = np.asarray(r["sp"]).transpose(1, 0, 2)
        hs[0, sl] = np.asarray(r["hs"]).transpose(2, 1, 0).reshape(NS, D)
        cso[0, sl] = np.asarray(r["cs"]).transpose(2, 3, 1, 0).reshape(NS, 3, D)
        sso[0, sl] = np.asarray(r["ss"])
    return (yp, ys, hp, cpo, spo, hs, cso, sso)


def kernel(**inputs):
    maps = _prep_inputs(inputs)
    if "nc" not in _NC_CACHE:
        _NC_CACHE["nc"] = build_program()
    nc = _NC_CACHE["nc"]
    res = run_bass_kernel_spmd(nc, maps, core_ids=list(range(NCORES)))
    return _assemble(res.results)
```

```python
import contextlib
import numpy as np
import concourse.bass as bass
import concourse.mybir as mybir
from concourse.bass_utils import run_bass_kernel_spmd

F32 = mybir.dt.float32
BF16 = mybir.dt.bfloat16
AF = mybir.ActivationFunctionType
ALU = mybir.AluOpType

NCORES = 8
D = 1024
SEQ = 2048
TW = 1024
NT = SEQ // TW
NS = 16
DFF = 2816
NJ = DFF // 128
EPS = 1e-6
DK = 128
DV = 256
NH = 4
OFF_XR, OFF_YR, OFF_Q, OFF_K, OFF_V, OFF_OG, OFF_LR, OFF_GA, OFF_GB = 0, 1024, 2048, 2560, 3072, 4096, 5120, 5136, 6160
SLOT = 4096
NSLOT = 3

P_GAIN = 0
P_CW = 48
P_CB = 80
P_BA = 88
P_BX = 96
P_LAM = 104
P_BLR = 112
P_GNG = 116
NPAR = 118
C_ONES1024 = 0
C_ONES256 = 128
C_IDENT = 256
C_MASK = 384
C_RMASK = 512
NCST = 1536

ENGS = ("pe", "act", "dve", "pool", "sp")


class _Op:
    __slots__ = ("eng", "idx", "fn", "deps", "dma", "sig", "sigval", "tag")

    def __init__(self, eng, idx, fn, dma):
        self.eng, self.idx, self.fn, self.dma = eng, idx, fn, dma
        self.tag = ""
        self.deps = {}
        self.sig = False
        self.sigval = None


class Prog:
    def __init__(self, nc):
        self.nc = nc
        self.ops = {e: [] for e in ENGS}
        self.last_write = {}
        self.readers = {}
        self.tag = ""
        self.tagmap = {}

    @staticmethod
    def _flat(keys):
        out = []
        for k in keys:
            if isinstance(k, list):
                out.extend(Prog._flat(k))
            else:
                out.append(k)
        return out

    def add(self, eng, fn, reads=(), writes=(), dma=None):
        op = _Op(eng, len(self.ops[eng]), fn, dma)
        op.tag = self.tag
        reads = self._flat(reads)
        writes = self._flat(writes)
        bk = [k for k in reads if isinstance(k, tuple) and k and k[0] == "bank"]
        if bk:
            reads = [k for k in reads if k not in bk]
            writes = writes + [k for k in bk if k not in writes]

        def dep(o):
            if o is None:
                return
            if o.dma is None and o.eng == "pe" and eng == "pe":
                return
            k = ("d", o.dma) if o.dma is not None else ("e", o.eng)
            cur = op.deps.get(k)
            if cur is None or o.idx > cur.idx:
                op.deps[k] = o

        for k in reads:
            dep(self.last_write.get(k))
        for k in writes:
            dep(self.last_write.get(k))
            for r in self.readers.get(k, ()):
                dep(r)
        for k in reads:
            self.readers.setdefault(k, []).append(op)
        for k in writes:
            self.last_write[k] = op
            self.readers[k] = []
        self.ops[eng].append(op)
        return op

    def emit(self, final_wait_eng="sp"):
        nc = self.nc
        for e in ENGS:
            for op in self.ops[e]:
                for d in op.deps.values():
                    d.sig = True
        for e in ENGS:
            for op in self.ops[e]:
                if op.dma is not None:
                    op.sig = True
        cnt = {}
        for e in ENGS:
            for op in self.ops[e]:
                if not op.sig:
                    continue
                k = ("d", op.dma) if op.dma is not None else ("e", op.eng)
                cnt[k] = cnt.get(k, 0) + (16 if op.dma is not None else 1)
                op.sigval = cnt[k]
        final = dict(cnt)
        with contextlib.ExitStack() as st:
            sems = {}
            for i, k in enumerate(cnt):
                sems[k] = st.enter_context(nc.semaphore("sem%d" % i))
            block = st.enter_context(nc.Block())
            engobj = {"pe": "tensor", "act": "scalar", "dve": "vector", "pool": "gpsimd", "sp": "sync"}

            def mk(e):
                def body(eng):
                    known = {}
                    for op in self.ops[e]:
                        for k, d in op.deps.items():
                            if known.get(k, 0) < d.sigval:
                                eng.wait_ge(sems[k], d.sigval)
                                known[k] = d.sigval
                        ins = op.fn(eng)
                        try:
                            self.tagmap[ins.ins.name] = op.tag
                        except Exception:
                            pass
                        if op.sig:
                            k = ("d", op.dma) if op.dma is not None else ("e", op.eng)
                            ins.then_inc(sems[k], 16 if op.dma is not None else 1)
                    if e == final_wait_eng:
                        for k, v in final.items():
                            if known.get(k, 0) < v:
                                eng.wait_ge(sems[k], v)
                return body

            for e in ENGS:
                if self.ops[e] or e == final_wait_eng:
                    getattr(block, engobj[e])(mk(e))


def _stream_plan():
    plan = []
    for f in (1,):
        pass
    def ffn(tag):
        out = []
        for jb in range(NJ // 2):
            out.append((tag + "gu", jb, 8 * 512))
        for m in range(8):
            out.append((tag + "dn", m, NJ * 128))
        return out
    plan += ffn("f1")
    for n in range(8):
        plan.append(("rnn", n, 8 * 256))
    for hd in range(NH):
        plan.append(("qk", hd, 8 * 256))
        plan.append(("vog", hd, 8 * 512))
    for m in range(8):
        plan.append(("gate", m, 8 * 256))
        plan.append(("br", m, 2 * 8 * 128))
    for mp in range(4):
        plan.append(("wout", mp, 2 * 8 * 128))
    plan += ffn("f2")
    offs = []
    o = 0
    for (_, _, n) in plan:
        offs.append(o)
        o += n
    return plan, offs, o


PLAN, PLAN_OFFS, WLEN = _stream_plan()


def _kc(w, cols):
    return np.ascontiguousarray(w[:, cols].reshape(8, 128, -1).transpose(1, 0, 2))


def _pack_weights(inp):
    wgu = {"f1": inp["ffn1_w_gu"][0], "f2": inp["ffn2_w_gu"][0]}
    wdn = {"f1": inp["ffn1_w_down"][0], "f2": inp["ffn2_w_down"][0]}
    w_in = inp["w_in"][0]
    wbr = inp["w_branch_rnn"][0]
    wbg = inp["w_branch_gla"][0]
    wout = inp["w_out"][0]
    ws = np.empty((128, WLEN), np.float32)
    ar = np.arange
    for (name, i, n), off in zip(PLAN, PLAN_OFFS):
        if name.endswith("gu"):
            w = wgu[name[:2]]
            cols = np.concatenate([ar(i * 256, i * 256 + 256), ar(DFF + i * 256, DFF + i * 256 + 256)])
            blk = _kc(w, cols)
        elif name.endswith("dn"):
            w = wdn[name[:2]]
            blk = w[:, i * 128:(i + 1) * 128].reshape(NJ, 128, 128).transpose(1, 0, 2)
        elif name == "rnn":
            cols = np.concatenate([ar(OFF_XR + i * 128, OFF_XR + i * 128 + 128), ar(OFF_YR + i * 128, OFF_YR + i * 128 + 128)])
            blk = _kc(w_in, cols)
        elif name == "qk":
            cols = np.concatenate([ar(OFF_Q + i * 128, OFF_Q + i * 128 + 128), ar(OFF_K + i * 128, OFF_K + i * 128 + 128)])
            blk = _kc(w_in, cols)
        elif name == "vog":
            cols = np.concatenate([ar(OFF_V + i * 256, OFF_V + i * 256 + 256), ar(OFF_OG + i * 256, OFF_OG + i * 256 + 256)])
            blk = _kc(w_in, cols)
        elif name == "gate":
            cols = np.concatenate([ar(OFF_GA + i * 128, OFF_GA + i * 128 + 128), ar(OFF_GB + i * 128, OFF_GB + i * 128 + 128)])
            blk = _kc(w_in, cols)
        elif name == "br":
            a = _kc(wbr, ar(i * 128, i * 128 + 128))
            b = _kc(wbg, ar(i * 128, i * 128 + 128))
            blk = np.stack([a, b], axis=1)
        elif name == "wout":
            a = _kc(wout, ar((2 * i) * 128, (2 * i) * 128 + 128))
            b = _kc(wout, ar((2 * i + 1) * 128, (2 * i + 1) * 128 + 128))
            blk = np.stack([a, b], axis=1)
        else:
            raise AssertionError(name)
        ws[:, off:off + n] = np.asarray(blk).reshape(128, n)
    return ws


def _fm(v):
    return np.ascontiguousarray(np.asarray(v).reshape(8, 128).T)


def _pack_params(inp):
    p = np.zeros((128, NPAR), np.float32)
    g = inp["norm_gains"][0]
    for i in range(6):
        p[:, P_GAIN + i * 8:P_GAIN + i * 8 + 8] = _fm(g[i])
    cw = inp["conv_w"][0]
    for j in range(4):
        p[:, P_CW + j * 8:P_CW + j * 8 + 8] = _fm(cw[j])
    p[:, P_CB:P_CB + 8] = _fm(inp["conv_b"][0])
    p[:, P_BA:P_BA + 8] = _fm(inp["rg_b_a"][0])
    p[:, P_BX:P_BX + 8] = _fm(inp["rg_b_x"][0])
    p[:, P_LAM:P_LAM + 8] = _fm(inp["rg_lambda"][0])
    p[:, P_BLR:P_BLR + 4] = np.asarray(inp["gla_b_lr"][0]).reshape(4, 128).T
    p[:, P_GNG:P_GNG + 2] = np.asarray(inp["gla_norm_g"][0]).reshape(2, 128).T
    return p


def _consts():
    c = np.zeros((128, NCST), np.float32)
    c[:, C_ONES1024:C_ONES1024 + 128] = 1.0 / 1024.0
    c[:, C_ONES256:C_ONES256 + 128] = 1.0 / 256.0
    c[:, C_IDENT:C_IDENT + 128] = np.eye(128, dtype=np.float32)
    s = np.arange(128)[:, None]
    cc = np.arange(128)[None, :]
    c[:, C_MASK:C_MASK + 128] = (s <= cc).astype(np.float32)
    rm = np.ones((1024,), np.float32)
    rm[::128] = 0.0
    c[:, C_RMASK:C_RMASK + 1024] = rm[None, :]
    return c


class Ctx:
    pass


def build_program(debug=None):
    nc = bass.Bass("TRN2", target_bir_lowering=False)
    di = lambda name, shape: nc.dram_tensor(name, shape, F32, kind="ExternalInput").ap()
    do = lambda name, shape: nc.dram_tensor(name, shape, F32, kind="ExternalOutput").ap()
    d_x = di("xT", [128, 8, SEQ])
    d_xs = di("xsT", [128, 8, NS])
    d_h0 = di("h0", [128, 8, NS])
    d_c0 = di("c0", [128, 8, NS, 3])
    d_s0 = di("s0", [NS, NH, 128, DV])
    d_ws = di("ws", [128, WLEN])
    d_par = di("par", [128, NPAR])
    d_cst = di("cst", [128, NCST])
    d_wlrin = di("wlrin", [128, 8, 16])
    d_rgw = di("rgw", [128, 2, 8, 128])
    d_wlr = di("wlr", [16, 512])
    o_y = do("yT", [128, 8, SEQ])
    o_ys = do("ysT", [128, 8, NS])
    o_hp = do("hp", [128, 8])
    o_cp = do("cp", [128, 8, 3])
    o_sp = do("sp", [128, NH, DV])
    o_hs = do("hs", [128, 8, NS])
    o_cs = do("cs", [128, 8, NS, 3])
    o_ss = do("ss", [NS, NH, 128, DV])
    dbg_out = {}
    if debug:
        for name, shape in debug.items():
            dbg_out[name] = do("dbg_" + name, shape)

    with contextlib.ExitStack() as st:
        def sb(name, shape, dt=F32):
            return st.enter_context(nc.sbuf_tensor("sb_" + name, shape, dt))

        P = Prog(nc)
        par = sb("par", [128, NPAR])
        cst = sb("cst", [128, NCST - 256])
        cbf = sb("cbf", [128, 512], BF16)
        wlrin = sb("wlrin", [128, 8, 16], BF16)
        rgw = sb("rgw", [128, 2, 8, 128], BF16)
        wlr = sb("wlr", [16, 512], BF16)
        der = sb("der", [128, 64])
        DC, DC2, DNB, DEPS, DEPS4, DHC, DHBA, DHBX, DQ25 = 0, 8, 16, 20, 21, 24, 32, 40, 48
        wring = [sb("wring%d" % i, [128, SLOT], BF16) for i in range(NSLOT)]
        ps = st.enter_context(nc.psum_tensor("ps", [128, 4096], F32))

        P.add("sp", lambda e: e.dma_start(out=par[:], in_=d_par), writes=["par"], dma="par")
        P.add("sp", lambda e: e.dma_start(out=cst[:], in_=d_cst[:, 256:NCST]), writes=["cst"], dma="cst")
        P.add("pool", lambda e: e.dma_start(out=cbf[:], in_=d_cst[:, 0:512]), writes=["cbf"], dma="cbf")
        P.add("pool", lambda e: e.dma_start(out=wlrin[:], in_=d_wlrin), writes=["wlrin"], dma="wlrin")
        P.add("pool", lambda e: e.dma_start(out=rgw[:], in_=d_rgw), writes=["rgw"], dma="rgw")
        P.add("pool", lambda e: e.dma_start(out=wlr[:], in_=d_wlr), writes=["wlr"], dma="wlr")
        ones1024 = cbf[:, 0:128]
        ones256 = cbf[:, 128:256]
        identb = cbf[:, 256:384]
        P.add("act", lambda e: e.activation(out=der[:, DC:DC + 8], in_=par[:, P_LAM:P_LAM + 8], func=AF.Exp, scale=-1.0),
              reads=["par"], writes=["der"])
        P.add("act", lambda e: e.activation(out=der[:, DC:DC + 8], in_=der[:, DC:DC + 8], func=AF.Ln, bias=1.0),
              reads=["der"], writes=["der"])
        P.add("dve", lambda e: e.tensor_scalar(out=der[:, DC2:DC2 + 8], in0=der[:, DC:DC + 8], scalar1=-16.0, scalar2=None, op0=ALU.mult),
              reads=["der"], writes=["der"])
        P.add("dve", lambda e: e.tensor_scalar(out=der[:, DC:DC + 8], in0=der[:, DC:DC + 8], scalar1=-8.0, scalar2=None, op0=ALU.mult),
              reads=["der"], writes=["der"])
        P.add("dve", lambda e: e.tensor_scalar(out=der[:, DNB:DNB + 4], in0=par[:, P_BLR:P_BLR + 4], scalar1=-1.0, scalar2=None, op0=ALU.mult),
              reads=["par", "der"], writes=["der"])
        P.add("dve", lambda e: e.tensor_scalar(out=der[:, DHC:DHC + 8], in0=der[:, DC:DC + 8], scalar1=0.5, scalar2=None, op0=ALU.mult),
              reads=["der"], writes=["der"])
        P.add("dve", lambda e: e.tensor_scalar(out=der[:, DHBA:DHBA + 8], in0=par[:, P_BA:P_BA + 8], scalar1=0.5, scalar2=None, op0=ALU.mult),
              reads=["par", "der"], writes=["der"])
        P.add("dve", lambda e: e.tensor_scalar(out=der[:, DHBX:DHBX + 8], in0=par[:, P_BX:P_BX + 8], scalar1=0.5, scalar2=None, op0=ALU.mult),
              reads=["par", "der"], writes=["der"])
        P.add("dve", lambda e: e.memset(der[:, DQ25:DQ25 + 1], 0.25), reads=["der"], writes=["der"])
        P.add("dve", lambda e: e.memset(der[:, DEPS:DEPS + 1], EPS), reads=["der"], writes=["der"])
        P.add("dve", lambda e: e.memset(der[:, DEPS4:DEPS4 + 1], 4.0 * EPS), reads=["der"], writes=["der"])

        def make_ctx(name, W, psA, psB, psC, psM, keyM, banks, psS=None):
            c = Ctx()
            c.name, c.W = name, W
            c.groups = [(g, min(g + 512, W)) for g in range(0, W, 512)]
            c.x = sb(name + "_x", [128, 8, W])
            c.xn = sb(name + "_xn", [128, 8, W], BF16)
            c.hreg = sb(name + "_h", [128, 24, W], BF16)
            c.y = sb(name + "_y", [128, 8, W])
            c.sq = sb(name + "_sq", [128, 2, W], BF16)
            c.rstd = sb(name + "_rstd", [128, W])
            c.ps = {"A": psA, "B": psB}
            if psC is not None:
                c.ps["C"] = psC
            c.banks = banks
            c.pend = None
            c.statset = "C"
            if psS is not None:
                c.ps["S"] = psS
                c.statset = "S"
            c.psM = psM
            c.keyM = keyM
            c.rr = 0
            c.sqi = 0
            return c

        BK = lambda *i: [("bank", j) for j in i]
        cp_ = make_ctx("p", TW, ps[:, 0:1024], ps[:, 1024:2048], ps[:, 2048:3072], ps[:, 2560:3072], BK(5),
                       {"A": BK(0, 1), "B": BK(2, 3), "C": BK(4, 5)})
        cp_.order = ["A", "B", "C"]
        b6, b7 = 3072, 3584
        cs_ = make_ctx("s", NS, ps[:, b6:b6 + 16], ps[:, b7:b7 + 16], None, ps[:, b6 + 16:b6 + 32], BK(6),
                       {"A": BK(6), "B": BK(7), "S": BK(6)}, psS=ps[:, b6 + 16:b6 + 32])
        cs_.order = ["A", "B"]
        cs_.sqall = sb("s_sqall", [128, 8, NS], BF16)

        def K(c, what, i=None):
            if what == "y" and i in (4, 5, 6):
                return [(c.name, "y", i, 0), (c.name, "y", i, 1)]
            return (c.name, what, i)

        def pskey(c, s):
            return c.banks[s]

        def nextset(c, avoid=()):
            order = c.order
            for _ in range(len(order) + 1):
                s = order[c.rr % len(order)]
                c.rr += 1
                if s not in avoid:
                    return s
            raise AssertionError

        def T(c, i):
            return c.y[:, i, :]

        halo_bf = sb("halo_bf", [128, 8, 3], BF16)
        halo = sb("halo", [128, 8, 3])
        hcar = sb("hcar", [128, 8])
        S_f = sb("S_f", [128, NH, DV])
        S_b = sb("S_b", [128, 2, DV], BF16)
        S_t = sb("S_t", [128, DV])
        lr_bf = sb("lr_bf", [16, TW], BF16)
        s_h0 = sb("s_h0", [128, 8, NS])
        s_c0 = sb("s_c0", [128, 8, NS, 3])
        s_cn = sb("s_cn", [128, 8, NS, 3])
        s_hn = sb("s_hn", [128, 8, NS])
        s_xr = sb("s_xr", [128, 8, NS])
        s_t = sb("s_t", [128, 12, NS])
        s_xcb3 = sb("s_xcb3", [128, 8, NS], BF16)
        s_u = sb("s_u", [128, 6, 8, NS])
        s_lr = sb("s_lr", [16, NS], BF16)
        s_ktm = sb("s_ktm", [16, 128], BF16)
        s_vtm = sb("s_vtm", [16, DV], BF16)
        s_km = sb("s_km", [16, 1, 4, 128], BF16)
        s_Sb = sb("s_Sb", [128, 1, 4, DV], BF16)
        s_S = sb("s_S", [128, 3, 4, DV])
        s_q = sb("s_q", [128, NS])
        s_qgb = sb("s_qgb", [128, NS], BF16)
        s_pb = sb("s_pb", [128, 2, NS], BF16)
        s_eg = sb("s_eg", [128, NS])
        s_vq = sb("s_vq", [128, 2, NS])
        s_o = sb("s_o", [128, 2, NS])

        wstate = {"i": 0}

        def load_block(tile, bi):
            name, idx, n = PLAN[bi]
            gi = wstate["i"]
            wstate["i"] += 1
            slot = gi % NSLOT
            off = PLAN_OFFS[bi]
            P.add("pool", lambda e, slot=slot, off=off, n=n: e.dma_start(out=wring[slot][:, 0:n], in_=d_ws[:, off:off + n]),
                  writes=[("w", slot)], dma=("w", slot))
            return slot

        def wv3(slot, n, inner):
            return wring[slot][:, 0:n].rearrange("p (k c) -> p k c", c=inner)

        def wv4(slot):
            return wring[slot][:, 0:2048].rearrange("p (a k c) -> p a k c", a=2, k=8)

        def mm(out, lhsT, rhs, start, stop, reads, writes):
            P.add("pe", lambda e: e.matmul(out, lhsT=lhsT, rhs=rhs, start=start, stop=stop), reads=reads, writes=writes)

        def act(out, in_, func, reads, writes, bias=None, scale=None):
            kw = {}
            if bias is not None:
                kw["bias"] = bias
            if scale is not None:
                kw["scale"] = scale
            P.add("act", lambda e: e.activation(out=out, in_=in_, func=func, **kw), reads=reads, writes=writes)

        def tt(out, in0, in1, op, reads, writes, eng="dve"):
            P.add(eng, lambda e: e.tensor_tensor(out=out, in0=in0, in1=in1, op=op), reads=reads, writes=writes)

        def stt(out, in0, scalar, in1, op0, op1, reads, writes, eng="dve"):
            P.add(eng, lambda e: e.scalar_tensor_tensor(out=out, in0=in0, scalar=scalar, in1=in1, op0=op0, op1=op1), reads=reads, writes=writes)

        def ts(out, in0, s1, s2, op0, op1, reads, writes, eng="dve"):
            if s2 is None:
                P.add(eng, lambda e: e.tensor_scalar(out=out, in0=in0, scalar1=s1, scalar2=None, op0=op0), reads=reads, writes=writes)
            else:
                P.add(eng, lambda e: e.tensor_scalar(out=out, in0=in0, scalar1=s1, scalar2=s2, op0=op0, op1=op1), reads=reads, writes=writes)

        def cp(out, in_, reads, writes, eng="dve"):
            P.add(eng, lambda e: e.tensor_copy(out=out, in_=in_), reads=reads, writes=writes)

        def proj(c, s, lhsT_of_kc, rhs_chunks, nk, reads, M=128):
            for kc in range(nk):
                for (g0, g1) in c.groups:
                    mm(c.ps[s][0:M, g0:g1], lhsT_of_kc(kc), rhs_chunks(kc)[:, g0:g1], kc == 0, kc == nk - 1,
                       reads=reads(kc), writes=[pskey(c, s)])

        def stat_acc(c, src, src_reads, first, last, ones, statset=None):
            statset = statset or c.statset
            i = c.sqi % 2
            c.sqi += 1
            act(c.sq[:, i, :], src, AF.Square, reads=src_reads, writes=[K(c, "sq", i)])
            for (g0, g1) in c.groups:
                mm(c.ps[statset][:, g0:g1], ones, c.sq[:, i, g0:g1], first, last,
                   reads=[K(c, "sq", i), "cbf"], writes=[pskey(c, statset)])

        def rstd_from_stat(c, half, statset=None):
            statset = statset or c.statset
            flush_stat(c)
            if half:
                act(c.rstd[:, :], c.ps[statset][:, 0:c.W], AF.Ln, reads=[pskey(c, statset), "der"], writes=[K(c, "rstd")],
                    bias=der[:, DEPS4:DEPS4 + 1], scale=4.0)
            else:
                act(c.rstd[:, :], c.ps[statset][:, 0:c.W], AF.Ln, reads=[pskey(c, statset), "der"], writes=[K(c, "rstd")],
                    bias=der[:, DEPS:DEPS + 1], scale=1.0)
            act(c.rstd[:, :], c.rstd[:, :], AF.Exp, reads=[K(c, "rstd")], writes=[K(c, "rstd")], scale=-0.5)

        def flush_stat(c):
            if c.pend is not None:
                i, first, last = c.pend
                c.pend = None
                for (g0, g1) in c.groups:
                    mm(c.ps[c.statset][:, g0:g1], ones1024, c.sq[:, i, g0:g1], first, last,
                       reads=[K(c, "sq", i), "cbf"], writes=[pskey(c, c.statset)])

        def stat_all(c, src3, src_keys):
            act(c.sqall[:, :, :], src3, AF.Square, reads=src_keys, writes=[K(c, "sqall")])
            for ch in range(8):
                mm(c.ps["S"][:, 0:c.W], ones1024, c.sqall[:, ch, :], ch == 0, ch == 7, reads=[K(c, "sqall"), "cbf"], writes=[pskey(c, "S")])

        def prenorm_finish(c, gi):
            rstd_from_stat(c, False)
            for ch in range(8):
                stt(c.xn[:, ch, :], c.x[:, ch, :], par[:, P_GAIN + gi * 8 + ch:P_GAIN + gi * 8 + ch + 1], c.rstd[:, :], ALU.mult, ALU.mult,
                    reads=[K(c, "x", ch), K(c, "rstd"), "par"], writes=[K(c, "xn", ch)])

        def xn_pre(c, gi, ch, eng="act"):
            g = par[:, P_GAIN + gi * 8 + ch:P_GAIN + gi * 8 + ch + 1]
            if eng == "act":
                act(c.xn[:, ch, :], c.x[:, ch, :], AF.Identity, reads=[K(c, "x", ch), "par"], writes=[K(c, "xn", ch)], scale=g)
            else:
                ts(c.xn[:, ch, :], c.x[:, ch, :], g, None, ALU.mult, None, reads=[K(c, "x", ch), "par"], writes=[K(c, "xn", ch)])

        def prenorm(c, gi, early=None):
            if c is cs_:
                s_prenorm(c, gi)
                return
            for ch in range(8):
                stat_acc(c, c.x[:, ch, :], [K(c, "x", ch)], ch == 0, ch == 7, ones1024)
                if gi in (0, 4):
                    xn_pre(c, gi, ch, eng="dve")
                    if early is not None:
                        early(ch)
            if gi in (0, 4):
                rstd_from_stat(c, False)
            else:
                prenorm_finish(c, gi)

        def epilogue(c, gi, half, after_chunk=None):
            if c is cs_:
                s_epilogue(c, gi, half)
                return
            rstd_from_stat(c, half)
            for ch in range(8):
                stt(c.y[:, ch, :], c.y[:, ch, :], par[:, P_GAIN + gi * 8 + ch:P_GAIN + gi * 8 + ch + 1], c.rstd[:, :], ALU.mult, ALU.mult,
                    reads=[K(c, "y", ch), K(c, "rstd"), "par"], writes=[K(c, "y", ch)])
                tt(c.x[:, ch, :], c.x[:, ch, :], c.y[:, ch, :], ALU.add, reads=[K(c, "x", ch), K(c, "y", ch)], writes=[K(c, "x", ch)])
                if after_chunk is not None:
                    after_chunk(ch)

        def epilogue_prenorm(c, gi_post, half, gi_pre, early=None):
            if c is cs_:
                epilogue(c, gi_post, half)
                prenorm(c, gi_pre)
                return
            if gi_pre in (0, 4):
                def after(ch):
                    stat_acc(c, c.x[:, ch, :], [K(c, "x", ch)], ch == 0, ch == 7, ones1024)
                    xn_pre(c, gi_pre, ch)
                    if early is not None:
                        early(ch)
                epilogue(c, gi_post, half, after_chunk=after)
                rstd_from_stat(c, False)
                return
            epilogue(c, gi_post, half, after_chunk=lambda ch: stat_acc(c, c.x[:, ch, :], [K(c, "x", ch)], ch == 0, ch == 7, ones1024))
            prenorm_finish(c, gi_pre)

        def out_chunk(c, s, m, first, last):
            act(c.y[:, m, :], c.ps[s][:, 0:c.W], AF.Copy, reads=[pskey(c, s)], writes=[K(c, "y", m)])
            if c is cs_:
                return
            i = c.sqi % 2
            c.sqi += 1
            act(c.sq[:, i, :], c.ps[s][:, 0:c.W], AF.Square, reads=[pskey(c, s)], writes=[K(c, "sq", i)])
            c.pend = (i, first, last)

        def ffn_early(c, slot, kc):
            w = wv3(slot, 8 * 512, 512)
            for s_, off in (("A", 0), ("B", 256)):
                for (g0, g1) in c.groups:
                    mm(c.ps[s_][:, g0:g1], w[:, kc, off:off + 128], c.xn[:, kc, g0:g1], kc == 0, kc == 7,
                       reads=[("w", slot), K(c, "xn", kc)], writes=[pskey(c, s_)])

        def ffn_up(c, slot, jb, early=False):
            w = wv3(slot, 8 * 512, 512)
            for jj in range(2):
                j = jb * 2 + jj
                if early and j == 0:
                    sg_, su_ = "A", "B"
                else:
                    sg_, su_ = nextset(c), nextset(c)
                    proj(c, sg_, lambda kc: w[:, kc, jj * 128:(jj + 1) * 128], lambda kc: c.xn[:, kc, :], 8,
                         lambda kc: [("w", slot), K(c, "xn", kc)])
                    proj(c, su_, lambda kc: w[:, kc, 256 + jj * 128:256 + (jj + 1) * 128], lambda kc: c.xn[:, kc, :], 8,
                         lambda kc: [("w", slot), K(c, "xn", kc)])
                ta, tb = (4, 5) if j % 2 == 0 else (6, 7)
                rk = K(c, "rstd")
                tt(T(c, ta), c.ps[sg_][:, 0:c.W], c.rstd[:, :], ALU.mult, reads=[pskey(c, sg_), rk], writes=[K(c, "y", ta)])
                act(T(c, ta), T(c, ta), AF.Silu, reads=[K(c, "y", ta)], writes=[K(c, "y", ta)])
                tt(T(c, tb), c.ps[su_][:, 0:c.W], c.rstd[:, :], ALU.mult, reads=[pskey(c, su_), rk], writes=[K(c, "y", tb)])
                tt(c.hreg[:, j, :], T(c, ta), T(c, tb), ALU.mult, reads=[K(c, "y", ta), K(c, "y", tb)], writes=[K(c, "h", j)])

        def ffn_down(c, slot, m):
            w = wv3(slot, NJ * 128, 128)
            s = nextset(c, avoid=("C",))
            proj(c, s, lambda j: w[:, j, :], lambda j: c.hreg[:, j, :], NJ, lambda j: [("w", slot), K(c, "h", j)])
            flush_stat(c)
            out_chunk(c, s, m, m == 0, m == 7)

        def merge_m(c, slot_g, slot_b, m):
            wg = wv3(slot_g, 8 * 256, 256)
            wb = wv4(slot_b)
            sA = nextset(c)
            proj(c, sA, lambda kc: wg[:, kc, 0:128], lambda kc: c.xn[:, kc, :], 8, lambda kc: [("w", slot_g), K(c, "xn", kc)])
            act(T(c, 0), c.ps[sA][:, 0:c.W], AF.Sigmoid, reads=[pskey(c, sA)], writes=[K(c, "y", 0)])
            sY = nextset(c)
            proj(c, sY, lambda n: wb[:, 0, n, :], lambda n: c.hreg[:, n, :], 8, lambda n: [("w", slot_b), K(c, "h", n)])
            tt(T(c, 0), T(c, 0), c.ps[sY][:, 0:c.W], ALU.mult, reads=[K(c, "y", 0), pskey(c, sY)], writes=[K(c, "y", 0)])
            sB = nextset(c)
            proj(c, sB, lambda kc: wg[:, kc, 128:256], lambda kc: c.xn[:, kc, :], 8, lambda kc: [("w", slot_g), K(c, "xn", kc)])
            act(T(c, 1), c.ps[sB][:, 0:c.W], AF.Sigmoid, reads=[pskey(c, sB)], writes=[K(c, "y", 1)])
            sY2 = nextset(c)
            proj(c, sY2, lambda n: wb[:, 1, n, :], lambda n: c.hreg[:, 8 + n, :], 8, lambda n: [("w", slot_b), K(c, "h", 8 + n)])
            tt(T(c, 1), T(c, 1), c.ps[sY2][:, 0:c.W], ALU.mult, reads=[K(c, "y", 1), pskey(c, sY2)], writes=[K(c, "y", 1)])
            tt(c.hreg[:, 16 + m, :], T(c, 0), T(c, 1), ALU.add, reads=[K(c, "y", 0), K(c, "y", 1)], writes=[K(c, "h", 16 + m)])

        def outproj(c, slot, i, m2):
            w = wv4(slot)
            s = nextset(c, avoid=("C",))
            proj(c, s, lambda m: w[:, i, m, :], lambda m: c.hreg[:, 16 + m, :], 8, lambda m: [("w", slot), K(c, "h", 16 + m)])
            flush_stat(c)
            out_chunk(c, s, m2, m2 == 0, m2 == 7)

        s_flat = lambda: s_u[:, :, :, :].rearrange("p a n t -> p (a n t)")
        g3 = lambda gi: par[:, P_GAIN + gi * 8:P_GAIN + gi * 8 + 8].rearrange("p (n o) -> p n o", o=1).to_broadcast([128, 8, NS])
        r3 = lambda c: c.rstd[:, :].rearrange("p (o t) -> p o t", o=1).to_broadcast([128, 8, NS])

        def s_prenorm(c, gi):
            stat_all(c, c.x[:, :, :], [K(c, "x", ch) for ch in range(8)])
            rstd_from_stat(c, False)
            tt(s_u[:, 5, :, :], c.x[:, :, :], g3(gi), ALU.mult, reads=[K(c, "x", ch) for ch in range(8)] + ["par"], writes=[("s_u", 5)])
            tt(c.xn[:, :, :], s_u[:, 5, :, :], r3(c), ALU.mult, reads=[("s_u", 5), K(c, "rstd")], writes=[K(c, "xn", ch) for ch in range(8)])

        def s_epilogue(c, gi, half):
            yk_ = [K(c, "y", ch) for ch in range(8)]
            stat_all(c, c.y[:, :, :], yk_)
            rstd_from_stat(c, half)
            tt(c.y[:, :, :], c.y[:, :, :], g3(gi), ALU.mult, reads=yk_ + ["par"], writes=yk_)
            tt(c.y[:, :, :], c.y[:, :, :], r3(c), ALU.mult, reads=yk_ + [K(c, "rstd")], writes=yk_)
            tt(c.x[:, :, :], c.x[:, :, :], c.y[:, :, :], ALU.add, reads=yk_ + [K(c, "x", ch) for ch in range(8)], writes=[K(c, "x", ch) for ch in range(8)])

        def s_ffn_up(c, slot, jb):
            w = wv3(slot, 8 * 512, 512)
            for jj in range(2):
                j = jb * 2 + jj
                for kc in range(8):
                    mm(ps[:, b6 + j * NS:b6 + (j + 1) * NS], w[:, kc, jj * 128:(jj + 1) * 128], c.xn[:, kc, :], kc == 0, kc == 7,
                       reads=[("w", slot), K(c, "xn", kc)], writes=BK(6))
                for kc in range(8):
                    mm(ps[:, b7 + j * NS:b7 + (j + 1) * NS], w[:, kc, 256 + jj * 128:256 + (jj + 1) * 128], c.xn[:, kc, :], kc == 0, kc == 7,
                       reads=[("w", slot), K(c, "xn", kc)], writes=BK(7))
            if jb == NJ // 2 - 1:
                sg = s_flat()[:, 0:NJ * NS]
                allu = [("s_u", i) for i in range(6)]
                act(sg, ps[:, b6:b6 + NJ * NS], AF.Silu, reads=BK(6), writes=allu)
                tt(c.hreg[:, 0:NJ, :].rearrange("p j t -> p (j t)"), sg, ps[:, b7:b7 + NJ * NS], ALU.mult, reads=allu + BK(7),
                   writes=[K(c, "h", j) for j in range(NJ)])

        def s_outchunks(c, slot_reads, lhs_of, rhs_of, nk, m, last):
            for k in range(nk):
                mm(ps[:, b6 + m * NS:b6 + (m + 1) * NS], lhs_of(k), rhs_of(k), k == 0, k == nk - 1, reads=slot_reads(k), writes=BK(6))
            if last:
                act(c.y[:, :, :], ps[:, b6:b6 + 8 * NS].rearrange("p (m t) -> p m t", t=NS), AF.Copy, reads=BK(6), writes=[K(c, "y", ch) for ch in range(8)])

        def s_ffn_down(c, slot, m):
            w = wv3(slot, NJ * 128, 128)
            s_outchunks(c, lambda j: [("w", slot), K(c, "h", j)], lambda j: w[:, j, :], lambda j: c.hreg[:, j, :], NJ, m, m == 7)

        def s_outproj(c, slot, i, m2):
            w = wv4(slot)
            s_outchunks(c, lambda m: [("w", slot), K(c, "h", 16 + m)], lambda m: w[:, i, m, :], lambda m: c.hreg[:, 16 + m, :], 8, m2, m2 == 7)

        def s_merge_m(c, slot_g, slot_b, m):
            wg = wv3(slot_g, 8 * 256, 256)
            wb = wv4(slot_b)
            cs0, cs1 = m * NS, (m + 1) * NS
            for kc in range(8):
                mm(ps[:, b6 + cs0:b6 + cs1], wg[:, kc, 0:128], c.xn[:, kc, :], kc == 0, kc == 7, reads=[("w", slot_g), K(c, "xn", kc)], writes=BK(6))
            for n in range(8):
                mm(ps[:, b6 + 128 + cs0:b6 + 128 + cs1], wb[:, 0, n, :], c.hreg[:, n, :], n == 0, n == 7, reads=[("w", slot_b), K(c, "h", n)], writes=BK(6))
            for kc in range(8):
                mm(ps[:, b7 + cs0:b7 + cs1], wg[:, kc, 128:256], c.xn[:, kc, :], kc == 0, kc == 7, reads=[("w", slot_g), K(c, "xn", kc)], writes=BK(7))
            for n in range(8):
                mm(ps[:, b7 + 128 + cs0:b7 + 128 + cs1], wb[:, 1, n, :], c.hreg[:, 8 + n, :], n == 0, n == 7, reads=[("w", slot_b), K(c, "h", 8 + n)], writes=BK(7))
            if m == 7:
                f = s_flat()
                allu = [("s_u", i) for i in range(6)]
                act(f[:, 0:128], ps[:, b6:b6 + 128], AF.Sigmoid, reads=BK(6), writes=allu)
                act(f[:, 128:256], ps[:, b7:b7 + 128], AF.Sigmoid, reads=BK(7), writes=allu)
                tt(f[:, 0:128], f[:, 0:128], ps[:, b6 + 128:b6 + 256], ALU.mult, reads=allu + BK(6), writes=allu)
                tt(f[:, 128:256], f[:, 128:256], ps[:, b7 + 128:b7 + 256], ALU.mult, reads=allu + BK(7), writes=allu)
                tt(c.hreg[:, 16:24, :].rearrange("p m t -> p (m t)"), f[:, 0:128], f[:, 128:256], ALU.add, reads=allu,
                   writes=[K(c, "h", 16 + mm_) for mm_ in range(8)])

        pcol = lambda base, i: par[:, base + i:base + i + 1]

        def Hs(c, k):
            ap = c.hreg[:, 16 + 2 * k:18 + 2 * k, :].rearrange("p a w -> p (a w)").bitcast(F32)
            return ap, [K(c, "h", 16 + 2 * k), K(c, "h", 17 + 2 * k)]

        def rnn_slots(c, p):
            if p == 0:
                d = {"gl": (T(c, 0), K(c, "y", 0)), "r": (T(c, 1), K(c, "y", 1)), "i": (T(c, 2), K(c, "y", 2)),
                     "a": (T(c, 3), K(c, "y", 3)), "m": (T(c, 7), K(c, "y", 7)), "xc": Hs(c, 2)}
            else:
                d = {"gl": (T(c, 4), K(c, "y", 4)), "r": (T(c, 5), K(c, "y", 5)), "i": (T(c, 6), K(c, "y", 6)),
                     "a": Hs(c, 0), "m": Hs(c, 1), "xc": Hs(c, 3)}
            return d

        def rnn_parts(c, slot, n, tile):
            p = n % 2
            R = rnn_slots(c, p)
            xcb_p = c.hreg[:, 8 + p, :]
            xcbk = K(c, "h", 8 + p)
            xrb = c.hreg[:, 10 + 2 * p:12 + 2 * p, :].rearrange("p a w -> p (a w)")
            xk = [K(c, "h", 10 + 2 * p), K(c, "h", 11 + 2 * p)]
            dgp = c.hreg[:, 14 + p, :]
            dgk = K(c, "h", 14 + p)
            st = {}

            def A1():
                w = wv3(slot, 8 * 256, 256)
                sx = nextset(c)
                proj(c, sx, lambda kc: w[:, kc, 0:128], lambda kc: c.xn[:, kc, :], 8, lambda kc: [("w", slot), K(c, "xn", kc)])
                sy = nextset(c)
                proj(c, sy, lambda kc: w[:, kc, 128:256], lambda kc: c.xn[:, kc, :], 8, lambda kc: [("w", slot), K(c, "xn", kc)])
                cp(xrb[:, 0:3], halo_bf[:, n, :], reads=["halo_bf"], writes=[xk])
                cp(xrb[:, 3:TW + 3], c.ps[sx][:, 0:TW], reads=[pskey(c, sx)], writes=[xk])
                cp(halo[:, n, :], c.ps[sx][:, TW - 3:TW], reads=[pskey(c, sx)], writes=["halo"])
                cp(halo_bf[:, n, :], xrb[:, TW:TW + 3], reads=[xk], writes=["halo_bf"])
                idb = cst[:, C_IDENT - 256:C_IDENT - 256 + 128].rearrange("p (o c) -> p o c", o=1).to_broadcast([128, 4, 128])
                cwb = par[:, P_CW + n:P_CW + n + 25:8].rearrange("p (j o) -> p j o", o=1).to_broadcast([128, 4, 128])
                tt(dgp[:, 0:512].rearrange("p (j c) -> p j c", c=128), idb, cwb, ALU.mult, reads=["cst", "par"], writes=[dgk])
                act(R["gl"][0], c.ps[sy][:, 0:TW], AF.Gelu_apprx_tanh, reads=[pskey(c, sy)], writes=[R["gl"][1]])
                sc = nextset(c)
                for j in range(4):
                    for (g0, g1) in c.groups:
                        mm(c.ps[sc][:, g0:g1], dgp[:, j * 128:(j + 1) * 128], xrb[:, j + g0:j + g1], j == 0, j == 3, reads=[dgk, xk], writes=[pskey(c, sc)])
                st["sc"] = sc

            def A2():
                sc = st["sc"]
                act(xcb_p, c.ps[sc][:, 0:TW], AF.Identity, reads=[pskey(c, sc), "par"], writes=[xcbk], bias=pcol(P_CB, n))
                ts(R["xc"][0], c.ps[sc][:, 0:TW], pcol(P_CB, n), None, ALU.add, None, reads=[pskey(c, sc), "par"], writes=[R["xc"][1]])
                s1 = nextset(c)
                for (g0, g1) in c.groups:
                    mm(c.ps[s1][:, g0:g1], rgw[:, 0, n, :], xcb_p[:, g0:g1], True, True, reads=["rgw", xcbk], writes=[pskey(c, s1)])
                s2 = nextset(c)
                for (g0, g1) in c.groups:
                    mm(c.ps[s2][:, g0:g1], rgw[:, 1, n, :], xcb_p[:, g0:g1], True, True, reads=["rgw", xcbk], writes=[pskey(c, s2)])
                act(R["r"][0], c.ps[s1][:, 0:TW], AF.Tanh, reads=[pskey(c, s1), "der"], writes=[R["r"][1]], scale=0.5, bias=der[:, DHBA + n:DHBA + n + 1])
                act(R["i"][0], c.ps[s2][:, 0:TW], AF.Tanh, reads=[pskey(c, s2), "der"], writes=[R["i"][1]], scale=0.5, bias=der[:, DHBX + n:DHBX + n + 1])

            def B_act():
                act(R["a"][0], R["r"][0], AF.Exp, reads=[R["r"][1], "der"], writes=[R["a"][1]], scale=der[:, DHC + n:DHC + n + 1], bias=der[:, DHC + n:DHC + n + 1])
                act(R["m"][0], R["a"][0], AF.Square, reads=[R["a"][1]], writes=[R["m"][1]])
                act(R["m"][0], R["m"][0], AF.Sqrt, reads=[R["m"][1], "der"], writes=[R["m"][1]], scale=-0.25, bias=der[:, DQ25:DQ25 + 1])

            def B_dve():
                if tile == 0:
                    P.add("dve", lambda e: e.memset(R["m"][0][:, 0:1], 0.5), reads=[R["m"][1]], writes=[R["m"][1]])
                stt(R["i"][0], R["i"][0], 1.0, R["m"][0], ALU.add, ALU.mult, reads=[R["i"][1], R["m"][1]], writes=[R["i"][1]])
                tt(R["i"][0], R["i"][0], R["xc"][0], ALU.mult, reads=[R["i"][1], R["xc"][1]], writes=[R["i"][1]])
                P.add("dve", lambda e: e.tensor_tensor_scan(out=R["r"][0], data0=R["a"][0], data1=R["i"][0], initial=hcar[:, n:n + 1], op0=ALU.mult, op1=ALU.add),
                      reads=[R["a"][1], R["i"][1], R["r"][1], "hcar"], writes=[R["r"][1]])
                cp(hcar[:, n:n + 1], R["r"][0][:, TW - 1:TW], reads=[R["r"][1]], writes=["hcar"])
                tt(c.hreg[:, n, :], R["r"][0], R["gl"][0], ALU.mult, reads=[R["r"][1], R["gl"][1]], writes=[K(c, "h", n)])

            return A1, A2, B_act, B_dve

        s_xrp = ps[:, b6 + 128:b6 + 256]
        s_yrp = ps[:, b7 + 128:b7 + 256]

        def rnn_sample_proj(c, slot, n):
            w = wv3(slot, 8 * 256, 256)
            for kc in range(8):
                mm(s_xrp[:, n * NS:(n + 1) * NS], w[:, kc, 0:128], c.xn[:, kc, :], kc == 0, kc == 7, reads=[("w", slot), K(c, "xn", kc)], writes=BK(6))
            for kc in range(8):
                mm(s_yrp[:, n * NS:(n + 1) * NS], w[:, kc, 128:256], c.xn[:, kc, :], kc == 0, kc == 7, reads=[("w", slot), K(c, "xn", kc)], writes=BK(7))

        def rnn_sample_tail(c, part):
            v3 = lambda ap: ap.rearrange("p (n t) -> p n t", t=NS)
            bc = lambda col: par[:, col:col + 8].rearrange("p (n o) -> p n o", o=1).to_broadcast([128, 8, NS])
            dbc = lambda col: der[:, col:col + 8].rearrange("p (n o) -> p n o", o=1).to_broadcast([128, 8, NS])
            U = lambda i: s_u[:, i, :, :]
            uk = lambda i: ("s_u", i)
            if part == "a":
                act(s_xr[:, :, :], v3(s_xrp), AF.Copy, reads=BK(6), writes=["s_xr"])
                act(U(0), v3(s_yrp), AF.Gelu_apprx_tanh, reads=BK(7), writes=[uk(0)])
                tt(U(1), s_c0[:, :, :, 0], bc(P_CW + 0), ALU.mult, reads=["s_c0", "par"], writes=[uk(1)])
                for j in (1, 2):
                    tt(U(2), s_c0[:, :, :, j], bc(P_CW + j * 8), ALU.mult, reads=["s_c0", "par"], writes=[uk(2)])
                    tt(U(1), U(1), U(2), ALU.add, reads=[uk(1), uk(2)], writes=[uk(1)])
                tt(U(2), s_xr[:, :, :], bc(P_CW + 3 * 8), ALU.mult, reads=["s_xr", "par"], writes=[uk(2)])
                tt(U(1), U(1), U(2), ALU.add, reads=[uk(1), uk(2)], writes=[uk(1)])
                tt(U(1), U(1), bc(P_CB), ALU.add, reads=[uk(1), "par"], writes=[uk(1)])
                act(s_xcb3[:, :, :], U(1), AF.Copy, reads=[uk(1)], writes=["s_xcb3"])
                return
            for n in range(8):
                mm(s_xrp[:, n * NS:(n + 1) * NS], rgw[:, 0, n, :], s_xcb3[:, n, :], True, True, reads=["rgw", "s_xcb3"], writes=BK(6))
                mm(s_yrp[:, n * NS:(n + 1) * NS], rgw[:, 1, n, :], s_xcb3[:, n, :], True, True, reads=["rgw", "s_xcb3"], writes=BK(7))
            tt(U(2), v3(s_xrp), bc(P_BA), ALU.add, reads=BK(6) + ["par"], writes=[uk(2)])
            tt(U(3), v3(s_yrp), bc(P_BX), ALU.add, reads=BK(7) + ["par"], writes=[uk(3)])
            act(U(2), U(2), AF.Sigmoid, reads=[uk(2)], writes=[uk(2)])
            act(U(3), U(3), AF.Sigmoid, reads=[uk(3)], writes=[uk(3)])
            tt(U(2), U(2), dbc(DC), ALU.mult, reads=[uk(2), "der"], writes=[uk(2)])
            act(U(4), U(2), AF.Exp, reads=[uk(2)], writes=[uk(4)])
            act(U(5), U(2), AF.Exp, reads=[uk(2)], writes=[uk(5)], scale=2.0)
            act(U(5), U(5), AF.Sqrt, reads=[uk(5)], writes=[uk(5)], scale=-1.0, bias=1.0)
            tt(U(3), U(3), U(5), ALU.mult, reads=[uk(3), uk(5)], writes=[uk(3)])
            tt(U(3), U(3), U(1), ALU.mult, reads=[uk(3), uk(1)], writes=[uk(3)])
            tt(U(4), U(4), s_h0[:, :, :], ALU.mult, reads=[uk(4), "s_h0"], writes=[uk(4)])
            tt(s_hn[:, :, :], U(4), U(3), ALU.add, reads=[uk(4), uk(3)], writes=["s_hn"])
            tt(c.hreg[:, 0:8, :], s_hn[:, :, :], U(0), ALU.mult, reads=["s_hn", uk(0)], writes=[K(c, "h", n) for n in range(8)])

        def gla_lr(c, lrb, lrkey):
            s = nextset(c)
            proj(c, s, lambda kc: wlrin[:, kc, :], lambda kc: c.xn[:, kc, :], 8, lambda kc: ["wlrin", K(c, "xn", kc)], M=16)
            act(lrb, c.ps[s][0:16, 0:c.W], AF.Copy, reads=[pskey(c, s)], writes=[lrkey])

        def gla_decay(c, hd, lrb, lrkey, Tl, kl, s=None):
            s = s or nextset(c)
            for (g0, g1) in c.groups:
                mm(c.ps[s][:, g0:g1], wlr[0:16, hd * 128:(hd + 1) * 128], lrb[0:16, g0:g1], True, True,
                   reads=["wlr", lrkey], writes=[pskey(c, s)])
            act(Tl, c.ps[s][:, 0:c.W], AF.Exp, reads=[pskey(c, s), "der"], writes=[kl], scale=-1.0, bias=der[:, DNB + hd:DNB + hd + 1])
            act(Tl, Tl, AF.Ln, reads=[kl], writes=[kl], bias=1.0)

        def gla_gate_out(c, hd, slot_v, o_aps, o_keys, Trs, krs, Tsg, ksg, Tt, ktt, stat_ps, stat_key, sg_pre=None):
            wv = wv3(slot_v, 8 * 512, 512)
            for (g0, g1) in c.groups:
                for dvc in range(2):
                    i = c.sqi % 2
                    c.sqi += 1
                    act(c.sq[:, i, g0:g1], o_aps[dvc][:, g0:g1], AF.Square, reads=[o_keys[dvc]], writes=[K(c, "sq", i)])
                    mm(stat_ps[:, 0:g1 - g0], ones256, c.sq[:, i, g0:g1], dvc == 0, dvc == 1, reads=[K(c, "sq", i), "cbf"], writes=stat_key)
                act(Trs[:, g0:g1], stat_ps[:, 0:g1 - g0], AF.Ln, reads=stat_key + ["der"], writes=[krs], bias=der[:, DEPS:DEPS + 1], scale=1.0)
            act(Trs, Trs, AF.Exp, reads=[krs], writes=[krs], scale=-0.5)
            for dvc in range(2):
                if sg_pre is not None:
                    Tsg_, ksg_ = sg_pre[dvc]
                else:
                    s = c.gla_og or nextset(c)
                    proj(c, s, lambda kc: wv[:, kc, 256 + dvc * 128:256 + (dvc + 1) * 128], lambda kc: c.xn[:, kc, :], 8,
                         lambda kc: [("w", slot_v), K(c, "xn", kc)])
                    act(Tsg, c.ps[s][:, 0:c.W], AF.Silu, reads=[pskey(c, s)], writes=[ksg])
                    Tsg_, ksg_ = Tsg, ksg
                stt(Tt, o_aps[dvc][:, 0:c.W], pcol(P_GNG, dvc), Trs, ALU.mult, ALU.mult, reads=[o_keys[dvc], "par", krs], writes=[ktt])
                tt(c.hreg[:, 8 + hd * 2 + dvc, :], Tt, Tsg_, ALU.mult, reads=[ktt, ksg_], writes=[K(c, "h", 8 + hd * 2 + dvc)])

        def gla_prompt(c, slot_qk, slot_v, hd, tile):
            wq = wv3(slot_qk, 8 * 256, 256)
            wv = wv3(slot_v, 8 * 512, 512)
            yk = lambda i: K(c, "y", i)
            gla_decay(c, hd, lr_bf, "lr_bf", T(c, 0), yk(0), s="C")
            P.add("dve", lambda e: e.tensor_tensor_scan(out=T(c, 1), data0=cst[:, C_RMASK - 256:C_RMASK - 256 + TW], data1=T(c, 0), initial=0.0, op0=ALU.mult, op1=ALU.add),
                  reads=[yk(0), "cst"], writes=[yk(1)])
            act(T(c, 2), T(c, 1), AF.Exp, reads=[yk(1)], writes=[yk(2)], scale=-1.0 / 16.0)
            act(T(c, 3), T(c, 1), AF.Exp, reads=[yk(1)], writes=[yk(3)], scale=1.0 / 16.0)
            qi = c.y[:, 4, :].bitcast(BF16)[:, 0:TW]
            ki = c.y[:, 4, :].bitcast(BF16)[:, TW:2 * TW]
            attT = c.y[:, 5, :].bitcast(BF16)[:, 0:TW]
            kiT = c.y[:, 5, :].bitcast(BF16)[:, TW:2 * TW]
            v_bf = c.y[:, 6, :].bitcast(BF16).rearrange("p (b v) -> p b v", v=DV)
            def vhalf(hf, sv):
                for bb in range(4):
                    bk = hf * 4 + bb
                    for kc in range(8):
                        mm(c.ps[sv][:, bb * DV:(bb + 1) * DV], c.xn[:, kc, bk * 128:(bk + 1) * 128], wv[:, kc, 0:DV], kc == 0, kc == 7,
                           reads=[("w", slot_v), K(c, "xn", kc)], writes=[pskey(c, sv)])
                act(c.y[:, 6, :].bitcast(BF16)[:, hf * TW:(hf + 1) * TW], c.ps[sv][:, 0:TW], AF.Copy, reads=[pskey(c, sv)], writes=[(c.name, "y", 6, hf)])
            sq_ = "A"
            proj(c, sq_, lambda kc: wq[:, kc, 0:128], lambda kc: c.xn[:, kc, :], 8, lambda kc: [("w", slot_qk), K(c, "xn", kc)])
            sk_ = "B"
            proj(c, sk_, lambda kc: wq[:, kc, 128:256], lambda kc: c.xn[:, kc, :], 8, lambda kc: [("w", slot_qk), K(c, "xn", kc)])
            stt(qi, c.ps[sq_][:, 0:TW], float(DK) ** -0.5, T(c, 2), ALU.mult, ALU.mult, reads=[pskey(c, sq_), yk(2)], writes=[(c.name, "y", 4, 0)])
            tt(ki, c.ps[sk_][:, 0:TW], T(c, 3), ALU.mult, reads=[pskey(c, sk_), yk(3)], writes=[(c.name, "y", 4, 1)])
            vhalf(0, "C")
            sa = "A"
            for bk in range(8):
                b0, b1 = bk * 128, (bk + 1) * 128
                mm(c.ps[sa][:, b0:b1], ki[:, b0:b1], qi[:, b0:b1], True, True, reads=[(c.name, "y", 4, 1), (c.name, "y", 4, 0)], writes=[pskey(c, sa)])
            maskb = cst[:, C_MASK - 256:C_MASK - 256 + 128].rearrange("p (o c) -> p o c", o=1).to_broadcast([128, 8, 128])
            tt(attT.rearrange("p (b c) -> p b c", c=128), c.ps[sa][:, 0:TW].rearrange("p (b c) -> p b c", c=128), maskb, ALU.mult,
               reads=[pskey(c, sa), "cst"], writes=[(c.name, "y", 5, 0)])
            vhalf(1, "B")
            psMb = c.psM.bitcast(BF16)
            for bk in range(8):
                b0, b1 = bk * 128, (bk + 1) * 128
                P.add("pe", lambda e, b0=b0, b1=b1: e.transpose(psMb[:, b0:b1], ki[:, b0:b1], identb),
                      reads=[(c.name, "y", 4, 1), "cbf"], writes=c.keyM)
            act(kiT, psMb[:, 0:TW], AF.Copy, reads=c.keyM, writes=[(c.name, "y", 5, 1)])
            osets = ["A", "B"]
            dSb = ps[:, 2048:2560]
            for bk in range(8):
                b0, b1 = bk * 128, (bk + 1) * 128
                sbi = bk % 2
                for dvc in range(2):
                    mm(c.ps[osets[dvc]][:, b0:b1], v_bf[:, bk, dvc * 128:(dvc + 1) * 128], attT[:, b0:b1], True, False,
                       reads=[K(c, "y", 6), (c.name, "y", 5, 0)], writes=[pskey(c, osets[dvc])])
                    mm(c.ps[osets[dvc]][:, b0:b1], S_b[:, sbi, dvc * 128:(dvc + 1) * 128], qi[:, b0:b1], False, True,
                       reads=[("S_b", sbi), (c.name, "y", 4, 0)], writes=[pskey(c, osets[dvc])])
                mk = BK(4)
                dS = dSb[:, (bk % 2) * 256:(bk % 2) * 256 + 256]
                mm(dS, kiT[:, b0:b1], v_bf[:, bk, :], True, True, reads=[(c.name, "y", 5, 1), K(c, "y", 6)], writes=mk)
                tt(S_t[:, :], S_f[:, hd, :], dS, ALU.add, reads=[("S_f", hd)] + mk, writes=["S_t"])
                eql = T(c, 2)[:, b1 - 1:b1]
                ts(S_b[:, 1 - sbi, :], S_t[:, :], eql, None, ALU.mult, None, reads=["S_t", yk(2)], writes=[("S_b", 1 - sbi)])
                ts(S_f[:, hd, :], S_t[:, :], eql, None, ALU.mult, None, reads=["S_t", yk(2)], writes=[("S_f", hd)])
                pi = bk // 2
                dvc, hf = pi // 2, pi % 2
                sgT, sgk = (T(c, 0), yk(0)) if dvc == 0 else (T(c, 3), yk(3))
                for kc in range(4 * (bk % 2), 4 * (bk % 2) + 4):
                    mm(c.psM[:, 0:512], wv[:, kc, 256 + dvc * 128:256 + (dvc + 1) * 128], c.xn[:, kc, hf * 512:(hf + 1) * 512], kc == 0, kc == 7,
                       reads=[("w", slot_v), K(c, "xn", kc)], writes=c.keyM)
                if bk % 2 == 1:
                    act(sgT[:, hf * 512:(hf + 1) * 512], c.psM[:, 0:512], AF.Silu, reads=c.keyM, writes=[sgk])
            c.gla_og = "C"
            gla_gate_out(c, hd, slot_v, [c.ps["A"], c.ps["B"]], [pskey(c, "A"), pskey(c, "B")], T(c, 7), yk(7), None, None, T(c, 1), yk(1),
                         c.psM, c.keyM, sg_pre=[(T(c, 0), yk(0)), (T(c, 3), yk(3))])

        s_state = {"loaded": set(), "q": 0}

        def s_piece_load(q):
            if q >= 16 or q in s_state["loaded"]:
                return
            s_state["loaded"].add(q)
            hd, p = q // 4, q % 4
            i = q % 3
            P.add("sp", lambda e: e.dma_start(out=s_S[:, i, :, :], in_=d_s0[p * 4:(p + 1) * 4, hd, :, :].rearrange("b k v -> k b v")),
                  writes=[("s_S", i)], dma=("s_S_in", i))

        def gla_sample(c, slot_qk, slot_v, hd):
            wq = wv3(slot_qk, 8 * 256, 256)
            wv = wv3(slot_v, 8 * 512, 512)
            tm = lambda i: s_t[:, i, :]
            tk = lambda i: ("s_t", i)
            s_piece_load(hd * 4)
            s_piece_load(hd * 4 + 1)
            gla_decay(c, hd, s_lr, "s_lr", tm(0), tk(0))
            act(s_eg[:, :], tm(0), AF.Exp, reads=[tk(0)], writes=["s_eg"], scale=-1.0 / 16.0)
            sq_ = nextset(c)
            proj(c, sq_, lambda kc: wq[:, kc, 0:128], lambda kc: c.xn[:, kc, :], 8, lambda kc: [("w", slot_qk), K(c, "xn", kc)])
            ts(s_q[:, :], c.ps[sq_][:, 0:NS], float(DK) ** -0.5, None, ALU.mult, None, reads=[pskey(c, sq_)], writes=["s_q"])
            tt(s_qgb[:, :], s_q[:, :], s_eg[:, :], ALU.mult, reads=["s_q", "s_eg"], writes=["s_qgb"])
            sk_ = nextset(c)
            proj(c, sk_, lambda kc: wq[:, kc, 128:256], lambda kc: c.xn[:, kc, :], 8, lambda kc: [("w", slot_qk), K(c, "xn", kc)])
            tt(tm(1), c.ps[sk_][:, 0:NS], s_q[:, :], ALU.mult, reads=[pskey(c, sk_), "s_q"], writes=[tk(1)])
            qk_ps = ps[:, b7 + 48:b7 + 64]
            cp(s_pb[:, 0, :], tm(1), reads=[tk(1)], writes=["s_pb"])
            tt(s_pb[:, 1, :], tm(1), s_pb[:, 0, :], ALU.subtract, reads=[tk(1), "s_pb"], writes=["s_pb"])
            mm(qk_ps, ones1024, s_pb[:, 0, :], True, False, reads=["cbf", "s_pb"], writes=BK(7))
            mm(qk_ps, ones1024, s_pb[:, 1, :], False, True, reads=["cbf", "s_pb"], writes=BK(7))
            ts(tm(2), qk_ps, 1024.0, None, ALU.mult, None, reads=BK(7), writes=[tk(2)])
            for dvc in range(2):
                sv = nextset(c)
                proj(c, sv, lambda kc: wv[:, kc, dvc * 128:(dvc + 1) * 128], lambda kc: c.xn[:, kc, :], 8, lambda kc: [("w", slot_v), K(c, "xn", kc)])
                tt(s_vq[:, dvc, :], c.ps[sv][:, 0:NS], tm(2), ALU.mult, reads=[pskey(c, sv), tk(2)], writes=[("s_vq", dvc)])
            bor = cp_.ps["C"]
            bk_ = cp_.banks["C"]
            for kc in range(8):
                mm(bor[0:NS, 0:128], c.xn[:, kc, :], wq[:, kc, 128:256], kc == 0, kc == 7, reads=[("w", slot_qk), K(c, "xn", kc)], writes=bk_)
            act(s_ktm[:, :], bor[0:NS, 0:128], AF.Copy, reads=bk_, writes=["s_ktm"])
            for kc in range(8):
                mm(bor[0:NS, 128:128 + DV], c.xn[:, kc, :], wv[:, kc, 0:DV], kc == 0, kc == 7, reads=[("w", slot_v), K(c, "xn", kc)], writes=bk_)
            act(s_vtm[:, :], bor[0:NS, 128:128 + DV], AF.Copy, reads=bk_, writes=["s_vtm"])
            so = [bor[:, 512:512 + NS], bor[:, 512 + NS:512 + 2 * NS]]
            dSp = [ps[:, b6 + 256:b6 + 512], ps[:, b7 + 256:b7 + 512]]
            dSk = [BK(6), BK(7)]
            for p in range(4):
                q = hd * 4 + p
                i = q % 3
                s_piece_load(q + 2)
                act(s_Sb[:, 0, :, :], s_S[:, i, :, :], AF.Copy, reads=[("s_S", i)], writes=[("s_Sb", 0)])
                for bb in range(4):
                    b = p * 4 + bb
                    for dvc in range(2):
                        mm(so[dvc][:, b:b + 1], s_Sb[:, 0, bb, dvc * 128:(dvc + 1) * 128], s_qgb[:, b:b + 1], True, True,
                           reads=[("s_Sb", 0), "s_qgb"], writes=bk_)
                ktb = s_ktm[:, :].rearrange("t (o k) -> t o k", o=1).to_broadcast([NS, 4, 128])
                dlt = cst[0:NS, C_IDENT - 256 + p * 4:C_IDENT - 256 + p * 4 + 4].rearrange("t (b o) -> t b o", o=1).to_broadcast([NS, 4, 128])
                tt(s_km[:, 0, :, :], ktb, dlt, ALU.mult, reads=["s_ktm", "cst"], writes=[("s_km", 0)])
                for bb in range(4):
                    b = p * 4 + bb
                    kmi = b % 2
                    mm(dSp[kmi], s_km[:, 0, bb, :], s_vtm[:, :], True, True, reads=[("s_km", 0), "s_vtm"], writes=dSk[kmi])
                    stt(s_S[:, i, bb, :], s_S[:, i, bb, :], s_eg[:, b:b + 1], dSp[kmi], ALU.mult, ALU.add,
                        reads=[("s_S", i), "s_eg"] + dSk[kmi], writes=[("s_S", i)])
                P.add("sp", lambda e, p=p, i=i: e.dma_start(out=o_ss[p * 4:(p + 1) * 4, hd, :, :].rearrange("b k v -> k b v"), in_=s_S[:, i, :, :]),
                      reads=[("s_S", i)], dma=("s_S_out", i))
            for dvc in range(2):
                tt(s_o[:, dvc, :], so[dvc], s_vq[:, dvc, :], ALU.add, reads=bk_ + [("s_vq", dvc)], writes=[("s_o", dvc)])
            c.gla_og = None
            gla_gate_out(c, hd, slot_v, [s_o[:, 0, :], s_o[:, 1, :]], [("s_o", 0), ("s_o", 1)], tm(7), tk(7), tm(8), tk(8), tm(9), tk(9),
                         c.psM, c.keyM)

        cp_.gla_avoid = ()
        cs_.gla_avoid = ()

        P.add("dve", lambda e: e.memset(halo[:, :, :], 0.0), writes=["halo"])
        P.add("dve", lambda e: e.memset(halo_bf[:, :, :], 0.0), writes=["halo_bf"])
        P.add("dve", lambda e: e.memset(hcar[:, :], 0.0), writes=["hcar"])
        P.add("dve", lambda e: e.memset(S_f[:, :, :], 0.0), writes=[("S_f", h) for h in range(NH)])
        P.add("sp", lambda e: e.dma_start(out=cs_.x[:, :, :], in_=d_xs), writes=[K(cs_, "x", ch) for ch in range(8)], dma="s_x_in")
        P.add("sp", lambda e: e.dma_start(out=s_h0[:, :, :], in_=d_h0), writes=["s_h0"], dma="s_h0")
        P.add("sp", lambda e: e.dma_start(out=s_c0[:, :, :, :], in_=d_c0), writes=["s_c0"], dma="s_c0")

        def load_x(tile, ch, eng="sp"):
            t0 = tile * TW
            P.add(eng, lambda e: e.dma_start(out=cp_.x[:, ch, :], in_=d_x[:, ch, t0:t0 + TW]), writes=[K(cp_, "x", ch)], dma=("xin_" + eng, ch))

        def store_y(tile, ch):
            t0 = tile * TW
            P.add("sp", lambda e: e.dma_start(out=o_y[:, ch, t0:t0 + TW], in_=cp_.x[:, ch, :]), reads=[K(cp_, "x", ch)], dma=("yout", ch))

        for ch in range(8):
            load_x(0, ch, "sp" if ch < 4 else "pool")
        deferred = []

        def run_deferred():
            while deferred:
                deferred.pop(0)()

        for tile in range(NT):
            ctxs = [cp_] + ([cs_] if tile == NT - 1 else [])
            bi = 0

            def ffn_body(gi_pre, bi, slot0=None):
                P.tag = "t%d:ffn%d_up" % (tile, gi_pre)
                for jb in range(NJ // 2):
                    if jb == 0 and slot0 is not None:
                        slot = slot0
                    else:
                        slot = load_block(tile, bi)
                    bi += 1
                    for c in ctxs:
                        if c is cs_:
                            run_deferred()
                            s_ffn_up(c, slot, jb)
                        else:
                            ffn_up(c, slot, jb, early=(jb == 0 and slot0 is not None))
                P.tag = "t%d:ffn%d_dn" % (tile, gi_pre)
                for m in range(8):
                    slot = load_block(tile, bi); bi += 1
                    for c in ctxs:
                        (s_ffn_down if c is cs_ else ffn_down)(c, slot, m)
                return bi

            P.tag = "t%d:prenorm0" % tile
            slot0 = load_block(tile, bi)
            for c in ctxs:
                if c is cs_:
                    deferred.append(lambda c=c: prenorm(c, 0))
                else:
                    prenorm(c, 0, early=lambda ch, slot0=slot0: ffn_early(cp_, slot0, ch))
            bi = ffn_body(0, bi, slot0)
            P.tag = "t%d:epi1" % tile
            for c in ctxs:
                if c is cs_:
                    deferred.append(lambda c=c: epilogue_prenorm(c, 1, True, 2))
                else:
                    epilogue_prenorm(c, 1, True, 2)
            P.tag = "t%d:lr" % tile
            gla_lr(cp_, lr_bf[:, :], "lr_bf")
            if cs_ in ctxs:
                deferred.append(lambda: gla_lr(cs_, s_lr[:, :], "s_lr"))
            P.tag = "t%d:rnn" % tile
            prev = None
            for n in range(8):
                slot = load_block(tile, bi); bi += 1
                A1, A2, Ba, Bd = rnn_parts(cp_, slot, n, tile)
                if prev is not None:
                    prev[0]()
                A1()
                if cs_ in ctxs:
                    run_deferred()
                    rnn_sample_proj(cs_, slot, n)
                A2()
                if prev is not None:
                    prev[1]()
                prev = (Ba, Bd)
            prev[0]()
            prev[1]()
            if cs_ in ctxs:
                rnn_sample_tail(cs_, "a")
            P.tag = "t%d:gla" % tile
            for hd in range(NH):
                P.add("dve", lambda e: e.memset(S_b[:, 0, :], 0.0), writes=[("S_b", 0)]) if tile == 0 else None
                if tile > 0:
                    cp(S_b[:, 0, :], S_f[:, hd, :], reads=[("S_f", hd)], writes=[("S_b", 0)])
                slot_qk = load_block(tile, bi); bi += 1
                slot_v = load_block(tile, bi); bi += 1
                gla_prompt(cp_, slot_qk, slot_v, hd, tile)
                if cs_ in ctxs and hd == 0:
                    rnn_sample_tail(cs_, "b")
                if cs_ in ctxs:
                    gla_sample(cs_, slot_qk, slot_v, hd)
            P.tag = "t%d:merge" % tile
            for m in range(8):
                slot_g = load_block(tile, bi); bi += 1
                slot_b = load_block(tile, bi); bi += 1
                for c in ctxs:
                    (s_merge_m if c is cs_ else merge_m)(c, slot_g, slot_b, m)
            P.tag = "t%d:outproj" % tile
            for mp in range(4):
                slot = load_block(tile, bi); bi += 1
                for i in range(2):
                    for c in ctxs:
                        (s_outproj if c is cs_ else outproj)(c, slot, i, mp * 2 + i)
            P.tag = "t%d:epi3" % tile
            slot4 = load_block(tile, bi)
            for c in ctxs:
                if c is cs_:
                    deferred.append(lambda c=c: epilogue_prenorm(c, 3, False, 4))
                else:
                    epilogue_prenorm(c, 3, False, 4, early=lambda ch, slot4=slot4: ffn_early(cp_, slot4, ch))
            bi = ffn_body(4, bi, slot4)
            assert bi == len(PLAN)
            P.tag = "t%d:epi5" % tile
            pend_load = []

            def after5(ch):
                store_y(tile, ch)
                if tile + 1 < NT:
                    pend_load.append(ch)
                    if len(pend_load) > 1:
                        load_x(tile + 1, pend_load.pop(0))

            for c in ctxs:
                epilogue(c, 5, True, after_chunk=after5 if c is cp_ else None)
            while pend_load:
                load_x(tile + 1, pend_load.pop(0))

        P.add("sp", lambda e: e.dma_start(out=o_ys, in_=cs_.x[:, :, :]), reads=[K(cs_, "x", ch) for ch in range(8)], dma="o_ys")
        P.add("sp", lambda e: e.dma_start(out=o_hp, in_=hcar[:, :]), reads=["hcar"], dma="o_hp")
        P.add("sp", lambda e: e.dma_start(out=o_cp, in_=halo[:, :, :]), reads=["halo"], dma="o_cp")
        P.add("sp", lambda e: e.dma_start(out=o_sp, in_=S_f[:, :, :]), reads=[("S_f", h) for h in range(NH)], dma="o_sp")
        P.add("sp", lambda e: e.dma_start(out=o_hs, in_=s_hn[:, :, :]), reads=["s_hn"], dma="o_hs")
        cp(s_cn[:, :, :, 0:2], s_c0[:, :, :, 1:3], reads=["s_c0"], writes=["s_cn"])
        cp(s_cn[:, :, :, 2], s_xr[:, :, :], reads=["s_xr", "s_cn"], writes=["s_cn"])
        P.add("sp", lambda e: e.dma_start(out=o_cs, in_=s_cn[:, :, :, :]), reads=["s_cn"], dma="o_cs")
        if debug:
            for name in debug:
                src, keys = DEBUG_SRC[name](locals())
                P.add("sp", lambda e, name=name, src=src: e.dma_start(out=dbg_out[name], in_=src), reads=keys, dma="dbg_" + name)
        P.emit()
    TAGMAP.clear()
    TAGMAP.update(P.tagmap)
    return nc


DEBUG_SRC = {}
TAGMAP = {}
_NC_CACHE = {}


def _prep_inputs(inp):
    inp = {k: np.asarray(v) for k, v in inp.items()}
    ws = _pack_weights(inp)
    par = _pack_params(inp)
    cst = _consts()
    w_in = inp["w_in"][0]
    wlrin = _kc(w_in, np.arange(OFF_LR, OFF_LR + 16))
    rgw = np.ascontiguousarray(np.stack([inp["rg_w_a"][0].transpose(1, 0, 2), inp["rg_w_x"][0].transpose(1, 0, 2)], axis=1)).astype(np.float32)
    wlr = np.ascontiguousarray(inp["gla_w_lr"][0])
    maps = []
    for c in range(NCORES):
        x = inp["x_prompt"][c]
        xT = np.ascontiguousarray(x.reshape(SEQ, 8, 128).transpose(2, 1, 0))
        sl = slice(c * NS, (c + 1) * NS)
        xs = inp["x_sample"][sl, 0, :]
        xsT = np.ascontiguousarray(xs.reshape(NS, 8, 128).transpose(2, 1, 0))
        h0 = np.ascontiguousarray(inp["state_rnn_h"][0, sl].reshape(NS, 8, 128).transpose(2, 1, 0))
        c0 = np.ascontiguousarray(inp["state_rnn_conv"][0, sl].reshape(NS, 3, 8, 128).transpose(3, 2, 0, 1))
        s0 = np.ascontiguousarray(inp["state_gla"][0, sl])
        maps.append({"xT": xT, "xsT": xsT, "h0": h0, "c0": c0, "s0": s0, "ws": ws, "par": par, "cst": cst,
                     "wlrin": wlrin, "rgw": rgw, "wlr": wlr})
    return maps


def _assemble(results):
    yp = np.empty((NCORES, SEQ, D), np.float32)
    ys = np.empty((NCORES * NS, 1, D), np.float32)
    hp = np.empty((1, NCORES, D), np.float32)
    cpo = np.empty((1, NCORES, 3, D), np.float32)
    spo = np.empty((1, NCORES, NH, 128, DV), np.float32)
    hs = np.empty((1, NCORES * NS, D), np.float32)
    cso = np.empty((1, NCORES * NS, 3, D), np.float32)
    sso = np.empty((1, NCORES * NS, NH, 128, DV), np.float32)
    for c, r in enumerate(results):
        sl = slice(c * NS, (c + 1) * NS)
        yp[c] = np.asarray(r["yT"]).transpose(2, 1, 0).reshape(SEQ, D)
        ys[sl, 0] = np.asarray(r["ysT"]).transpose(2, 1, 0).reshape(NS, D)
        hp[0, c] = np.asarray(r["hp"]).T.reshape(D)
        cpo[0, c] = np.asarray(r["cp"]).transpose(2, 1, 0).reshape(3, D)
        spo[0, c] = np.asarray(r["sp"]).transpose(1, 0, 2)
        hs[0, sl] = np.asarray(r["hs"]).transpose(2, 1, 0).reshape(NS, D)
        cso[0, sl] = np.asarray(r["cs"]).transpose(2, 3, 1, 0).reshape(NS, 3, D)
        sso[0, sl] = np.asarray(r["ss"])
    return (yp, ys, hp, cpo, spo, hs, cso, sso)


def kernel(**inputs):
    maps = _prep_inputs(inputs)
    if "nc" not in _NC_CACHE:
        _NC_CACHE["nc"] = build_program()
    nc = _NC_CACHE["nc"]
    res = run_bass_kernel_spmd(nc, maps, core_ids=list(range(NCORES)))
    return _assemble(res.results)
```
